# Optimizing a Trainium2 kernel written in Bass

```python
import math, functools
import jax, jax.numpy as jnp
from jax import lax
import numpy as np

D_MODEL = 2048
BATCH = 8
SEQ = 4096
DEPTH = 4

N_MIXERS = 3
EPS = 1e-6
CONF_KERNEL_WIDTH = 31
POOL_WINDOWS = (2, 4, 8, 16)
N_POOL_GROUPS = len(POOL_WINDOWS)
POOL_GROUP_DIM = D_MODEL // N_POOL_GROUPS
HEAD_DIM = 64
N_HEADS = D_MODEL // HEAD_DIM
N_KV_HEADS = N_HEADS // 8
GROUP_SIZE = N_HEADS // N_KV_HEADS
WINDOW = 128
BLOCK = 128
ROT_DIM = HEAD_DIM // 4
ROPE_THETA = 500000.0
D_FF = 5632
FFN_CONV_WIDTH = 3

kernel_name = "hybrid_conv_pool_swa_trunk"


def rms_norm(x, g):
    xf = x.astype(jnp.float32)
    y = xf * lax.rsqrt(jnp.mean(xf * xf, axis=-1, keepdims=True) + EPS)
    return (y * g.astype(jnp.float32)).astype(x.dtype)


def layer_norm(x, g, b):
    xf = x.astype(jnp.float32)
    mu = jnp.mean(xf, axis=-1, keepdims=True)
    xc = xf - mu
    var = jnp.mean(xc * xc, axis=-1, keepdims=True)
    y = xc * lax.rsqrt(var + EPS) * g.astype(jnp.float32) + b.astype(jnp.float32)
    return y.astype(x.dtype)


def causal_depthwise_conv(x, w, b):
    width, channels = w.shape
    y = lax.conv_general_dilated(
        x, w[:, None, :].astype(x.dtype), window_strides=(1,), padding=[(width - 1, 0)],
        dimension_numbers=("NWC", "WIO", "NWC"), feature_group_count=channels)
    return y + b.astype(x.dtype)


def conformer_conv_module(h, w_in, b_in, dw_w, dw_b, ln_g, ln_b, w_out, b_out):
    u = h @ w_in + b_in
    a, gate = jnp.split(u, 2, axis=-1)
    u = a * jax.nn.sigmoid(gate)
    u = causal_depthwise_conv(u, dw_w, dw_b)
    u = jax.nn.silu(layer_norm(u, ln_g, ln_b))
    return u @ w_out + b_out


def multiscale_pool_mixer(h, w_group, scale):
    bsz, seq, dim = h.shape
    hf = h.astype(jnp.float32)
    cs = jnp.concatenate([jnp.zeros((bsz, 1, dim), jnp.float32), jnp.cumsum(hf, axis=1)], axis=1)
    upper = cs[:, 1:]
    t = jnp.arange(seq)
    pooled = []
    for g, w in enumerate(POOL_WINDOWS):
        sl = slice(g * POOL_GROUP_DIM, (g + 1) * POOL_GROUP_DIM)
        lower = jnp.concatenate(
            [jnp.zeros((bsz, w - 1, POOL_GROUP_DIM), jnp.float32), cs[:, :seq + 1 - w, sl]], axis=1)
        count = jnp.minimum(t + 1, w).astype(jnp.float32)[None, :, None]
        pooled.append((upper[..., sl] - lower) / count)
    mixed = (jnp.concatenate(pooled, axis=-1) - hf).astype(h.dtype)
    mixed = mixed.reshape(bsz, seq, N_POOL_GROUPS, POOL_GROUP_DIM)
    y = jnp.einsum("bsgc,gcd->bsgd", mixed, w_group).reshape(bsz, seq, dim)
    return y * scale


def apply_partial_rotary(x, cos, sin):
    half = ROT_DIM // 2
    xr = x[..., :ROT_DIM].astype(jnp.float32)
    x1, x2 = xr[..., :half], xr[..., half:]
    rot = jnp.concatenate([x1 * cos - x2 * sin, x2 * cos + x1 * sin], axis=-1)
    return jnp.concatenate([rot.astype(x.dtype), x[..., ROT_DIM:]], axis=-1)


def banded_sink_attention(q, k, v, sinks):
    bsz, seq = q.shape[:2]
    nb = seq // BLOCK
    scale = 1.0 / math.sqrt(HEAD_DIM)
    qb = q.reshape(bsz, nb, BLOCK, N_KV_HEADS, GROUP_SIZE, HEAD_DIM).transpose(1, 0, 3, 4, 2, 5)

    def band(t):
        tb = t.reshape(bsz, nb, BLOCK, N_KV_HEADS, HEAD_DIM)
        prev = jnp.concatenate([jnp.zeros_like(tb[:, :1]), tb[:, :-1]], axis=1)
        return jnp.concatenate([prev, tb], axis=2).transpose(1, 0, 3, 2, 4)

    kb, vb = band(k), band(v)
    qi = jnp.arange(BLOCK)[:, None]
    kj = jnp.arange(2 * BLOCK)[None, :]
    diff = qi + BLOCK - kj
    in_window = (diff >= 0) & (diff < WINDOW)
    sink = sinks.astype(jnp.float32).reshape(N_KV_HEADS, GROUP_SIZE)[None, :, :, None, None]

    def block_fn(args):
        n, qn, kn, vn = args
        s = jnp.einsum("bkgqd,bkjd->bkgqj", qn.astype(jnp.float32), kn.astype(jnp.float32)) * scale
        valid = in_window & ((n * BLOCK + kj - BLOCK) >= 0)
        s = jnp.where(valid, s, -jnp.inf)
        m = jnp.maximum(jnp.max(s, axis=-1, keepdims=True), sink)
        p = jnp.exp(s - m)
        denom = jnp.sum(p, axis=-1, keepdims=True) + jnp.exp(sink - m)
        o = jnp.einsum("bkgqj,bkjd->bkgqd", p, vn.astype(jnp.float32)) / denom
        return o.astype(q.dtype)

    out = lax.map(block_fn, (jnp.arange(nb), qb, kb, vb))
    return out.transpose(1, 0, 4, 2, 3, 5).reshape(bsz, seq, N_HEADS * HEAD_DIM)


def swa_sink_attention(h, positions, w_qkv, q_norm_g, k_norm_g, sinks, w_o):
    bsz, seq, _ = h.shape
    qkv = h @ w_qkv
    q, k, v = jnp.split(qkv, [N_HEADS * HEAD_DIM, (N_HEADS + N_KV_HEADS) * HEAD_DIM], axis=-1)
    q = rms_norm(q.reshape(bsz, seq, N_HEADS, HEAD_DIM), q_norm_g)
    k = rms_norm(k.reshape(bsz, seq, N_KV_HEADS, HEAD_DIM), k_norm_g)
    v = v.reshape(bsz, seq, N_KV_HEADS, HEAD_DIM)
    inv_freq = ROPE_THETA ** (-jnp.arange(0, ROT_DIM, 2, dtype=jnp.float32) / ROT_DIM)
    ang = positions.astype(jnp.float32)[..., None] * inv_freq
    cos, sin = jnp.cos(ang)[:, :, None, :], jnp.sin(ang)[:, :, None, :]
    q = apply_partial_rotary(q, cos, sin)
    k = apply_partial_rotary(k, cos, sin)
    o = banded_sink_attention(q, k, v, sinks)
    return o @ w_o


def conv_gated_mlp(h, w_up, dw_w, dw_b, w_down):
    u = causal_depthwise_conv(h @ w_up, dw_w, dw_b)
    gate, val = jnp.split(u, 2, axis=-1)
    return (jax.nn.silu(gate) * val) @ w_down


def setup_inputs(seed: int = 0) -> dict:
    key = jax.random.key(seed)
    keys = iter(jax.random.split(key, 64))

    def nrm(shape, scale):
        return jax.random.normal(next(keys), shape, jnp.float32) * scale

    def gain(shape):
        return 1.0 + nrm(shape, 0.02)

    d = D_MODEL
    out = {}
    out["x"] = nrm((BATCH, SEQ, d), 1.0)
    offset = jax.random.randint(next(keys), (BATCH, 1), 0, 4096, dtype=jnp.int32)
    out["positions"] = (jnp.arange(SEQ, dtype=jnp.int32)[None, :] + offset).astype(jnp.int32)

    def add_conformer(p):
        out[p + "norm_g"] = gain((d,))
        out[p + "a_w_in"] = nrm((d, 2 * d), d ** -0.5)
        out[p + "a_b_in"] = nrm((2 * d,), 0.02)
        out[p + "a_dw_w"] = nrm((CONF_KERNEL_WIDTH, d), CONF_KERNEL_WIDTH ** -0.5)
        out[p + "a_dw_b"] = nrm((d,), 0.02)
        out[p + "a_ln_g"] = gain((d,))
        out[p + "a_ln_b"] = nrm((d,), 0.02)
        out[p + "a_w_out"] = nrm((d, d), d ** -0.5)
        out[p + "a_b_out"] = nrm((d,), 0.02)

    def add_ffn(p):
        out[p + "ffn_norm_g"] = gain((d,))
        out[p + "ffn_w_up"] = nrm((d, 2 * D_FF), d ** -0.5)
        out[p + "ffn_dw_w"] = nrm((FFN_CONV_WIDTH, 2 * D_FF), FFN_CONV_WIDTH ** -0.5)
        out[p + "ffn_dw_b"] = nrm((2 * D_FF,), 0.02)
        out[p + "ffn_w_down"] = nrm((D_FF, d), D_FF ** -0.5)

    add_conformer("l0_")
    add_ffn("l0_")
    out["l1_norm_g"] = gain((d,))
    out["l1_b_w_group"] = nrm((N_POOL_GROUPS, POOL_GROUP_DIM, POOL_GROUP_DIM), POOL_GROUP_DIM ** -0.5)
    out["l1_b_scale"] = 1.0 + nrm((d,), 0.1)
    add_ffn("l1_")
    out["l2_norm_g"] = gain((d,))
    out["l2_c_w_qkv"] = nrm((d, (N_HEADS + 2 * N_KV_HEADS) * HEAD_DIM), d ** -0.5)
    out["l2_c_q_norm_g"] = gain((HEAD_DIM,))
    out["l2_c_k_norm_g"] = gain((HEAD_DIM,))
    out["l2_c_sinks"] = nrm((N_HEADS,), 1.0)
    out["l2_c_w_o"] = nrm((N_HEADS * HEAD_DIM, d), (N_HEADS * HEAD_DIM) ** -0.5)
    add_ffn("l2_")
    add_conformer("l3_")
    add_ffn("l3_")
    return out


def reference(x, positions,
              l0_norm_g, l0_a_w_in, l0_a_b_in, l0_a_dw_w, l0_a_dw_b, l0_a_ln_g, l0_a_ln_b, l0_a_w_out, l0_a_b_out,
              l0_ffn_norm_g, l0_ffn_w_up, l0_ffn_dw_w, l0_ffn_dw_b, l0_ffn_w_down,
              l1_norm_g, l1_b_w_group, l1_b_scale,
              l1_ffn_norm_g, l1_ffn_w_up, l1_ffn_dw_w, l1_ffn_dw_b, l1_ffn_w_down,
              l2_norm_g, l2_c_w_qkv, l2_c_q_norm_g, l2_c_k_norm_g, l2_c_sinks, l2_c_w_o,
              l2_ffn_norm_g, l2_ffn_w_up, l2_ffn_dw_w, l2_ffn_dw_b, l2_ffn_w_down,
              l3_norm_g, l3_a_w_in, l3_a_b_in, l3_a_dw_w, l3_a_dw_b, l3_a_ln_g, l3_a_ln_b, l3_a_w_out, l3_a_b_out,
              l3_ffn_norm_g, l3_ffn_w_up, l3_ffn_dw_w, l3_ffn_dw_b, l3_ffn_w_down):
    mixers = [
        lambda h: conformer_conv_module(h, l0_a_w_in, l0_a_b_in, l0_a_dw_w, l0_a_dw_b,
                                        l0_a_ln_g, l0_a_ln_b, l0_a_w_out, l0_a_b_out),
        lambda h: multiscale_pool_mixer(h, l1_b_w_group, l1_b_scale),
        lambda h: swa_sink_attention(h, positions, l2_c_w_qkv, l2_c_q_norm_g, l2_c_k_norm_g,
                                     l2_c_sinks, l2_c_w_o),
        lambda h: conformer_conv_module(h, l3_a_w_in, l3_a_b_in, l3_a_dw_w, l3_a_dw_b,
                                        l3_a_ln_g, l3_a_ln_b, l3_a_w_out, l3_a_b_out),
    ]
    mixer_norms = [l0_norm_g, l1_norm_g, l2_norm_g, l3_norm_g]
    ffns = [
        (l0_ffn_norm_g, l0_ffn_w_up, l0_ffn_dw_w, l0_ffn_dw_b, l0_ffn_w_down),
        (l1_ffn_norm_g, l1_ffn_w_up, l1_ffn_dw_w, l1_ffn_dw_b, l1_ffn_w_down),
        (l2_ffn_norm_g, l2_ffn_w_up, l2_ffn_dw_w, l2_ffn_dw_b, l2_ffn_w_down),
        (l3_ffn_norm_g, l3_ffn_w_up, l3_ffn_dw_w, l3_ffn_dw_b, l3_ffn_w_down),
    ]
    for i in range(DEPTH):
        x = x + mixers[i](rms_norm(x, mixer_norms[i]))
        g, w_up, dw_w, dw_b, w_down = ffns[i]
        x = x + conv_gated_mlp(rms_norm(x, g), w_up, dw_w, dw_b, w_down)
    return x
```

```python
import math
import numpy as np
import concourse.bass as bass
import concourse.mybir as mybir
from concourse.bass_utils import run_bass_kernel_spmd

F32 = mybir.dt.float32
BF16 = mybir.dt.bfloat16
I32 = mybir.dt.int32
AF = mybir.ActivationFunctionType
ALU = mybir.AluOpType

P = 128
D = 2048
CH = 16
T = 512
DFF = 5632
NF = 44
EPS = 1e-6
RING = 4
SLOTW = 5632
NEG = -30000.0
TWO_PI = 2.0 * math.pi
ROPE_THETA = 500000.0
POOL_W = (2, 4, 8, 16)
MIXERS = ("conf", "pool", "attn", "conf")


class Sem:
    def __init__(self, h, name):
        self.h = h
        self.name = name
        self.val = 0


class Prog:
    ENGS = ("pe", "act", "dve", "pool", "sp")

    def __init__(self, nc, sems):
        self.nc = nc
        self.q = {e: [] for e in self.ENGS}
        self.esem = {e: sems[e] for e in ("pe", "act", "dve", "pool")}
        self.waited = {e: {} for e in self.ENGS}
        self.tok = {}
        self.big_frontier = []
        self.dry = False
        self.n_inst = 0
        self.small = set()

    def reset(self):
        self.tok = {}
        self.big_frontier = []

    def _entry(self, t):
        e = self.tok.get(t)
        if e is None:
            e = {"w": None, "r": []}
            if isinstance(t, tuple) and isinstance(t[0], str) and t[0].startswith("big"):
                e["r"] = list(self.big_frontier)
            self.tok[t] = e
        return e

    def switch_big(self):
        if self.dry:
            return
        fr = {}
        for t in list(self.tok.keys()):
            if isinstance(t, tuple) and isinstance(t[0], str) and t[0].startswith("big"):
                e = self.tok.pop(t)
                for d in ([e["w"]] if e["w"] else []) + e["r"]:
                    if d[0].name not in fr or fr[d[0].name][1] < d[1]:
                        fr[d[0].name] = d
        for d in self.big_frontier:
            if d[0].name not in fr or fr[d[0].name][1] < d[1]:
                fr[d[0].name] = d
        self.big_frontier = list(fr.values())

    def set_writer(self, t, sem, val):
        self.tok[t] = {"w": (sem, val), "r": []}

    def emit(self, eng, fn, reads=(), writes=(), dsem=None, small=False):
        if self.dry:
            return
        deps = {}

        def add(d):
            if d is None:
                return
            s, v = d
            if dsem is None and eng in self.esem and s is self.esem[eng] and (s.name, v) not in self.small:
                return
            if s.name not in deps or deps[s.name][1] < v:
                deps[s.name] = (s, v)

        for t in reads:
            add(self._entry(t)["w"])
        for t in writes:
            e = self._entry(t)
            add(e["w"])
            for d in e["r"]:
                add(d)
        wl = []
        wd = self.waited[eng]
        for name, (s, v) in deps.items():
            if wd.get(name, 0) >= v:
                continue
            wd[name] = v
            wl.append((s.h, v))
        if dsem is not None:
            dsem.val += 16
            cid = (dsem, dsem.val)
            inc = (dsem.h, 16)
        else:
            s = self.esem[eng]
            s.val += 1
            cid = (s, s.val)
            inc = (s.h, 1)

        def run(e, wl=wl, fn=fn, inc=inc):
            for h, v in wl:
                e.wait_ge(h, v)
            fn(e).then_inc(inc[0], inc[1])

        self.q[eng].append(run)
        self.n_inst += 1
        if small:
            self.small.add((cid[0].name, cid[1]))
        for t in writes:
            self.tok[t] = {"w": cid, "r": []}
        for t in reads:
            if t in writes:
                continue
            e = self._entry(t)
            r = [d for d in e["r"] if d[0] is not cid[0]]
            r.append(cid)
            e["r"] = r
        return cid

    def wait_all(self, eng, cids):
        if self.dry:
            return
        wl = [(s.h, v) for s, v in cids]

        def run(e, wl=wl):
            for h, v in wl:
                e.wait_ge(h, v)

        self.q[eng].append(run)


def _fm(v):
    v = np.asarray(v, dtype=np.float32)
    return np.ascontiguousarray(v.reshape(-1, P).T)


def _slab_pair(w, npair):
    K = w.shape[0]
    kc = K // P
    a = np.asarray(w).reshape(kc, P, 2, npair, P)
    return np.ascontiguousarray(a.transpose(3, 1, 2, 0, 4)).reshape(npair, P, 2 * kc * P)


def _slab_m(w):
    K, M = w.shape
    a = np.asarray(w).reshape(K // P, P, M // P, P)
    return np.ascontiguousarray(a.transpose(2, 1, 0, 3)).reshape(M // P, P, (K // P) * P)


def make_consts():
    c = {}
    c["ident"] = np.eye(P, dtype=np.float32)
    c["ones"] = np.ones((P, P), np.float32)
    bo = np.zeros((P, P), np.float32)
    bo[:64, :64] = 1
    bo[64:, 64:] = 1
    c["bones"] = bo
    pm = np.zeros((P, P), np.float32)
    for m in range(P):
        dh = m % 64
        if dh < 8:
            pm[m + 8, m] = 1
        elif dh < 16:
            pm[m - 8, m] = 1
    c["perm"] = pm
    ii = np.arange(P)[:, None]
    jj = np.arange(P)[None, :]
    cbf = {}
    cbf["ident"] = np.eye(P, dtype=np.float32)
    cbf["mprev"] = np.where(jj > ii, 0.0, NEG).astype(np.float32)
    cbf["mcur"] = np.where(jj <= ii, 0.0, NEG).astype(np.float32)
    cbf["identq"] = np.tile(np.eye(P, dtype=np.float32), (1, 4))
    inv_freq = (np.float32(ROPE_THETA) ** (-(np.arange(0, 16, 2, dtype=np.float32)) / np.float32(16))).astype(np.float32)
    invf = np.zeros((P, 1), np.float32)
    nsgn = np.zeros((P, 1), np.float32)
    for p in range(P):
        dh = p % 64
        if dh < 16:
            invf[p, 0] = inv_freq[dh % 8]
            nsgn[p, 0] = -1.0 if dh < 8 else 1.0
    c["invf"] = invf
    c["sgn"] = nsgn
    ic = np.zeros((P, 4 * 16), np.float32)
    for g, w in enumerate(POOL_W):
        ic[:, g * 16:(g + 1) * 16] = (1.0 / np.minimum(np.arange(16) + 1, w)).astype(np.float32)[None, :]
    c["invcnt"] = ic
    cbarr = np.ascontiguousarray(np.concatenate([cbf["ident"], cbf["mprev"], cbf["mcur"], cbf["identq"]], axis=1))
    offs = {}
    cols = []
    o = 0
    for k, v in c.items():
        offs[k] = (o, v.shape[1])
        cols.append(v)
        o += v.shape[1]
    return np.ascontiguousarray(np.concatenate(cols, axis=1)), offs, cbarr


def pack_vec(inp):
    parts = []
    offs = {}
    o = 0

    def add(name, a):
        nonlocal o
        a = np.ascontiguousarray(a, dtype=np.float32)
        offs[name] = (o, a.shape[1])
        parts.append(a)
        o += a.shape[1]

    for L in range(4):
        p = "l%d_" % L
        add(p + "norm_g", _fm(inp[p + "norm_g"]))
        add(p + "ffn_norm_g", _fm(inp[p + "ffn_norm_g"]))
        dw = np.asarray(inp[p + "ffn_dw_w"], np.float32).reshape(3, 88, P).transpose(2, 1, 0).reshape(P, 88 * 3)
        add(p + "ffn_dw_w", dw)
        add(p + "ffn_dw_b", _fm(inp[p + "ffn_dw_b"]))
        if MIXERS[L] == "conf":
            add(p + "a_b_in", _fm(inp[p + "a_b_in"]))
            dwc = np.asarray(inp[p + "a_dw_w"], np.float32).reshape(31, 16, P).transpose(2, 1, 0).reshape(P, 16 * 31)
            add(p + "a_dw_w", dwc)
            add(p + "a_dw_b", _fm(inp[p + "a_dw_b"]))
            add(p + "a_ln_g", _fm(inp[p + "a_ln_g"]))
            add(p + "a_ln_b", _fm(inp[p + "a_ln_b"]))
            add(p + "a_b_out", _fm(inp[p + "a_b_out"]))
        elif MIXERS[L] == "pool":
            add(p + "b_scale", _fm(inp[p + "b_scale"]))
        else:
            add(p + "c_q_norm_g", np.tile(np.asarray(inp[p + "c_q_norm_g"], np.float32), 2).reshape(P, 1))
            add(p + "c_k_norm_g", np.tile(np.asarray(inp[p + "c_k_norm_g"], np.float32), 2).reshape(P, 1))
            add(p + "c_sinks", np.broadcast_to(np.asarray(inp[p + "c_sinks"], np.float32)[None, :], (P, 32)))
    return np.ascontiguousarray(np.concatenate(parts, axis=1)), offs


def weight_slabs(inp, layers):
    out = {}
    for L in layers:
        p = "l%d_" % L
        out["wup%d" % L] = _slab_pair(inp[p + "ffn_w_up"], NF)
        out["wdn%d" % L] = _slab_m(inp[p + "ffn_w_down"])
        if MIXERS[L] == "conf":
            out["win%d" % L] = _slab_pair(inp[p + "a_w_in"], 16)
            out["wout%d" % L] = _slab_m(inp[p + "a_w_out"])
        elif MIXERS[L] == "pool":
            wg = np.asarray(inp[p + "b_w_group"]).reshape(4, 4, P, 512)
            out["wg%d" % L] = np.ascontiguousarray(wg.transpose(0, 2, 1, 3)).reshape(4, P, 2048)
        else:
            wqkv = np.asarray(inp[p + "c_w_qkv"])
            out["wq%d" % L] = _slab_m(wqkv[:, :2048])
            wk = wqkv[:, 2048:2304].reshape(16, P, 4, 64)
            wk = np.concatenate([wk, wk], axis=3)
            out["wk%d" % L] = np.ascontiguousarray(wk.transpose(2, 1, 0, 3)).reshape(4, P, 2048)
            wv = wqkv[:, 2304:2560].reshape(16, P, 2, 2, 64)
            wv = np.stack([wv, wv], axis=4)
            out["wv%d" % L] = np.ascontiguousarray(wv.transpose(2, 1, 0, 3, 4, 5)).reshape(2, P, 16 * 256)
            out["wo%d" % L] = _slab_m(inp[p + "c_w_o"])
    return out


def build_program(S, layers, slab_shapes, voffs, coffs, nvec, ncst, do_ffn=True, do_mixer=True):
    NT = S // T
    nc = bass.Bass("TRN2", target_bir_lowering=False)
    x_d = nc.dram_tensor("x", [S, D], F32, kind="ExternalInput").ap()
    pos_d = nc.dram_tensor("posb", [P, S], I32, kind="ExternalInput").ap()
    vec_d = nc.dram_tensor("vec", [P, nvec], F32, kind="ExternalInput").ap()
    cst_d = nc.dram_tensor("cst", [P, ncst], F32, kind="ExternalInput").ap()
    cstb_d = nc.dram_tensor("cstb", [P, 7 * P], F32, kind="ExternalInput").ap()
    out_d = nc.dram_tensor("out", [S, D], F32, kind="ExternalOutput").ap()
    w_in = {}
    w_sc = {}
    for name, shp in slab_shapes.items():
        w_in[name] = nc.dram_tensor(name, list(shp), F32, kind="ExternalInput").ap()
        w_sc[name] = nc.dram_tensor(name + "_b", list(shp), BF16).ap()
    for L in layers:
        if MIXERS[L] == "conf":
            w_sc["cvw%d" % L] = nc.dram_tensor("cvw%d_b" % L, [16, P, 31 * P], BF16).ap()

    sb = nc.alloc_sbuf_tensor
    xT = sb("xT", [P, CH, T], F32)
    hb = sb("hb", [P, CH, T], BF16)
    BIG = sb("BIG", [P, NF * T], BF16)
    ring = [sb("ring%d" % i, [P, SLOTW], BF16) for i in range(RING)]
    vec = sb("vecs", [P, nvec], F32)
    cst = sb("csts", [P, ncst], F32)
    cb = sb("cstbs", [P, 7 * P], BF16)
    R2 = sb("R2", [P, CH * (30 + T)], BF16)
    R2f = R2[:, :].bitcast(F32)
    PBW = 15 + T
    pbuf = [[R2f[:, (a * 2 + b) * PBW:(a * 2 + b + 1) * PBW] for b in range(2)] for a in range(2)]
    ubuf = [[R2f[:, 4 * PBW + (a * 2 + b) * T: 4 * PBW + (a * 2 + b + 1) * T] for b in range(2)] for a in range(2)]
    sqb = [sb("sq%d" % i, [P, T], F32) for i in range(4)]
    tmpf = [sb("tf%d" % i, [P, T], F32) for i in range(6)]
    ffst = {L: sb("ffst%d" % L, [P, 88, 2], F32) for L in layers}
    Gb = R2[:, :].rearrange("p (c t) -> p c t", c=CH)
    cvst = {L: sb("cvst%d" % L, [P, CH, 30], BF16) for L in layers if MIXERS[L] == "conf"}
    plst = sb("plst", [P, CH, 15], F32)
    KT = sb("KT", [P, 4, P + T], BF16)
    VT = sb("VT", [P, 5, 512], BF16)
    est = sb("est", [P, 32], F32)
    psum = [nc.alloc_psum_tensor("ps%d" % i, [P, 512], F32) for i in range(8)]

    actb = BIG[:, 0:NF * T].rearrange("p (j t) -> p j t", j=NF)
    bigf = BIG[:, :].bitcast(F32)
    yb = bigf[:, 0:CH * T].rearrange("p (c t) -> p c t", c=CH)
    Hb = bigf[:, 0:CH * (15 + T)].rearrange("p (c t) -> p c t", c=CH)
    QT = BIG[:, 0:CH * T].rearrange("p (c t) -> p c t", c=CH)
    PTb = [BIG[:, CH * T + i * 2048: CH * T + (i + 1) * 2048].rearrange("p (k q) -> p k q", k=2) for i in range(2)]
    stage = [bigf[:, i * D:(i + 1) * D] for i in range(2)]
    cosb = bigf[:, 6144:6144 + T]
    sinb = bigf[:, 6144 + T:6144 + 2 * T]
    dgb = [BIG[:, i * 31 * P:(i + 1) * 31 * P] for i in range(2)]
    posi = tmpf[3][:, :].bitcast(I32)

    def V(name, L=None):
        key = name if L is None else "l%d_%s" % (L, name)
        o, n = voffs[key]
        return vec[:, o:o + n]

    def C(name):
        o, n = coffs[name]
        return cst[:, o:o + n]

    identb = cb[:, 0:P]
    mprevb = cb[:, P:2 * P]
    mcurb = cb[:, 2 * P:3 * P]
    identqb = cb[:, 3 * P:7 * P]

    import contextlib
    with contextlib.ExitStack() as es:
        sems = {}
        for nm in ("pe", "act", "dve", "pool", "pre", "ld", "ld2", "xin0", "xin1", "xout0", "xout1", "misc0", "misc1") + tuple("r%d" % i for i in range(RING)):
            sems[nm] = Sem(es.enter_context(nc.semaphore(nm)), nm)
        for nm in slab_shapes:
            sems["c_" + nm] = Sem(es.enter_context(nc.semaphore("c_" + nm)), "c_" + nm)
        pg = Prog(nc, sems)
        emit = pg.emit
        state = {"ps": 0}

        held = set()

        def next_ps():
            while True:
                i = state["ps"] % 8
                state["ps"] += 1
                if i not in held:
                    return psum[i], ("ps", i)

        class WStream:
            def __init__(self):
                self.plan = []
                self.i = 0
                self.issued = 0

            def _issue(self, k):
                name, idx, width = self.plan[k]
                slot = k % RING
                src = w_sc[name][idx]
                emit("sp", lambda e, slot=slot, src=src, width=width: e.dma_start(out=ring[slot][:, 0:width], in_=src),
                     reads=[("wsc", name), ("wscratch2", 0), ("wscratch2", 1)], writes=[("ring", slot)], dsem=sems["r%d" % slot])

            def start(self):
                self.i = 0
                self.issued = 0
                while self.issued < min(RING, len(self.plan)):
                    self._issue(self.issued)
                    self.issued += 1

            def get(self, name, idx, width):
                if pg.dry:
                    self.plan.append((name, idx, width))
                    self.i += 1
                    return 0
                assert self.plan[self.i] == (name, idx, width), (self.plan[self.i], name, idx, width)
                slot = self.i % RING
                self.i += 1
                return slot

            def release(self, slot):
                if pg.dry:
                    return
                if self.issued < len(self.plan):
                    assert self.issued % RING == slot
                    self._issue(self.issued)
                    self.issued += 1

        W = WStream()

        def mm_group(ps_ap, pairs, start=True, stop=True):
            def fn(e):
                n = len(pairs)
                ins = None
                for i, (l, r) in enumerate(pairs):
                    ins = e.matmul(ps_ap, l, r, start=(start and i == 0), stop=(stop and i == n - 1))
                return ins
            return fn

        def act_fn(out, in_, func, bias=None, scale=None):
            kw = {}
            if bias is not None:
                kw["bias"] = bias
            if scale is not None:
                kw["scale"] = scale
            return lambda e: e.activation(out=out, in_=in_, func=func, **kw)

        def stt(out, in0, scalar, in1, op0, op1):
            return lambda e: e.scalar_tensor_tensor(out=out, in0=in0, scalar=scalar, in1=in1, op0=op0, op1=op1)

        def tt(out, in0, in1, op):
            return lambda e: e.tensor_tensor(out=out, in0=in0, in1=in1, op=op)

        def cp(out, in_):
            return lambda e: e.tensor_copy(out=out, in_=in_)

        def setup():
            emit("sp", lambda e: e.dma_start(out=vec[:, :], in_=vec_d), writes=[("vec",)], dsem=sems["ld"])
            emit("sp", lambda e: e.dma_start(out=cst[:, :], in_=cst_d), writes=[("cst",)], dsem=sems["ld2"])
            if not pg.dry:
                sems["pre"].val += 16
                pg.q["pool"].append(lambda e: e.dma_start(out=cb[:, :], in_=cstb_d).then_inc(sems["pre"].h, 16))
                pg.set_writer(("cb",), sems["pre"], sems["pre"].val)
            for L in layers:
                emit("pool", lambda e, L=L: e.memset(ffst[L][:, :, :], 0.0), writes=[("ffst", L)])
                if MIXERS[L] == "conf":
                    emit("pool", lambda e, L=L: e.memset(cvst[L][:, :, :], 0.0), writes=[("cvst", L)])
            emit("pool", lambda e: e.memset(plst[:, :, :], 0.0), writes=[("plst",)])
            emit("pool", lambda e: e.memset(KT[:, :, :], 0.0), writes=[("KT",)])
            emit("pool", lambda e: e.memset(VT[:, :, :], 0.0), writes=[("VT",)])
            if 2 in layers:
                emit("act", act_fn(est[:, :], V("c_sinks", 2), AF.Exp), reads=[("vec",)], writes=[("est",)])
            k = 0
            for L in layers:
                if MIXERS[L] != "conf":
                    continue
                dwc = V("a_dw_w", L)
                for c in range(CH):
                    buf = dgb[k % 2]
                    tokb = ("bigdgb", k % 2)
                    for j in range(31):
                        eng = "dve"
                        emit(eng, (lambda e, buf=buf, j=j, c=c, dwc=dwc: e.tensor_scalar_mul(
                            out=buf[:, j * P:(j + 1) * P], in0=identb, scalar1=dwc[:, c * 31 + j:c * 31 + j + 1])),
                            reads=[("cb",), ("vec",)], writes=[tokb + (j % 2,)])
                    emit("sp", lambda e, buf=buf, L=L, c=c: e.dma_start(out=w_sc["cvw%d" % L][c], in_=buf),
                         reads=[tokb + (0,), tokb + (1,)], writes=[("cvwd", L, c)], dsem=sems["misc%d" % (k % 2)])
                    k += 1
            if not pg.dry:
                pg.set_writer(("wscratch2", 0), sems["misc0"], sems["misc0"].val)
                pg.set_writer(("wscratch2", 1), sems["misc1"], sems["misc1"].val)

        def layer_tensors(L):
            m = MIXERS[L]
            if m == "conf":
                names = ["win%d" % L, "wout%d" % L]
            elif m == "pool":
                names = ["wg%d" % L]
            else:
                names = ["wq%d" % L, "wk%d" % L, "wv%d" % L, "wo%d" % L]
            return names + ["wup%d" % L, "wdn%d" % L]

        def cast_layer(L):
            if pg.dry:
                return
            for name in layer_tensors(L):
                shp = slab_shapes[name]
                ns, width = shp[0], shp[2]
                per = max(1, (8 << 20) // (P * width * 4))
                sm = sems["c_" + name]
                s0 = 0
                while s0 < ns:
                    s1 = min(ns, s0 + per)
                    src = w_in[name][s0:s1].rearrange("j p w -> (j p) w")
                    dst = w_sc[name][s0:s1].rearrange("j p w -> (j p) w")
                    sm.val += 16
                    pg.q["pool"].append(lambda e, src=src, dst=dst, sm=sm: e.dma_start(out=dst, in_=src).then_inc(sm.h, 16))
                    s0 = s1
                pg.set_writer(("wsc", name), sm, sm.val)

        def load_x(ti):
            pg.switch_big()
            xs_state["ready"] = False
            for tb in range(4):
                k = tb % 2
                st = stage[k]
                r0 = ti * T + tb * P
                emit("sp", lambda e, st=st, r0=r0: e.dma_start(out=st, in_=x_d[r0:r0 + P, :]),
                     writes=[("bigstage", k)], dsem=sems["xin%d" % k])
                for cg in range(4):
                    ps, pt = next_ps()

                    def tr(e, ps=ps, st=st, cg=cg):
                        ins = None
                        for q in range(4):
                            c = cg * 4 + q
                            ins = e.transpose(out=ps[:, q * P:(q + 1) * P], in_=st[:, c * P:(c + 1) * P], identity=C("ident"))
                        return ins
                    emit("pe", tr, reads=[("bigstage", k), ("cst",)], writes=[pt])
                    eng = "act" if cg % 2 == 0 else "dve"
                    dst = xT[:, cg * 4:(cg + 1) * 4, tb * P:(tb + 1) * P]
                    src = ps[:, :].rearrange("p (a b) -> p a b", a=4)
                    if eng == "act":
                        emit("act", act_fn(dst, src, AF.Copy), reads=[pt], writes=[("x", cg * 4 + q) for q in range(4)])
                    else:
                        emit("dve", cp(dst, src), reads=[pt], writes=[("x", cg * 4 + q) for q in range(4)])

        def store_x(ti):
            pg.switch_big()
            cids = []
            for tb in range(4):
                k = tb % 2
                st = stage[k]
                for cg in range(4):
                    ps, pt = next_ps()

                    def tr(e, ps=ps, cg=cg, tb=tb):
                        ins = None
                        for q in range(4):
                            c = cg * 4 + q
                            ins = e.transpose(out=ps[:, q * P:(q + 1) * P], in_=xT[:, c, tb * P:(tb + 1) * P], identity=C("ident"))
                        return ins
                    emit("pe", tr, reads=[("x", cg * 4 + q) for q in range(4)] + [("cst",)], writes=[pt])
                    eng = "act" if cg % 2 == 0 else "dve"
                    dst = st[:, cg * 512:(cg + 1) * 512]
                    if eng == "act":
                        emit("act", act_fn(dst, ps[:, :], AF.Copy), reads=[pt], writes=[("bigstage", k, cg)])
                    else:
                        emit("dve", cp(dst, ps[:, :]), reads=[pt], writes=[("bigstage", k, cg)])
                r0 = ti * T + tb * P
                cid = emit("sp", lambda e, st=st, r0=r0: e.dma_start(out=out_d[r0:r0 + P, :], in_=st),
                           reads=[("bigstage", k, cg) for cg in range(4)], writes=[("outd", ti, tb)], dsem=sems["xout%d" % k])
                for cg in range(4):
                    pass
                cids.append(cid)
            return cids

        def col_stats(src_fn, n, ps, pt, func=AF.Square, bias_fn=None):
            items = []
            for c in range(n):
                sq = sqb[c % 4]
                emit("act", act_fn(sq[:, :], src_fn(c)[0], func, bias=(bias_fn(c) if bias_fn else None)),
                     reads=src_fn(c)[1], writes=[("sq", c % 4)])
                items.append((sq[:, :], [("sq", c % 4)]))
                presum_step(items, c, (4, 5))
            presum_mm(ps, pt, (4, 5))

        def presum_step(items, c, accs):
            if c < 2:
                return
            par = c % 2
            eng = ("pool", "dve")[par]
            acc = tmpf[accs[par]]
            at = ("tf", accs[par])
            if c < 4:
                emit(eng, tt(acc[:, :], items[c - 2][0], items[c][0], ALU.add), reads=items[c - 2][1] + items[c][1], writes=[at])
            else:
                emit(eng, tt(acc[:, :], acc[:, :], items[c][0], ALU.add), reads=[at] + items[c][1], writes=[at])

        def presum_mm(ps, pt, accs):
            a0, a1 = tmpf[accs[0]], tmpf[accs[1]]

            def fn(e):
                e.matmul(ps[:, :], C("ones"), a0[:, :], start=True, stop=False)
                return e.matmul(ps[:, :], C("ones"), a1[:, :], start=False, stop=True)
            emit("pe", fn, reads=[("tf", accs[0]), ("tf", accs[1]), ("cst",)], writes=[pt])

        def rstd_from(ps, pt, scale, out_i):
            sd = tmpf[out_i]
            emit("act", act_fn(sd[:, :], ps[:, :], AF.Sqrt, bias=V_eps(), scale=scale), reads=[pt, ("cst",)], writes=[("tf", out_i)])
            emit("dve", lambda e: e.reciprocal(out=sd[:, :], in_=sd[:, :]), reads=[("tf", out_i)], writes=[("tf", out_i)])
            return sd

        def V_eps():
            return epsb[:, 0:1]

        xs_state = {"items": [], "ready": False}

        def xstat(c):
            if c == 0:
                xs_state["items"] = []
            sq = sqb[c % 4]
            emit("act", act_fn(sq[:, :], xT[:, c, :], AF.Square), reads=[("x", c)], writes=[("sq", c % 4)])
            xs_state["items"].append((sq[:, :], [("sq", c % 4)]))
            presum_step(xs_state["items"], c, (4, 5))
            if c == CH - 1:
                xs_state["ready"] = True

        def x_sumsq(ps, pt):
            if xs_state["ready"]:
                xs_state["ready"] = False
                presum_mm(ps, pt, (4, 5))
            else:
                col_stats(lambda c: (xT[:, c, :], [("x", c)]), CH, ps, pt)

        def prog_groups(sb_list):
            outs = [next_ps() for _ in sb_list]
            slots = sorted(set(sl for sl, _ in sb_list))
            for kc in range(CH):
                def fn(e, kc=kc):
                    ins = None
                    for (sl, base), (ps, pt) in zip(sb_list, outs):
                        ins = e.matmul(ps[:, :], ring[sl][:, base + kc * P: base + (kc + 1) * P], hb[:, kc, :],
                                       start=(kc == 0), stop=(kc == CH - 1))
                    return ins
                emit("pe", fn, reads=[("ring", sl) for sl in slots] + [("hb", kc)], writes=[pt for _, pt in outs])
            return outs

        def rmsnorm_to_hb(gname, L):
            g = V(gname, L)
            ps, pt = next_ps()
            x_sumsq(ps, pt)
            rs = rstd_from(ps, pt, 1.0 / D, 0)
            for c in range(CH):
                emit("dve", stt(hb[:, c, :], xT[:, c, :], g[:, c:c + 1], rs[:, :], ALU.mult, ALU.mult),
                     reads=[("x", c), ("tf", 0), ("vec",)], writes=[("hb", c)])

        def ffn(L, ti, stats_next=True):
            pg.switch_big()
            rmsnorm_to_hb("ffn_norm_g", L)
            dww = V("ffn_dw_w", L)
            dwb = V("ffn_dw_b", L)

            def post(j, half, ps, pt):
                f = half * NF + j
                pb = pbuf[half][j % 2]
                pbt = ("bigpb", half, j % 2)
                ub = ubuf[half][j % 2]
                ubt = ("bigub", half, j % 2)
                emit("pool", cp(pb[:, 0:2], ffst[L][:, f, :]), reads=[("ffst", L, f), ("ffst", L)], writes=[pbt + (0,)])
                emit("act", act_fn(pb[:, 2:2 + T], ps[:, :], AF.Copy), reads=[pt], writes=[pbt + (1,)])
                emit("pool", cp(ffst[L][:, f, :], pb[:, T:T + 2]), reads=[pbt + (0,), pbt + (1,)], writes=[("ffst", L, f)])
                emit("act", act_fn(ub[:, :], ps[:, :], AF.Identity, bias=dwb[:, f:f + 1], scale=dww[:, f * 3 + 2: f * 3 + 3]),
                     reads=[pt, ("vec",)], writes=[ubt])
                emit("dve", stt(ub[:, :], pb[:, 1:1 + T], dww[:, f * 3 + 1:f * 3 + 2], ub[:, :], ALU.mult, ALU.add),
                     reads=[pbt + (0,), pbt + (1,), ubt], writes=[ubt])
                emit("dve", stt(ub[:, :], pb[:, 0:T], dww[:, f * 3:f * 3 + 1], ub[:, :], ALU.mult, ALU.add),
                     reads=[pbt + (0,), pbt + (1,), ubt], writes=[ubt])

            def gate(j):
                ug = ubuf[0][j % 2]
                uv = ubuf[1][j % 2]
                emit("act", act_fn(ug[:, :], ug[:, :], AF.Silu), reads=[("bigub", 0, j % 2)], writes=[("bigub", 0, j % 2)])
                emit("dve", tt(actb[:, j, :], ug[:, :], uv[:, :], ALU.mult),
                     reads=[("bigub", 0, j % 2), ("bigub", 1, j % 2)], writes=[("bigact", j)])

            s0 = W.get("wup%d" % L, 0, 4096)
            s1 = W.get("wup%d" % L, 1, 4096)
            grp = prog_groups([(s0, 0), (s0, 2048), (s1, 0), (s1, 2048)])
            W.release(s0)
            W.release(s1)
            post(0, 0, *grp[0])
            post(0, 1, *grp[1])
            gate(0)
            post(1, 0, *grp[2])
            post(1, 1, *grp[3])
            gate(1)
            for j in range(2, NF):
                slot = W.get("wup%d" % L, j, 4096)
                for half in range(2):
                    ps, pt = next_ps()
                    base = half * 2048
                    emit("pe", mm_group(ps[:, :], [(ring[slot][:, base + kc * P: base + (kc + 1) * P], hb[:, kc, :]) for kc in range(CH)]),
                         reads=[("ring", slot)] + [("hb", kc) for kc in range(CH)], writes=[pt])
                    post(j, half, ps, pt)
                W.release(slot)
                gate(j)
            for m in range(CH):
                slot = W.get("wdn%d" % L, m, 5632)
                ps, pt = next_ps()
                emit("pe", mm_group(ps[:, :], [(ring[slot][:, k * P:(k + 1) * P], actb[:, k, :]) for k in range(NF)]),
                     reads=[("ring", slot)] + [("bigact", k) for k in range(NF)], writes=[pt])
                W.release(slot)
                emit("dve", tt(xT[:, m, :], xT[:, m, :], ps[:, :], ALU.add), reads=[pt, ("x", m)], writes=[("x", m)])
                if stats_next:
                    xstat(m)

        def conformer(L, ti):
            pg.switch_big()
            rmsnorm_to_hb("norm_g", L)
            b_in = V("a_b_in", L)
            emit("pool", cp(Gb[:, :, 0:30], cvst[L][:, :, :]), reads=[("cvst", L)], writes=[("bigGh",)])
            s0 = W.get("win%d" % L, 0, 4096)
            s1 = W.get("win%d" % L, 1, 4096)
            grp0 = prog_groups([(s0, 0), (s0, 2048), (s1, 0), (s1, 2048)])
            W.release(s0)
            W.release(s1)
            for c in range(CH):
                if c < 2:
                    (psa, pta), (psg, ptg) = grp0[2 * c], grp0[2 * c + 1]
                else:
                    slot = W.get("win%d" % L, c, 4096)
                    psa, pta = next_ps()
                    emit("pe", mm_group(psa[:, :], [(ring[slot][:, kc * P:(kc + 1) * P], hb[:, kc, :]) for kc in range(CH)]),
                         reads=[("ring", slot)] + [("hb", kc) for kc in range(CH)], writes=[pta])
                    psg, ptg = next_ps()
                    emit("pe", mm_group(psg[:, :], [(ring[slot][:, 2048 + kc * P:2048 + (kc + 1) * P], hb[:, kc, :]) for kc in range(CH)]),
                         reads=[("ring", slot)] + [("hb", kc) for kc in range(CH)], writes=[ptg])
                    W.release(slot)
                sg = tmpf[2 + c % 2]
                emit("act", act_fn(sg[:, :], psg[:, :], AF.Sigmoid, bias=b_in[:, CH + c:CH + c + 1]),
                     reads=[ptg, ("vec",)], writes=[("tf", 2 + c % 2)])
                emit("dve", stt(Gb[:, c, 30:30 + T], psa[:, :], b_in[:, c:c + 1], sg[:, :], ALU.add, ALU.mult),
                     reads=[pta, ("tf", 2 + c % 2), ("vec",)], writes=[("bigG", c)])
            emit("pool", cp(cvst[L][:, :, :], Gb[:, :, T:T + 30]), reads=[("bigG", c) for c in range(CH)], writes=[("cvst", L)])
            dw_b = V("a_dw_b", L)
            it_y = []
            it_q = []
            for c in range(CH):
                slot = W.get("cvw%d" % L, c, 31 * P)
                ps, pt = next_ps()
                emit("pe", mm_group(ps[:, :], [(ring[slot][:, j * P:(j + 1) * P], Gb[:, c, j:j + T]) for j in range(31)]),
                     reads=[("ring", slot), ("bigG", c), ("bigGh",)], writes=[pt])
                W.release(slot)
                emit("act", act_fn(yb[:, c, :], ps[:, :], AF.Identity, bias=dw_b[:, c:c + 1]), reads=[pt, ("vec",)], writes=[("bigy", c)])
                sq = sqb[c % 4]
                emit("act", act_fn(sq[:, :], ps[:, :], AF.Square, bias=dw_b[:, c:c + 1]), reads=[pt, ("vec",)], writes=[("sq", c % 4)])
                it_y.append((yb[:, c, :], [("bigy", c)]))
                it_q.append((sq[:, :], [("sq", c % 4)]))
                presum_step(it_y, c, (2, 3))
                presum_step(it_q, c, (4, 5))
            ps_s, pt_s = next_ps()
            ps_q, pt_q = next_ps()
            presum_mm(ps_s, pt_s, (2, 3))
            presum_mm(ps_q, pt_q, (4, 5))
            mean = tmpf[1]
            emit("act", act_fn(mean[:, :], ps_s[:, :], AF.Identity, scale=1.0 / D), reads=[pt_s], writes=[("tf", 1)])
            msq = tmpf[4]
            emit("dve", tt(msq[:, :], mean[:, :], mean[:, :], ALU.mult), reads=[("tf", 1)], writes=[("tf", 4)])
            var = tmpf[5]
            emit("dve", stt(var[:, :], ps_q[:, :], 1.0 / D, msq[:, :], ALU.mult, ALU.subtract), reads=[pt_q, ("tf", 4)], writes=[("tf", 5)])
            emit("act", act_fn(var[:, :], var[:, :], AF.Sqrt, bias=V_eps()), reads=[("tf", 5), ("cst",)], writes=[("tf", 5)])
            emit("dve", lambda e: e.reciprocal(out=var[:, :], in_=var[:, :]), reads=[("tf", 5)], writes=[("tf", 5)])
            ln_g = V("a_ln_g", L)
            ln_b = V("a_ln_b", L)
            for c in range(CH):
                eng = "dve" if c % 3 != 2 else "pool"
                emit(eng, tt(yb[:, c, :], yb[:, c, :], mean[:, :], ALU.subtract), reads=[("bigy", c), ("tf", 1)], writes=[("bigy", c)])
                emit(eng, tt(yb[:, c, :], yb[:, c, :], var[:, :], ALU.mult), reads=[("bigy", c), ("tf", 5)], writes=[("bigy", c)])
                emit("act", act_fn(hb[:, c, :], yb[:, c, :], AF.Silu, bias=ln_b[:, c:c + 1], scale=ln_g[:, c:c + 1]),
                     reads=[("bigy", c), ("vec",)], writes=[("hb", c)])
            b_out = V("a_b_out", L)
            sl3 = [W.get("wout%d" % L, m, 2048) for m in range(3)]
            grp3 = prog_groups([(sl, 0) for sl in sl3])
            for sl in sl3:
                W.release(sl)
            for m in range(CH):
                if m < 3:
                    ps, pt = grp3[m]
                else:
                    slot = W.get("wout%d" % L, m, 2048)
                    ps, pt = next_ps()
                    emit("pe", mm_group(ps[:, :], [(ring[slot][:, kc * P:(kc + 1) * P], hb[:, kc, :]) for kc in range(CH)]),
                         reads=[("ring", slot)] + [("hb", kc) for kc in range(CH)], writes=[pt])
                    W.release(slot)
                emit("dve", stt(xT[:, m, :], ps[:, :], b_out[:, m:m + 1], xT[:, m, :], ALU.add, ALU.add),
                     reads=[pt, ("x", m), ("vec",)], writes=[("x", m)])
                xstat(m)

        def pool_mixer(L, ti):
            pg.switch_big()
            g = V("norm_g", L)
            ps, pt = next_ps()
            x_sumsq(ps, pt)
            rs = rstd_from(ps, pt, 1.0 / D, 0)
            emit("pool", cp(Hb[:, :, 0:15], plst[:, :, :]), reads=[("plst",)], writes=[("bigHh",)], small=True)
            for c in range(CH):
                emit("dve", stt(Hb[:, c, 15:15 + T], xT[:, c, :], g[:, c:c + 1], rs[:, :], ALU.mult, ALU.mult),
                     reads=[("x", c), ("tf", 0), ("vec",)], writes=[("bigH", c)])
            emit("pool", cp(plst[:, :, :], Hb[:, :, T:T + 15]), reads=[("bigH", c) for c in range(CH)], writes=[("plst",)])
            ic = C("invcnt")
            for c in range(CH):
                gi = c // 4
                w = POOL_W[gi]
                eng = "dve" if c % 2 == 0 else "pool"
                bufs = [pbuf[0][c % 2], pbuf[1][c % 2]]
                bt = [("bigpb", 0, c % 2, 0), ("bigpb", 1, c % 2, 0)]
                src = Hb[:, c, :]
                srct = [("bigH", c), ("bigHh",)]
                sh = 1
                bi = 0
                while sh < w:
                    dst = bufs[bi]
                    n = 15 + T
                    emit(eng, tt(dst[:, sh:n], src[:, sh:n], src[:, 0:n - sh], ALU.add), reads=srct, writes=[bt[bi]])
                    src = dst
                    srct = [bt[bi]]
                    bi ^= 1
                    sh *= 2
                emit("dve", stt(hb[:, c, :], src[:, 15:15 + T], 1.0 / w, Hb[:, c, 15:15 + T], ALU.mult, ALU.subtract),
                     reads=srct + [("bigH", c)], writes=[("hb", c)])
                if ti == 0:
                    t16 = tmpf[2 + c % 2]
                    emit(eng, tt(t16[:, 0:16], src[:, 15:31], ic[:, gi * 16:(gi + 1) * 16], ALU.mult), reads=srct + [("cst",)], writes=[("tf", 2 + c % 2)], small=True)
                    emit(eng, tt(hb[:, c, 0:16], t16[:, 0:16], Hb[:, c, 15:31], ALU.subtract), reads=[("tf", 2 + c % 2), ("bigH", c)], writes=[("hb", c)], small=True)
            sc = V("b_scale", L)
            for gi in range(4):
                slot = W.get("wg%d" % L, gi, 2048)
                for mc in range(4):
                    ps, pt = next_ps()
                    emit("pe", mm_group(ps[:, :], [(ring[slot][:, kc * 512 + mc * P: kc * 512 + (mc + 1) * P], hb[:, gi * 4 + kc, :]) for kc in range(4)]),
                         reads=[("ring", slot)] + [("hb", gi * 4 + kc) for kc in range(4)], writes=[pt])
                    m = gi * 4 + mc
                    emit("dve", stt(xT[:, m, :], ps[:, :], sc[:, m:m + 1], xT[:, m, :], ALU.mult, ALU.add),
                         reads=[pt, ("x", m), ("vec",)], writes=[("x", m)])
                    xstat(m)
                W.release(slot)

        def attention(L, ti):
            pg.switch_big()
            rmsnorm_to_hb("norm_g", L)
            emit("sp", lambda e: e.dma_start(out=posi, in_=pos_d[:, ti * T:(ti + 1) * T]), writes=[("tf", 3)], dsem=sems["ld"])
            ang = tmpf[1]
            emit("dve", cp(ang[:, :], posi), reads=[("tf", 3)], writes=[("tf", 1)])
            emit("dve", lambda e: e.tensor_single_scalar(out=ang[:, :], in_=ang[:, :], scalar=C("invf")[:, 0:1], op=ALU.mult),
                 reads=[("tf", 1), ("cst",)], writes=[("tf", 1)])
            emit("dve", lambda e: e.tensor_single_scalar(out=ang[:, :], in_=ang[:, :], scalar=1.0 / TWO_PI, op=ALU.mult),
                 reads=[("tf", 1)], writes=[("tf", 1)])
            for which, dstb, dtk in ((0, sinb, ("bigsin",)), (1, cosb, ("bigcos",))):
                qq = tmpf[4]
                if which == 1:
                    emit("dve", lambda e: e.tensor_single_scalar(out=ang[:, :], in_=ang[:, :], scalar=0.25, op=ALU.add),
                         reads=[("tf", 1)], writes=[("tf", 1)])
                emit("dve", cp(posi, ang[:, :]), reads=[("tf", 1)], writes=[("tf", 3)])
                emit("dve", cp(qq[:, :], posi), reads=[("tf", 3)], writes=[("tf", 4)])
                emit("dve", tt(qq[:, :], ang[:, :], qq[:, :], ALU.subtract), reads=[("tf", 1), ("tf", 4)], writes=[("tf", 4)])
                emit("act", act_fn(dstb, qq[:, :], AF.Sin, scale=6.28318), reads=[("tf", 4)], writes=[dtk])
            emit("dve", lambda e: e.tensor_single_scalar(out=sinb, in_=sinb, scalar=C("sgn")[:, 0:1], op=ALU.mult),
                 reads=[("bigsin",), ("cst",)], writes=[("bigsin",)])

            gq = V("c_q_norm_g", L)
            gk = V("c_k_norm_g", L)

            chunks = [("wq%d" % L, c, gq, QT[:, c, :], ("bigQ", c)) for c in range(CH)] + \
                     [("wk%d" % L, kv, gk, KT[:, kv, P:P + T], ("KTc", kv)) for kv in range(4)]
            sl3 = [W.get("wq%d" % L, c, 2048) for c in range(3)]
            grpq = prog_groups([(sl, 0) for sl in sl3])
            for sl in sl3:
                W.release(sl)
            st = {}

            def stageA(k):
                name, idx, gcol, dst, dtok = chunks[k]
                if k < 3:
                    ps, pt = grpq[k]
                else:
                    slot = W.get(name, idx, 2048)
                    ps, pt = next_ps()
                    emit("pe", mm_group(ps[:, :], [(ring[slot][:, kc * P:(kc + 1) * P], hb[:, kc, :]) for kc in range(CH)]),
                         reads=[("ring", slot)] + [("hb", kc) for kc in range(CH)], writes=[pt])
                    W.release(slot)
                held.add(pt[1])
                sq = sqb[k % 2]
                emit("act", act_fn(sq[:, :], ps[:, :], AF.Square), reads=[pt], writes=[("sq", k % 2)])
                st[k] = (ps, pt)

            def stageB(k):
                name, idx, gcol, dst, dtok = chunks[k]
                ps, pt = st[k]
                sq = sqb[k % 2]
                ps2, pt2 = next_ps()
                emit("pe", lambda e: e.matmul(ps2[:, :], C("bones"), sq[:, :], start=True, stop=True), reads=[("sq", k % 2), ("cst",)], writes=[pt2])
                rs = tmpf[2 + k % 2]
                rt = ("tf", 2 + k % 2)
                emit("act", act_fn(rs[:, :], ps2[:, :], AF.Sqrt, bias=V_eps(), scale=1.0 / 64), reads=[pt2, ("cst",)], writes=[rt])
                emit("dve", lambda e: e.reciprocal(out=rs[:, :], in_=rs[:, :]), reads=[rt], writes=[rt])
                qn = ubuf[0][k % 2]
                qt = ("bigub", 0, k % 2)
                emit("dve", stt(qn[:, :], ps[:, :], gcol[:, 0:1], rs[:, :], ALU.mult, ALU.mult), reads=[pt, rt, ("vec",)], writes=[qt])
                held.discard(pt[1])

            def stageC(k):
                name, idx, gcol, dst, dtok = chunks[k]
                qn = ubuf[0][k % 2]
                qt = ("bigub", 0, k % 2)
                ps3, pt3 = next_ps()
                emit("pe", lambda e: e.matmul(ps3[:, :], C("perm"), qn[:, :], start=True, stop=True), reads=[qt, ("cst",)], writes=[pt3])
                t2 = ubuf[1][k % 2]
                tt2 = ("bigub", 1, k % 2)
                emit("dve", tt(t2[:, :], ps3[:, :], sinb, ALU.mult), reads=[pt3, ("bigsin",)], writes=[tt2])
                emit("pool", tt(qn[:, :], qn[:, :], cosb, ALU.mult), reads=[qt, ("bigcos",)], writes=[qt])
                emit("dve", tt(dst, qn[:, :], t2[:, :], ALU.add), reads=[qt, tt2], writes=[dtok])

            nqk = len(chunks)
            for i in range(nqk + 2):
                if i < nqk:
                    stageA(i)
                if 0 <= i - 1 < nqk:
                    stageB(i - 1)
                if 0 <= i - 2 < nqk:
                    stageC(i - 2)
            for half in range(2):
                slot = W.get("wv%d" % L, half, 4096)
                for tb in range(4):
                    ps, pt = next_ps()
                    emit("pe", mm_group(ps[:, 0:256], [(hb[:, kc, tb * P:(tb + 1) * P], ring[slot][:, kc * 256:(kc + 1) * 256]) for kc in range(CH)]),
                         reads=[("ring", slot)] + [("hb", kc) for kc in range(CH)], writes=[pt])
                    emit("act", act_fn(VT[:, 1 + tb, half * 256:(half + 1) * 256], ps[:, 0:256], AF.Copy), reads=[pt], writes=[("VTc", tb, half)])
                W.release(slot)
            OT = hb
            kpi = 0
            for kv in range(4):
                for qb in range(4):
                    first = (ti == 0 and qb == 0)
                    pt_buf = PTb[kpi % 2]
                    ptt = ("bigP", kpi % 2)
                    kpi += 1
                    kbs = [1] if first else [0, 1]
                    for par in range(2):
                        for kb in kbs:
                            ps, pt = next_ps()
                            kcol = qb * P + kb * P
                            lhs = KT[par * 64:(par + 1) * 64, kv, kcol:kcol + P]
                            rhs = QT[par * 64:(par + 1) * 64, kv * 4:(kv + 1) * 4, qb * P:(qb + 1) * P]
                            mk = mprevb if kb == 0 else mcurb

                            def sfn(e, ps=ps, lhs=lhs, rhs=rhs, mk=mk):
                                e.matmul(ps[:, :], lhs, rhs, start=True, stop=False)
                                return e.matmul(ps[:, :], mk, identqb, start=False, stop=True)
                            emit("pe", sfn, reads=[("KTc", kv), ("KTh",), ("cb",)] + [("bigQ", kv * 4 + i) for i in range(4)], writes=[pt])
                            emit("act", act_fn(pt_buf[:, kb, par * 512:(par + 1) * 512], ps[:, :], AF.Exp, scale=0.125), reads=[pt], writes=[ptt + (par, kb)])
                    for par in range(2):
                        psn, ptn = next_ps()
                        psd, ptd = next_ps()

                        def nfn(e, psn=psn, par=par, kbs=kbs, qb=qb, kv=kv, pt_buf=pt_buf):
                            ins = None
                            for i, kb in enumerate(kbs):
                                vsrc = VT[:, qb + kb, kv * P:(kv + 1) * P]
                                ins = e.matmul(psn[:, :], vsrc, pt_buf[:, kb, par * 512:(par + 1) * 512], start=(i == 0), stop=(i == len(kbs) - 1))
                            return ins

                        def dfn(e, psd=psd, par=par, kbs=kbs, pt_buf=pt_buf):
                            ins = None
                            for i, kb in enumerate(kbs):
                                ins = e.matmul(psd[:, :], identb_ones, pt_buf[:, kb, par * 512:(par + 1) * 512], start=(i == 0), stop=(i == len(kbs) - 1))
                            return ins
                        rd = [ptt + (par, kb) for kb in kbs]
                        emit("pe", nfn, reads=rd + [("VTc", t_, h_) for t_ in range(4) for h_ in range(2)] + [("VTh",)], writes=[ptn])
                        emit("pe", dfn, reads=rd + [("cb",)], writes=[ptd])
                        lo, hi = par * 64, (par + 1) * 64
                        den = tmpf[4 + par]
                        dt_ = ("tf", 4 + par)
                        esv = est[lo:hi, kv * 8 + par: kv * 8 + 8: 2].unsqueeze(2).to_broadcast([64, 4, P])
                        emit("dve", tt(den[lo:hi, :].rearrange("p (a b) -> p a b", a=4), psd[lo:hi, :].rearrange("p (a b) -> p a b", a=4), esv, ALU.add),
                             reads=[ptd, ("est",)], writes=[dt_])
                        emit("dve", lambda e, den=den, lo=lo, hi=hi: e.reciprocal(out=den[lo:hi, :], in_=den[lo:hi, :]), reads=[dt_], writes=[dt_])
                        emit("dve", tt(OT[lo:hi, kv * 4:(kv + 1) * 4, qb * P:(qb + 1) * P], psn[lo:hi, :].rearrange("p (a b) -> p a b", a=4),
                                       den[lo:hi, :].rearrange("p (a b) -> p a b", a=4), ALU.mult),
                             reads=[ptn, dt_], writes=[("hb", kv * 4 + i_) for i_ in range(4)])
            emit("pool", cp(KT[:, :, 0:P], KT[:, :, T:T + P]), reads=[("KTc", kv) for kv in range(4)], writes=[("KTh",)])
            emit("pool", cp(VT[:, 0, :], VT[:, 4, :]), reads=[("VTc", 3, 0), ("VTc", 3, 1)], writes=[("VTh",)])
            for m in range(CH):
                slot = W.get("wo%d" % L, m, 2048)
                ps, pt = next_ps()
                emit("pe", mm_group(ps[:, :], [(ring[slot][:, kc * P:(kc + 1) * P], OT[:, kc, :]) for kc in range(CH)]),
                     reads=[("ring", slot)] + [("hb", kc) for kc in range(CH)], writes=[pt])
                W.release(slot)
                emit("dve", tt(xT[:, m, :], xT[:, m, :], ps[:, :], ALU.add), reads=[pt, ("x", m)], writes=[("x", m)])
                xstat(m)

        epsb = sb("epsb", [P, 1], F32)
        negpi = sb("negpi", [P, 1], F32)
        onesb = sb("onesb", [P, P], BF16)
        identb_ones = onesb[:, :]

        def body():
            state["ps"] = 0
            pg.reset()
            emit("pool", lambda e: e.memset(epsb[:, :], EPS), writes=[("cst2",)])
            emit("pool", lambda e: e.memset(negpi[:, :], -math.pi), writes=[("cst2",)])
            emit("pool", lambda e: e.memset(onesb[:, :], 1.0), writes=[("cst2",)])
            setup()
            for L in layers[:1]:
                cast_layer(L)
            if not pg.dry:
                cid = (sems["pool"], 3)
                for eng in ("pe", "act", "dve", "sp"):
                    pg.wait_all(eng, [cid])
            W.start() if not pg.dry else None
            outc = []
            for ti in range(NT):
                load_x(ti)
                for li, L in enumerate(layers):
                    if ti == 0 and li + 1 < len(layers):
                        cast_layer(layers[li + 1])
                    if do_mixer:
                        {"conf": conformer, "pool": pool_mixer, "attn": attention}[MIXERS[L]](L, ti)
                    if do_ffn:
                        ffn(L, ti, stats_next=(li + 1 < len(layers)))
                outc += store_x(ti)
            return outc

        pg.dry = True
        body()
        pg.dry = False
        outc = body()
        last = {}
        for s, v in outc:
            last[s.name] = (s, max(v, last.get(s.name, (s, 0))[1]))
        pg.wait_all("sp", list(last.values()))

        with nc.Block() as block:
            @block.sync
            def _(e):
                for f in pg.q["sp"]:
                    f(e)

            @block.tensor
            def _(e):
                for f in pg.q["pe"]:
                    f(e)

            @block.scalar
            def _(e):
                for f in pg.q["act"]:
                    f(e)

            @block.vector
            def _(e):
                for f in pg.q["dve"]:
                    f(e)

            @block.gpsimd
            def _(e):
                for f in pg.q["pool"]:
                    f(e)
        print("instructions:", pg.n_inst, {k: len(v) for k, v in pg.q.items()}, "sbuf left", nc.sbuf_bytes_remaining)
    return nc


def prepare_shared(inp, layers):
    vecs, voffs = pack_vec(inp)
    csts, coffs, cbarr = make_consts()
    slabs = weight_slabs(inp, layers)
    return vecs, voffs, csts, coffs, slabs, cbarr


def run(inp, S, layers, n_cores, do_ffn=True, do_mixer=True):
    vecs, voffs, csts, coffs, slabs, cbarr = prepare_shared(inp, layers)
    slab_shapes = {k: v.shape for k, v in slabs.items()}
    import time as _time
    _t0 = _time.time()
    nc = build_program(S, layers, slab_shapes, voffs, coffs, vecs.shape[1], csts.shape[1], do_ffn=do_ffn, do_mixer=do_mixer)
    print("build time", _time.time() - _t0, flush=True)
    x = np.asarray(inp["x"], np.float32)
    pos = np.asarray(inp["positions"], np.int32)
    in_maps = []
    for b in range(n_cores):
        m = {"x": np.ascontiguousarray(x[b]),
             "posb": np.ascontiguousarray(np.broadcast_to(pos[b][None, :], (P, S))),
             "vec": vecs, "cst": csts, "cstb": cbarr}
        m.update(slabs)
        in_maps.append(m)
    _t0 = _time.time()
    import os as _os
    _tr = _os.environ.get("KTRACE", "0") == "1"
    res = run_bass_kernel_spmd(nc, in_maps, core_ids=list(range(n_cores)), **({"trace": True} if _tr else {}))
    if _tr:
        print("exec_time_ns", res.exec_time_ns, flush=True)
    print("run time", _time.time() - _t0, flush=True)
    return np.stack([r["out"] for r in res.results], axis=0)


_INPUT_NAMES = (
    "x", "positions",
    "l0_norm_g", "l0_a_w_in", "l0_a_b_in", "l0_a_dw_w", "l0_a_dw_b", "l0_a_ln_g", "l0_a_ln_b", "l0_a_w_out", "l0_a_b_out",
    "l0_ffn_norm_g", "l0_ffn_w_up", "l0_ffn_dw_w", "l0_ffn_dw_b", "l0_ffn_w_down",
    "l1_norm_g", "l1_b_w_group", "l1_b_scale",
    "l1_ffn_norm_g", "l1_ffn_w_up", "l1_ffn_dw_w", "l1_ffn_dw_b", "l1_ffn_w_down",
    "l2_norm_g", "l2_c_w_qkv", "l2_c_q_norm_g", "l2_c_k_norm_g", "l2_c_sinks", "l2_c_w_o",
    "l2_ffn_norm_g", "l2_ffn_w_up", "l2_ffn_dw_w", "l2_ffn_dw_b", "l2_ffn_w_down",
    "l3_norm_g", "l3_a_w_in", "l3_a_b_in", "l3_a_dw_w", "l3_a_dw_b", "l3_a_ln_g", "l3_a_ln_b", "l3_a_w_out", "l3_a_b_out",
    "l3_ffn_norm_g", "l3_ffn_w_up", "l3_ffn_dw_w", "l3_ffn_dw_b", "l3_ffn_w_down",
)


def kernel(**inputs):
    inp = {k: inputs[k] for k in _INPUT_NAMES}
    return run(inp, 4096, [0, 1, 2, 3], 8).astype(np.float32)
```

```python
import math
import numpy as np
import concourse.bass as bass
import concourse.mybir as mybir
from concourse.bass_utils import run_bass_kernel_spmd

F32 = mybir.dt.float32
BF16 = mybir.dt.bfloat16
I32 = mybir.dt.int32
AF = mybir.ActivationFunctionType
ALU = mybir.AluOpType

P = 128
D = 2048
CH = 16
T = 512
DFF = 5632
NF = 44
EPS = 1e-6
RING = 4
SLOTW = 5632
NEG = -30000.0
TWO_PI = 2.0 * math.pi
ROPE_THETA = 500000.0
POOL_W = (2, 4, 8, 16)
CONV_DVE_TAPS = 6
MIXERS = ("conf", "pool", "attn", "conf")


class Sem:
    def __init__(self, h, name):
        self.h = h
        self.name = name
        self.val = 0


class Prog:
    ENGS = ("pe", "act", "dve", "pool", "sp")

    def __init__(self, nc, sems):
        self.nc = nc
        self.q = {e: [] for e in self.ENGS}
        self.esem = {e: sems[e] for e in ("pe", "act", "dve", "pool")}
        self.waited = {e: {} for e in self.ENGS}
        self.tok = {}
        self.big_frontier = []
        self.dry = False
        self.n_inst = 0
        self.small = set()

    def reset(self):
        self.tok = {}
        self.big_frontier = []

    def _entry(self, t):
        e = self.tok.get(t)
        if e is None:
            e = {"w": None, "r": []}
            if isinstance(t, tuple) and isinstance(t[0], str) and t[0].startswith("big"):
                e["r"] = list(self.big_frontier)
            self.tok[t] = e
        return e

    def switch_big(self):
        if self.dry:
            return
        fr = {}
        for t in list(self.tok.keys()):
            if isinstance(t, tuple) and isinstance(t[0], str) and t[0].startswith("big"):
                e = self.tok.pop(t)
                for d in ([e["w"]] if e["w"] else []) + e["r"]:
                    if d[0].name not in fr or fr[d[0].name][1] < d[1]:
                        fr[d[0].name] = d
        for d in self.big_frontier:
            if d[0].name not in fr or fr[d[0].name][1] < d[1]:
                fr[d[0].name] = d
        self.big_frontier = list(fr.values())

    def set_writer(self, t, sem, val):
        self.tok[t] = {"w": (sem, val), "r": []}

    def emit(self, eng, fn, reads=(), writes=(), dsem=None, small=False):
        if self.dry:
            return
        deps = {}

        def add(d):
            if d is None:
                return
            s, v = d
            if dsem is None and eng in self.esem and s is self.esem[eng] and (s.name, v) not in self.small:
                return
            if s.name not in deps or deps[s.name][1] < v:
                deps[s.name] = (s, v)

        for t in reads:
            add(self._entry(t)["w"])
        for t in writes:
            e = self._entry(t)
            add(e["w"])
            for d in e["r"]:
                add(d)
        wl = []
        wd = self.waited[eng]
        for name, (s, v) in deps.items():
            if wd.get(name, 0) >= v:
                continue
            wd[name] = v
            wl.append((s.h, v))
        if dsem is not None:
            dsem.val += 16
            cid = (dsem, dsem.val)
            inc = (dsem.h, 16)
        else:
            s = self.esem[eng]
            s.val += 1
            cid = (s, s.val)
            inc = (s.h, 1)

        def run(e, wl=wl, fn=fn, inc=inc):
            for h, v in wl:
                e.wait_ge(h, v)
            fn(e).then_inc(inc[0], inc[1])

        self.q[eng].append(run)
        self.n_inst += 1
        if small:
            self.small.add((cid[0].name, cid[1]))
        for t in writes:
            self.tok[t] = {"w": cid, "r": []}
        for t in reads:
            if t in writes:
                continue
            e = self._entry(t)
            r = [d for d in e["r"] if d[0] is not cid[0]]
            r.append(cid)
            e["r"] = r
        return cid

    def wait_all(self, eng, cids):
        if self.dry:
            return
        wl = [(s.h, v) for s, v in cids]

        def run(e, wl=wl):
            for h, v in wl:
                e.wait_ge(h, v)

        self.q[eng].append(run)


def _fm(v):
    v = np.asarray(v, dtype=np.float32)
    return np.ascontiguousarray(v.reshape(-1, P).T)


def _slab_pair(w, npair):
    K = w.shape[0]
    kc = K // P
    a = np.asarray(w).reshape(kc, P, 2, npair, P)
    return np.ascontiguousarray(a.transpose(3, 1, 2, 0, 4)).reshape(npair, P, 2 * kc * P)


def _slab_m(w):
    K, M = w.shape
    a = np.asarray(w).reshape(K // P, P, M // P, P)
    return np.ascontiguousarray(a.transpose(2, 1, 0, 3)).reshape(M // P, P, (K // P) * P)


def make_consts():
    c = {}
    c["ident"] = np.eye(P, dtype=np.float32)
    c["ones"] = np.ones((P, P), np.float32)
    bo = np.zeros((P, P), np.float32)
    bo[:64, :64] = 1
    bo[64:, 64:] = 1
    c["bones"] = bo
    pm = np.zeros((P, P), np.float32)
    for m in range(P):
        dh = m % 64
        if dh < 8:
            pm[m + 8, m] = 1
        elif dh < 16:
            pm[m - 8, m] = 1
    c["perm"] = pm
    ii = np.arange(P)[:, None]
    jj = np.arange(P)[None, :]
    cbf = {}
    cbf["ident"] = np.eye(P, dtype=np.float32)
    cbf["mprev"] = np.where(jj > ii, 0.0, NEG).astype(np.float32)
    cbf["mcur"] = np.where(jj <= ii, 0.0, NEG).astype(np.float32)
    cbf["identq"] = np.tile(np.eye(P, dtype=np.float32), (1, 4))
    inv_freq = (np.float32(ROPE_THETA) ** (-(np.arange(0, 16, 2, dtype=np.float32)) / np.float32(16))).astype(np.float32)
    invf = np.zeros((P, 1), np.float32)
    nsgn = np.zeros((P, 1), np.float32)
    for p in range(P):
        dh = p % 64
        if dh < 16:
            invf[p, 0] = inv_freq[dh % 8]
            nsgn[p, 0] = -1.0 if dh < 8 else 1.0
    c["invf"] = invf
    c["sgn"] = nsgn
    ic = np.zeros((P, 4 * 16), np.float32)
    for g, w in enumerate(POOL_W):
        ic[:, g * 16:(g + 1) * 16] = (1.0 / np.minimum(np.arange(16) + 1, w)).astype(np.float32)[None, :]
    c["invcnt"] = ic
    cbarr = np.ascontiguousarray(np.concatenate([cbf["ident"], cbf["mprev"], cbf["mcur"], cbf["identq"]], axis=1))
    offs = {}
    cols = []
    o = 0
    for k, v in c.items():
        offs[k] = (o, v.shape[1])
        cols.append(v)
        o += v.shape[1]
    return np.ascontiguousarray(np.concatenate(cols, axis=1)), offs, cbarr


def pack_vec(inp):
    parts = []
    offs = {}
    o = 0

    def add(name, a):
        nonlocal o
        a = np.ascontiguousarray(a, dtype=np.float32)
        offs[name] = (o, a.shape[1])
        parts.append(a)
        o += a.shape[1]

    for L in range(4):
        p = "l%d_" % L
        add(p + "norm_g", _fm(inp[p + "norm_g"]))
        add(p + "ffn_norm_g", _fm(inp[p + "ffn_norm_g"]))
        dw = np.asarray(inp[p + "ffn_dw_w"], np.float32).reshape(3, 88, P).transpose(2, 1, 0).reshape(P, 88 * 3)
        add(p + "ffn_dw_w", dw)
        add(p + "ffn_dw_b", _fm(inp[p + "ffn_dw_b"]))
        if MIXERS[L] == "conf":
            add(p + "a_b_in", _fm(inp[p + "a_b_in"]))
            dwc = np.asarray(inp[p + "a_dw_w"], np.float32).reshape(31, 16, P).transpose(2, 1, 0).reshape(P, 16 * 31)
            add(p + "a_dw_w", dwc)
            add(p + "a_dw_b", _fm(inp[p + "a_dw_b"]))
            add(p + "a_ln_g", _fm(inp[p + "a_ln_g"]))
            add(p + "a_ln_b", _fm(inp[p + "a_ln_b"]))
            add(p + "a_b_out", _fm(inp[p + "a_b_out"]))
        elif MIXERS[L] == "pool":
            add(p + "b_scale", _fm(inp[p + "b_scale"]))
        else:
            add(p + "c_q_norm_g", np.tile(np.asarray(inp[p + "c_q_norm_g"], np.float32), 2).reshape(P, 1))
            add(p + "c_k_norm_g", np.tile(np.asarray(inp[p + "c_k_norm_g"], np.float32), 2).reshape(P, 1))
            add(p + "c_sinks", np.broadcast_to(np.asarray(inp[p + "c_sinks"], np.float32)[None, :], (P, 32)))
    return np.ascontiguousarray(np.concatenate(parts, axis=1)), offs


def weight_slabs(inp, layers):
    out = {}
    for L in layers:
        p = "l%d_" % L
        out["wup%d" % L] = _slab_pair(inp[p + "ffn_w_up"], NF)
        out["wdn%d" % L] = _slab_m(inp[p + "ffn_w_down"])
        if MIXERS[L] == "conf":
            out["win%d" % L] = _slab_pair(inp[p + "a_w_in"], 16)
            out["wout%d" % L] = _slab_m(inp[p + "a_w_out"])
        elif MIXERS[L] == "pool":
            wg = np.asarray(inp[p + "b_w_group"]).reshape(4, 4, P, 512)
            out["wg%d" % L] = np.ascontiguousarray(wg.transpose(0, 2, 1, 3)).reshape(4, P, 2048)
        else:
            wqkv = np.asarray(inp[p + "c_w_qkv"])
            out["wq%d" % L] = _slab_m(wqkv[:, :2048])
            wk = wqkv[:, 2048:2304].reshape(16, P, 4, 64)
            wk = np.concatenate([wk, wk], axis=3)
            out["wk%d" % L] = np.ascontiguousarray(wk.transpose(2, 1, 0, 3)).reshape(4, P, 2048)
            wv = wqkv[:, 2304:2560].reshape(16, P, 2, 2, 64)
            wv = np.stack([wv, wv], axis=4)
            out["wv%d" % L] = np.ascontiguousarray(wv.transpose(2, 1, 0, 3, 4, 5)).reshape(2, P, 16 * 256)
            out["wo%d" % L] = _slab_m(inp[p + "c_w_o"])
    return out


def build_program(S, layers, slab_shapes, voffs, coffs, nvec, ncst, do_ffn=True, do_mixer=True):
    NT = S // T
    nc = bass.Bass("TRN2", target_bir_lowering=False)
    x_d = nc.dram_tensor("x", [S, D], F32, kind="ExternalInput").ap()
    pos_d = nc.dram_tensor("posb", [P, S], I32, kind="ExternalInput").ap()
    vec_d = nc.dram_tensor("vec", [P, nvec], F32, kind="ExternalInput").ap()
    cst_d = nc.dram_tensor("cst", [P, ncst], F32, kind="ExternalInput").ap()
    cstb_d = nc.dram_tensor("cstb", [P, 7 * P], F32, kind="ExternalInput").ap()
    out_d = nc.dram_tensor("out", [S, D], F32, kind="ExternalOutput").ap()
    w_in = {}
    w_sc = {}
    for name, shp in slab_shapes.items():
        w_in[name] = nc.dram_tensor(name, list(shp), F32, kind="ExternalInput").ap()
        w_sc[name] = nc.dram_tensor(name + "_b", list(shp), BF16).ap()
    for L in layers:
        if MIXERS[L] == "conf":
            w_sc["cvw%d" % L] = nc.dram_tensor("cvw%d_b" % L, [16, P, 31 * P], BF16).ap()

    sb = nc.alloc_sbuf_tensor
    xT = sb("xT", [P, CH, T], F32)
    hb = sb("hb", [P, CH, T], BF16)
    BIG = sb("BIG", [P, NF * T], BF16)
    ring = [sb("ring%d" % i, [P, SLOTW], BF16) for i in range(RING)]
    vec = sb("vecs", [P, nvec], F32)
    cst = sb("csts", [P, ncst], F32)
    cb = sb("cstbs", [P, 7 * P], BF16)
    R2 = sb("R2", [P, CH * (30 + T)], BF16)
    R2f = R2[:, :].bitcast(F32)
    PBW = 15 + T
    pbuf = [[R2f[:, (a * 2 + b) * PBW:(a * 2 + b + 1) * PBW] for b in range(2)] for a in range(2)]
    ubuf = [[R2f[:, 4 * PBW + (a * 2 + b) * T: 4 * PBW + (a * 2 + b + 1) * T] for b in range(2)] for a in range(2)]
    sqb = [sb("sq%d" % i, [P, T], F32) for i in range(4)]
    tmpf = [sb("tf%d" % i, [P, T], F32) for i in range(6)]
    ffst = {L: sb("ffst%d" % L, [P, 88, 2], F32) for L in layers}
    Gb = R2[:, :].rearrange("p (c t) -> p c t", c=CH)
    cvst = {L: sb("cvst%d" % L, [P, CH, 30], BF16) for L in layers if MIXERS[L] == "conf"}
    plst = sb("plst", [P, CH, 15], F32)
    KT = sb("KT", [P, 4, P + T], BF16)
    VT = sb("VT", [P, 5, 512], BF16)
    est = sb("est", [P, 32], F32)
    psum = [nc.alloc_psum_tensor("ps%d" % i, [P, 512], F32) for i in range(8)]

    actb = BIG[:, 0:NF * T].rearrange("p (j t) -> p j t", j=NF)
    bigf = BIG[:, :].bitcast(F32)
    yb = bigf[:, 0:CH * T].rearrange("p (c t) -> p c t", c=CH)
    Hb = bigf[:, 0:CH * (15 + T)].rearrange("p (c t) -> p c t", c=CH)
    QT = BIG[:, 0:CH * T].rearrange("p (c t) -> p c t", c=CH)
    PTb = [BIG[:, CH * T + i * 2048: CH * T + (i + 1) * 2048].rearrange("p (k q) -> p k q", k=2) for i in range(2)]
    stage = [bigf[:, i * D:(i + 1) * D] for i in range(2)]
    cosb = bigf[:, 6144:6144 + T]
    sinb = bigf[:, 6144 + T:6144 + 2 * T]
    dgb = [BIG[:, i * 31 * P:(i + 1) * 31 * P] for i in range(2)]
    posi = tmpf[3][:, :].bitcast(I32)

    def V(name, L=None):
        key = name if L is None else "l%d_%s" % (L, name)
        o, n = voffs[key]
        return vec[:, o:o + n]

    def C(name):
        o, n = coffs[name]
        return cst[:, o:o + n]

    identb = cb[:, 0:P]
    mprevb = cb[:, P:2 * P]
    mcurb = cb[:, 2 * P:3 * P]
    identqb = cb[:, 3 * P:7 * P]

    import contextlib
    with contextlib.ExitStack() as es:
        sems = {}
        for nm in ("pe", "act", "dve", "pool", "pre", "ld", "ld2", "xin0", "xin1", "xout0", "xout1", "misc0", "misc1") + tuple("r%d" % i for i in range(RING)):
            sems[nm] = Sem(es.enter_context(nc.semaphore(nm)), nm)
        for nm in slab_shapes:
            sems["c_" + nm] = Sem(es.enter_context(nc.semaphore("c_" + nm)), "c_" + nm)
        pg = Prog(nc, sems)
        emit = pg.emit
        state = {"ps": 0}

        held = set()

        def next_ps():
            while True:
                i = state["ps"] % 8
                state["ps"] += 1
                if i not in held:
                    return psum[i], ("ps", i)

        class WStream:
            def __init__(self):
                self.plan = []
                self.i = 0
                self.issued = 0

            def _issue(self, k):
                name, idx, width = self.plan[k]
                slot = k % RING
                src = w_sc[name][idx]
                emit("sp", lambda e, slot=slot, src=src, width=width: e.dma_start(out=ring[slot][:, 0:width], in_=src),
                     reads=[("wsc", name), ("wscratch2", 0), ("wscratch2", 1)], writes=[("ring", slot)], dsem=sems["r%d" % slot])

            def start(self):
                self.i = 0
                self.issued = 0
                while self.issued < min(RING, len(self.plan)):
                    self._issue(self.issued)
                    self.issued += 1

            def get(self, name, idx, width):
                if pg.dry:
                    self.plan.append((name, idx, width))
                    self.i += 1
                    return 0
                assert self.plan[self.i] == (name, idx, width), (self.plan[self.i], name, idx, width)
                slot = self.i % RING
                self.i += 1
                return slot

            def release(self, slot):
                if pg.dry:
                    return
                if self.issued < len(self.plan):
                    assert self.issued % RING == slot
                    self._issue(self.issued)
                    self.issued += 1

        W = WStream()

        def mm_group(ps_ap, pairs, start=True, stop=True):
            def fn(e):
                n = len(pairs)
                ins = None
                for i, (l, r) in enumerate(pairs):
                    ins = e.matmul(ps_ap, l, r, start=(start and i == 0), stop=(stop and i == n - 1))
                return ins
            return fn

        def act_fn(out, in_, func, bias=None, scale=None):
            kw = {}
            if bias is not None:
                kw["bias"] = bias
            if scale is not None:
                kw["scale"] = scale
            return lambda e: e.activation(out=out, in_=in_, func=func, **kw)

        def stt(out, in0, scalar, in1, op0, op1):
            return lambda e: e.scalar_tensor_tensor(out=out, in0=in0, scalar=scalar, in1=in1, op0=op0, op1=op1)

        def tt(out, in0, in1, op):
            return lambda e: e.tensor_tensor(out=out, in0=in0, in1=in1, op=op)

        def cp(out, in_):
            return lambda e: e.tensor_copy(out=out, in_=in_)

        def setup():
            emit("sp", lambda e: e.dma_start(out=vec[:, :], in_=vec_d), writes=[("vec",)], dsem=sems["ld"])
            emit("sp", lambda e: e.dma_start(out=cst[:, :], in_=cst_d), writes=[("cst",)], dsem=sems["ld2"])
            if not pg.dry:
                sems["pre"].val += 16
                pg.q["pool"].append(lambda e: e.dma_start(out=cb[:, :], in_=cstb_d).then_inc(sems["pre"].h, 16))
                pg.set_writer(("cb",), sems["pre"], sems["pre"].val)
            for L in layers:
                emit("pool", lambda e, L=L: e.memset(ffst[L][:, :, :], 0.0), writes=[("ffst", L)])
                if MIXERS[L] == "conf":
                    emit("pool", lambda e, L=L: e.memset(cvst[L][:, :, :], 0.0), writes=[("cvst", L)])
            emit("pool", lambda e: e.memset(plst[:, :, :], 0.0), writes=[("plst",)])
            emit("pool", lambda e: e.memset(KT[:, :, :], 0.0), writes=[("KT",)])
            emit("pool", lambda e: e.memset(VT[:, :, :], 0.0), writes=[("VT",)])
            if 2 in layers:
                emit("act", act_fn(est[:, :], V("c_sinks", 2), AF.Exp), reads=[("vec",)], writes=[("est",)])
            k = 0
            for L in layers:
                if MIXERS[L] != "conf":
                    continue
                dwc = V("a_dw_w", L)
                for c in range(CH):
                    buf = dgb[k % 2]
                    tokb = ("bigdgb", k % 2)
                    for j in range(31):
                        eng = "dve"
                        emit(eng, (lambda e, buf=buf, j=j, c=c, dwc=dwc: e.tensor_scalar_mul(
                            out=buf[:, j * P:(j + 1) * P], in0=identb, scalar1=dwc[:, c * 31 + j:c * 31 + j + 1])),
                            reads=[("cb",), ("vec",)], writes=[tokb + (j % 2,)])
                    emit("sp", lambda e, buf=buf, L=L, c=c: e.dma_start(out=w_sc["cvw%d" % L][c], in_=buf),
                         reads=[tokb + (0,), tokb + (1,)], writes=[("cvwd", L, c)], dsem=sems["misc%d" % (k % 2)])
                    k += 1
            if not pg.dry:
                pg.set_writer(("wscratch2", 0), sems["misc0"], sems["misc0"].val)
                pg.set_writer(("wscratch2", 1), sems["misc1"], sems["misc1"].val)

        def layer_tensors(L):
            m = MIXERS[L]
            if m == "conf":
                names = ["win%d" % L, "wout%d" % L]
            elif m == "pool":
                names = ["wg%d" % L]
            else:
                names = ["wq%d" % L, "wk%d" % L, "wv%d" % L, "wo%d" % L]
            return names + ["wup%d" % L, "wdn%d" % L]

        def cast_layer(L):
            if pg.dry:
                return
            for name in layer_tensors(L):
                shp = slab_shapes[name]
                ns, width = shp[0], shp[2]
                per = max(1, (8 << 20) // (P * width * 4))
                sm = sems["c_" + name]
                s0 = 0
                while s0 < ns:
                    s1 = min(ns, s0 + per)
                    src = w_in[name][s0:s1].rearrange("j p w -> (j p) w")
                    dst = w_sc[name][s0:s1].rearrange("j p w -> (j p) w")
                    sm.val += 16
                    pg.q["pool"].append(lambda e, src=src, dst=dst, sm=sm: e.dma_start(out=dst, in_=src).then_inc(sm.h, 16))
                    s0 = s1
                pg.set_writer(("wsc", name), sm, sm.val)

        def load_x(ti):
            pg.switch_big()
            xs_state["ready"] = False
            for tb in range(4):
                k = tb % 2
                st = stage[k]
                r0 = ti * T + tb * P
                emit("sp", lambda e, st=st, r0=r0: e.dma_start(out=st, in_=x_d[r0:r0 + P, :]),
                     writes=[("bigstage", k)], dsem=sems["xin%d" % k])
                for cg in range(4):
                    ps, pt = next_ps()

                    def tr(e, ps=ps, st=st, cg=cg):
                        ins = None
                        for q in range(4):
                            c = cg * 4 + q
                            ins = e.transpose(out=ps[:, q * P:(q + 1) * P], in_=st[:, c * P:(c + 1) * P], identity=C("ident"))
                        return ins
                    emit("pe", tr, reads=[("bigstage", k), ("cst",)], writes=[pt])
                    eng = "act" if cg % 2 == 0 else "dve"
                    dst = xT[:, cg * 4:(cg + 1) * 4, tb * P:(tb + 1) * P]
                    src = ps[:, :].rearrange("p (a b) -> p a b", a=4)
                    if eng == "act":
                        emit("act", act_fn(dst, src, AF.Copy), reads=[pt], writes=[("x", cg * 4 + q) for q in range(4)])
                    else:
                        emit("dve", cp(dst, src), reads=[pt], writes=[("x", cg * 4 + q) for q in range(4)])
                    if tb == 3:
                        for q in range(4):
                            xstat(cg * 4 + q)

        def store_x(ti):
            pg.switch_big()
            cids = []
            for tb in range(4):
                k = tb % 2
                st = stage[k]
                for cg in range(4):
                    ps, pt = next_ps()

                    def tr(e, ps=ps, cg=cg, tb=tb):
                        ins = None
                        for q in range(4):
                            c = cg * 4 + q
                            ins = e.transpose(out=ps[:, q * P:(q + 1) * P], in_=xT[:, c, tb * P:(tb + 1) * P], identity=C("ident"))
                        return ins
                    emit("pe", tr, reads=[("x", cg * 4 + q) for q in range(4)] + [("cst",)], writes=[pt])
                    eng = "act" if cg % 2 == 0 else "dve"
                    dst = st[:, cg * 512:(cg + 1) * 512]
                    if eng == "act":
                        emit("act", act_fn(dst, ps[:, :], AF.Copy), reads=[pt], writes=[("bigstage", k, cg)])
                    else:
                        emit("dve", cp(dst, ps[:, :]), reads=[pt], writes=[("bigstage", k, cg)])
                r0 = ti * T + tb * P
                cid = emit("sp", lambda e, st=st, r0=r0: e.dma_start(out=out_d[r0:r0 + P, :], in_=st),
                           reads=[("bigstage", k, cg) for cg in range(4)], writes=[("outd", ti, tb)], dsem=sems["xout%d" % k])
                for cg in range(4):
                    pass
                cids.append(cid)
            return cids

        def col_stats(src_fn, n, ps, pt, func=AF.Square, bias_fn=None):
            items = []
            for c in range(n):
                sq = sqb[c % 4]
                emit("act", act_fn(sq[:, :], src_fn(c)[0], func, bias=(bias_fn(c) if bias_fn else None)),
                     reads=src_fn(c)[1], writes=[("sq", c % 4)])
                items.append((sq[:, :], [("sq", c % 4)]))
                presum_step(items, c, (4, 5))
            presum_mm(ps, pt, (4, 5))

        def presum_step(items, c, accs, engs=("dve", "dve")):
            if c < 2:
                return
            par = c % 2
            eng = engs[par]
            acc = tmpf[accs[par]]
            at = ("tf", accs[par])
            if c < 4:
                emit(eng, tt(acc[:, :], items[c - 2][0], items[c][0], ALU.add), reads=items[c - 2][1] + items[c][1], writes=[at])
            else:
                emit(eng, tt(acc[:, :], acc[:, :], items[c][0], ALU.add), reads=[at] + items[c][1], writes=[at])

        def presum_mm(ps, pt, accs):
            a0, a1 = tmpf[accs[0]], tmpf[accs[1]]

            def fn(e):
                e.matmul(ps[:, :], C("ones"), a0[:, :], start=True, stop=False)
                return e.matmul(ps[:, :], C("ones"), a1[:, :], start=False, stop=True)
            emit("pe", fn, reads=[("tf", accs[0]), ("tf", accs[1]), ("cst",)], writes=[pt])

        def rstd_from(ps, pt, scale, out_i):
            sd = tmpf[out_i]
            emit("act", act_fn(sd[:, :], ps[:, :], AF.Sqrt, bias=V_eps(), scale=scale), reads=[pt, ("cst",)], writes=[("tf", out_i)])
            emit("dve", lambda e: e.reciprocal(out=sd[:, :], in_=sd[:, :]), reads=[("tf", out_i)], writes=[("tf", out_i)])
            return sd

        def V_eps():
            return epsb[:, 0:1]

        xs_state = {"items": [], "ready": False}

        def xstat(c):
            if c == 0:
                xs_state["items"] = []
            sq = sqb[c % 4]
            emit("act", act_fn(sq[:, :], xT[:, c, :], AF.Square), reads=[("x", c)], writes=[("sq", c % 4)])
            xs_state["items"].append((sq[:, :], [("sq", c % 4)]))
            presum_step(xs_state["items"], c, (4, 5))
            if c == CH - 1:
                xs_state["ready"] = True

        def x_sumsq(ps, pt):
            if xs_state["ready"]:
                xs_state["ready"] = False
                presum_mm(ps, pt, (4, 5))
            else:
                col_stats(lambda c: (xT[:, c, :], [("x", c)]), CH, ps, pt)

        def prog_groups(sb_list):
            outs = [next_ps() for _ in sb_list]
            slots = sorted(set(sl for sl, _ in sb_list))
            for kc in range(CH):
                def fn(e, kc=kc):
                    ins = None
                    for (sl, base), (ps, pt) in zip(sb_list, outs):
                        ins = e.matmul(ps[:, :], ring[sl][:, base + kc * P: base + (kc + 1) * P], hb[:, kc, :],
                                       start=(kc == 0), stop=(kc == CH - 1))
                    return ins
                emit("pe", fn, reads=[("ring", sl) for sl in slots] + [("hb", kc)], writes=[pt for _, pt in outs])
            return outs

        def rmsnorm_to_hb(gname, L):
            g = V(gname, L)
            ps, pt = next_ps()
            x_sumsq(ps, pt)
            rs = rstd_from(ps, pt, 1.0 / D, 0)
            for c in range(CH):
                emit("dve", stt(hb[:, c, :], xT[:, c, :], g[:, c:c + 1], rs[:, :], ALU.mult, ALU.mult),
                     reads=[("x", c), ("tf", 0), ("vec",)], writes=[("hb", c)])

        def ffn(L, ti, stats_next=True):
            pg.switch_big()
            rmsnorm_to_hb("ffn_norm_g", L)
            dww = V("ffn_dw_w", L)
            dwb = V("ffn_dw_b", L)

            def post(j, half, ps, pt):
                f = half * NF + j
                pb = pbuf[half][j % 2]
                pbt = ("bigpb", half, j % 2)
                ub = ubuf[half][j % 2]
                ubt = ("bigub", half, j % 2)
                emit("pool", cp(pb[:, 0:2], ffst[L][:, f, :]), reads=[("ffst", L, f), ("ffst", L)], writes=[pbt + (0,)])
                emit("act", act_fn(pb[:, 2:2 + T], ps[:, :], AF.Copy), reads=[pt], writes=[pbt + (1,)])
                emit("pool", cp(ffst[L][:, f, :], pb[:, T:T + 2]), reads=[pbt + (0,), pbt + (1,)], writes=[("ffst", L, f)])
                emit("act", act_fn(ub[:, :], ps[:, :], AF.Identity, bias=dwb[:, f:f + 1], scale=dww[:, f * 3 + 2: f * 3 + 3]),
                     reads=[pt, ("vec",)], writes=[ubt])
                emit("dve", stt(ub[:, :], pb[:, 1:1 + T], dww[:, f * 3 + 1:f * 3 + 2], ub[:, :], ALU.mult, ALU.add),
                     reads=[pbt + (0,), pbt + (1,), ubt], writes=[ubt])
                emit("dve", stt(ub[:, :], pb[:, 0:T], dww[:, f * 3:f * 3 + 1], ub[:, :], ALU.mult, ALU.add),
                     reads=[pbt + (0,), pbt + (1,), ubt], writes=[ubt])

            def gate(j):
                ug = ubuf[0][j % 2]
                uv = ubuf[1][j % 2]
                emit("act", act_fn(ug[:, :], ug[:, :], AF.Silu), reads=[("bigub", 0, j % 2)], writes=[("bigub", 0, j % 2)])
                emit("dve", tt(actb[:, j, :], ug[:, :], uv[:, :], ALU.mult),
                     reads=[("bigub", 0, j % 2), ("bigub", 1, j % 2)], writes=[("bigact", j)])

            s0 = W.get("wup%d" % L, 0, 4096)
            s1 = W.get("wup%d" % L, 1, 4096)
            grp = prog_groups([(s0, 0), (s0, 2048), (s1, 0), (s1, 2048)])
            W.release(s0)
            W.release(s1)
            post(0, 0, *grp[0])
            post(0, 1, *grp[1])
            gate(0)
            post(1, 0, *grp[2])
            post(1, 1, *grp[3])
            gate(1)
            for j in range(2, NF):
                slot = W.get("wup%d" % L, j, 4096)
                for half in range(2):
                    ps, pt = next_ps()
                    base = half * 2048
                    emit("pe", mm_group(ps[:, :], [(ring[slot][:, base + kc * P: base + (kc + 1) * P], hb[:, kc, :]) for kc in range(CH)]),
                         reads=[("ring", slot)] + [("hb", kc) for kc in range(CH)], writes=[pt])
                    post(j, half, ps, pt)
                W.release(slot)
                gate(j)
            for m in range(CH):
                slot = W.get("wdn%d" % L, m, 5632)
                ps, pt = next_ps()
                emit("pe", mm_group(ps[:, :], [(ring[slot][:, k * P:(k + 1) * P], actb[:, k, :]) for k in range(NF)]),
                     reads=[("ring", slot)] + [("bigact", k) for k in range(NF)], writes=[pt])
                W.release(slot)
                emit("dve", tt(xT[:, m, :], xT[:, m, :], ps[:, :], ALU.add), reads=[pt, ("x", m)], writes=[("x", m)])
                if stats_next:
                    xstat(m)

        def conformer(L, ti):
            pg.switch_big()
            rmsnorm_to_hb("norm_g", L)
            b_in = V("a_b_in", L)
            emit("pool", cp(Gb[:, :, 0:30], cvst[L][:, :, :]), reads=[("cvst", L)], writes=[("bigGh",)])
            s0 = W.get("win%d" % L, 0, 4096)
            s1 = W.get("win%d" % L, 1, 4096)
            grp0 = prog_groups([(s0, 0), (s0, 2048), (s1, 0), (s1, 2048)])
            W.release(s0)
            W.release(s1)
            for c in range(CH):
                if c < 2:
                    (psa, pta), (psg, ptg) = grp0[2 * c], grp0[2 * c + 1]
                else:
                    slot = W.get("win%d" % L, c, 4096)
                    psa, pta = next_ps()
                    emit("pe", mm_group(psa[:, :], [(ring[slot][:, kc * P:(kc + 1) * P], hb[:, kc, :]) for kc in range(CH)]),
                         reads=[("ring", slot)] + [("hb", kc) for kc in range(CH)], writes=[pta])
                    psg, ptg = next_ps()
                    emit("pe", mm_group(psg[:, :], [(ring[slot][:, 2048 + kc * P:2048 + (kc + 1) * P], hb[:, kc, :]) for kc in range(CH)]),
                         reads=[("ring", slot)] + [("hb", kc) for kc in range(CH)], writes=[ptg])
                    W.release(slot)
                sg = tmpf[2 + c % 2]
                emit("act", act_fn(sg[:, :], psg[:, :], AF.Sigmoid, bias=b_in[:, CH + c:CH + c + 1]),
                     reads=[ptg, ("vec",)], writes=[("tf", 2 + c % 2)])
                emit("dve", stt(Gb[:, c, 30:30 + T], psa[:, :], b_in[:, c:c + 1], sg[:, :], ALU.add, ALU.mult),
                     reads=[pta, ("tf", 2 + c % 2), ("vec",)], writes=[("bigG", c)])
            emit("pool", cp(cvst[L][:, :, :], Gb[:, :, T:T + 30]), reads=[("bigG", c) for c in range(CH)], writes=[("cvst", L)])
            dw_b = V("a_dw_b", L)
            it_y = []
            it_q = []
            dwc = V("a_dw_w", L)
            ND = CONV_DVE_TAPS

            def dve_taps(c):
                acc = tmpf[c % 2]
                at = ("tf", c % 2)
                rd = [("bigG", c), ("bigGh",), ("vec",)]
                emit("act", act_fn(acc[:, :], Gb[:, c, 0:T], AF.Identity, scale=dwc[:, c * 31:c * 31 + 1]), reads=rd, writes=[at])
                for j in range(1, ND):
                    emit("dve", stt(acc[:, :], Gb[:, c, j:j + T], dwc[:, c * 31 + j:c * 31 + j + 1], acc[:, :], ALU.mult, ALU.add),
                         reads=rd + [at], writes=[at])

            if ND > 0:
                dve_taps(0)
            for c in range(CH):
                slot = W.get("cvw%d" % L, c, 31 * P)
                ps, pt = next_ps()
                emit("pe", mm_group(ps[:, :], [(ring[slot][:, j * P:(j + 1) * P], Gb[:, c, j:j + T]) for j in range(ND, 31)]),
                     reads=[("ring", slot), ("bigG", c), ("bigGh",)], writes=[pt])
                W.release(slot)
                sq = sqb[c % 4]
                if ND > 0:
                    if c + 1 < CH:
                        dve_taps(c + 1)
                    emit("dve", stt(yb[:, c, :], ps[:, :], dw_b[:, c:c + 1], tmpf[c % 2][:, :], ALU.add, ALU.add),
                         reads=[pt, ("tf", c % 2), ("vec",)], writes=[("bigy", c)])
                    emit("act", act_fn(sq[:, :], yb[:, c, :], AF.Square), reads=[("bigy", c)], writes=[("sq", c % 4)])
                else:
                    emit("act", act_fn(yb[:, c, :], ps[:, :], AF.Identity, bias=dw_b[:, c:c + 1]), reads=[pt, ("vec",)], writes=[("bigy", c)])
                    emit("act", act_fn(sq[:, :], ps[:, :], AF.Square, bias=dw_b[:, c:c + 1]), reads=[pt, ("vec",)], writes=[("sq", c % 4)])
                it_y.append((yb[:, c, :], [("bigy", c)]))
                it_q.append((sq[:, :], [("sq", c % 4)]))
                presum_step(it_y, c, (2, 3))
                presum_step(it_q, c, (4, 5))
            ps_s, pt_s = next_ps()
            ps_q, pt_q = next_ps()
            presum_mm(ps_s, pt_s, (2, 3))
            presum_mm(ps_q, pt_q, (4, 5))
            mean = tmpf[1]
            emit("act", act_fn(mean[:, :], ps_s[:, :], AF.Identity, scale=1.0 / D), reads=[pt_s], writes=[("tf", 1)])
            msq = tmpf[4]
            emit("dve", tt(msq[:, :], mean[:, :], mean[:, :], ALU.mult), reads=[("tf", 1)], writes=[("tf", 4)])
            var = tmpf[5]
            emit("dve", stt(var[:, :], ps_q[:, :], 1.0 / D, msq[:, :], ALU.mult, ALU.subtract), reads=[pt_q, ("tf", 4)], writes=[("tf", 5)])
            emit("act", act_fn(var[:, :], var[:, :], AF.Sqrt, bias=V_eps()), reads=[("tf", 5), ("cst",)], writes=[("tf", 5)])
            emit("dve", lambda e: e.reciprocal(out=var[:, :], in_=var[:, :]), reads=[("tf", 5)], writes=[("tf", 5)])
            ln_g = V("a_ln_g", L)
            ln_b = V("a_ln_b", L)
            for c in range(CH):
                eng = "dve"
                emit(eng, tt(yb[:, c, :], yb[:, c, :], mean[:, :], ALU.subtract), reads=[("bigy", c), ("tf", 1)], writes=[("bigy", c)])
                emit(eng, tt(yb[:, c, :], yb[:, c, :], var[:, :], ALU.mult), reads=[("bigy", c), ("tf", 5)], writes=[("bigy", c)])
                emit("act", act_fn(hb[:, c, :], yb[:, c, :], AF.Silu, bias=ln_b[:, c:c + 1], scale=ln_g[:, c:c + 1]),
                     reads=[("bigy", c), ("vec",)], writes=[("hb", c)])
            b_out = V("a_b_out", L)
            sl3 = [W.get("wout%d" % L, m, 2048) for m in range(3)]
            grp3 = prog_groups([(sl, 0) for sl in sl3])
            for sl in sl3:
                W.release(sl)
            for m in range(CH):
                if m < 3:
                    ps, pt = grp3[m]
                else:
                    slot = W.get("wout%d" % L, m, 2048)
                    ps, pt = next_ps()
                    emit("pe", mm_group(ps[:, :], [(ring[slot][:, kc * P:(kc + 1) * P], hb[:, kc, :]) for kc in range(CH)]),
                         reads=[("ring", slot)] + [("hb", kc) for kc in range(CH)], writes=[pt])
                    W.release(slot)
                emit("dve", stt(xT[:, m, :], ps[:, :], b_out[:, m:m + 1], xT[:, m, :], ALU.add, ALU.add),
                     reads=[pt, ("x", m), ("vec",)], writes=[("x", m)])
                xstat(m)

        def pool_mixer(L, ti):
            pg.switch_big()
            g = V("norm_g", L)
            ps, pt = next_ps()
            x_sumsq(ps, pt)
            rs = rstd_from(ps, pt, 1.0 / D, 0)
            emit("pool", cp(Hb[:, :, 0:15], plst[:, :, :]), reads=[("plst",)], writes=[("bigHh",)], small=True)
            ic = C("invcnt")
            sc = V("b_scale", L)
            for gi in range(4):
                w = POOL_W[gi]
                for c in range(gi * 4, gi * 4 + 4):
                    emit("dve", stt(Hb[:, c, 15:15 + T], xT[:, c, :], g[:, c:c + 1], rs[:, :], ALU.mult, ALU.mult),
                         reads=[("x", c), ("tf", 0), ("vec",)], writes=[("bigH", c)])
                for c in range(gi * 4, gi * 4 + 4):
                    eng = "dve"
                    bufs = [pbuf[0][c % 2], pbuf[1][c % 2]]
                    bt = [("bigpb", 0, c % 2, 0), ("bigpb", 1, c % 2, 0)]
                    src = Hb[:, c, :]
                    srct = [("bigH", c), ("bigHh",)]
                    sh = 1
                    bi = 0
                    while sh < w:
                        dst = bufs[bi]
                        n = 15 + T
                        emit(eng, tt(dst[:, sh:n], src[:, sh:n], src[:, 0:n - sh], ALU.add), reads=srct, writes=[bt[bi]])
                        src = dst
                        srct = [bt[bi]]
                        bi ^= 1
                        sh *= 2
                    emit("dve", stt(hb[:, c, :], src[:, 15:15 + T], 1.0 / w, Hb[:, c, 15:15 + T], ALU.mult, ALU.subtract),
                         reads=srct + [("bigH", c)], writes=[("hb", c)])
                    if ti == 0:
                        t16 = tmpf[2 + c % 2]
                        emit(eng, tt(t16[:, 0:16], src[:, 15:31], ic[:, gi * 16:(gi + 1) * 16], ALU.mult), reads=srct + [("cst",)], writes=[("tf", 2 + c % 2)], small=True)
                        emit(eng, tt(hb[:, c, 0:16], t16[:, 0:16], Hb[:, c, 15:31], ALU.subtract), reads=[("tf", 2 + c % 2), ("bigH", c)], writes=[("hb", c)], small=True)
                slot = W.get("wg%d" % L, gi, 2048)
                for mc in range(4):
                    ps, pt = next_ps()
                    emit("pe", mm_group(ps[:, :], [(ring[slot][:, kc * 512 + mc * P: kc * 512 + (mc + 1) * P], hb[:, gi * 4 + kc, :]) for kc in range(4)]),
                         reads=[("ring", slot)] + [("hb", gi * 4 + kc) for kc in range(4)], writes=[pt])
                    m = gi * 4 + mc
                    emit("dve", stt(xT[:, m, :], ps[:, :], sc[:, m:m + 1], xT[:, m, :], ALU.mult, ALU.add),
                         reads=[pt, ("x", m), ("vec",)], writes=[("x", m)])
                    xstat(m)
                W.release(slot)
            emit("pool", cp(plst[:, :, :], Hb[:, :, T:T + 15]), reads=[("bigH", c) for c in range(CH)], writes=[("plst",)])

        def attention(L, ti):
            pg.switch_big()
            rmsnorm_to_hb("norm_g", L)
            emit("sp", lambda e: e.dma_start(out=posi, in_=pos_d[:, ti * T:(ti + 1) * T]), writes=[("tf", 3)], dsem=sems["ld"])
            ang = tmpf[1]
            emit("dve", cp(ang[:, :], posi), reads=[("tf", 3)], writes=[("tf", 1)])
            emit("dve", lambda e: e.tensor_single_scalar(out=ang[:, :], in_=ang[:, :], scalar=C("invf")[:, 0:1], op=ALU.mult),
                 reads=[("tf", 1), ("cst",)], writes=[("tf", 1)])
            emit("dve", lambda e: e.tensor_single_scalar(out=ang[:, :], in_=ang[:, :], scalar=1.0 / TWO_PI, op=ALU.mult),
                 reads=[("tf", 1)], writes=[("tf", 1)])
            for which, dstb, dtk in ((0, sinb, ("bigsin",)), (1, cosb, ("bigcos",))):
                qq = tmpf[4]
                if which == 1:
                    emit("dve", lambda e: e.tensor_single_scalar(out=ang[:, :], in_=ang[:, :], scalar=0.25, op=ALU.add),
                         reads=[("tf", 1)], writes=[("tf", 1)])
                emit("dve", cp(posi, ang[:, :]), reads=[("tf", 1)], writes=[("tf", 3)])
                emit("dve", cp(qq[:, :], posi), reads=[("tf", 3)], writes=[("tf", 4)])
                emit("dve", tt(qq[:, :], ang[:, :], qq[:, :], ALU.subtract), reads=[("tf", 1), ("tf", 4)], writes=[("tf", 4)])
                emit("act", act_fn(dstb, qq[:, :], AF.Sin, scale=6.28318), reads=[("tf", 4)], writes=[dtk])
            emit("dve", lambda e: e.tensor_single_scalar(out=sinb, in_=sinb, scalar=C("sgn")[:, 0:1], op=ALU.mult),
                 reads=[("bigsin",), ("cst",)], writes=[("bigsin",)])

            gq = V("c_q_norm_g", L)
            gk = V("c_k_norm_g", L)

            chunks = [("wq%d" % L, c, gq, QT[:, c, :], ("bigQ", c)) for c in range(CH)] + \
                     [("wk%d" % L, kv, gk, KT[:, kv, P:P + T], ("KTc", kv)) for kv in range(4)]
            sl3 = [W.get("wq%d" % L, c, 2048) for c in range(3)]
            grpq = prog_groups([(sl, 0) for sl in sl3])
            for sl in sl3:
                W.release(sl)
            st = {}

            def stageA(k):
                name, idx, gcol, dst, dtok = chunks[k]
                if k < 3:
                    ps, pt = grpq[k]
                else:
                    slot = W.get(name, idx, 2048)
                    ps, pt = next_ps()
                    emit("pe", mm_group(ps[:, :], [(ring[slot][:, kc * P:(kc + 1) * P], hb[:, kc, :]) for kc in range(CH)]),
                         reads=[("ring", slot)] + [("hb", kc) for kc in range(CH)], writes=[pt])
                    W.release(slot)
                held.add(pt[1])
                sq = sqb[k % 2]
                emit("act", act_fn(sq[:, :], ps[:, :], AF.Square), reads=[pt], writes=[("sq", k % 2)])
                st[k] = (ps, pt)

            def stageB(k):
                name, idx, gcol, dst, dtok = chunks[k]
                ps, pt = st[k]
                sq = sqb[k % 2]
                ps2, pt2 = next_ps()
                emit("pe", lambda e: e.matmul(ps2[:, :], C("bones"), sq[:, :], start=True, stop=True), reads=[("sq", k % 2), ("cst",)], writes=[pt2])
                rs = tmpf[2 + k % 2]
                rt = ("tf", 2 + k % 2)
                emit("act", act_fn(rs[:, :], ps2[:, :], AF.Sqrt, bias=V_eps(), scale=1.0 / 64), reads=[pt2, ("cst",)], writes=[rt])
                emit("dve", lambda e: e.reciprocal(out=rs[:, :], in_=rs[:, :]), reads=[rt], writes=[rt])
                qn = ubuf[0][k % 2]
                qt = ("bigub", 0, k % 2)
                emit("dve", stt(qn[:, :], ps[:, :], gcol[:, 0:1], rs[:, :], ALU.mult, ALU.mult), reads=[pt, rt, ("vec",)], writes=[qt])
                held.discard(pt[1])

            def stageC(k):
                name, idx, gcol, dst, dtok = chunks[k]
                qn = ubuf[0][k % 2]
                qt = ("bigub", 0, k % 2)
                ps3, pt3 = next_ps()
                emit("pe", lambda e: e.matmul(ps3[:, :], C("perm"), qn[:, :], start=True, stop=True), reads=[qt, ("cst",)], writes=[pt3])
                t2 = ubuf[1][k % 2]
                tt2 = ("bigub", 1, k % 2)
                emit("dve", tt(t2[:, :], ps3[:, :], sinb, ALU.mult), reads=[pt3, ("bigsin",)], writes=[tt2])
                emit("dve", tt(qn[:, :], qn[:, :], cosb, ALU.mult), reads=[qt, ("bigcos",)], writes=[qt])
                emit("dve", tt(dst, qn[:, :], t2[:, :], ALU.add), reads=[qt, tt2], writes=[dtok])

            nqk = len(chunks)
            for i in range(nqk + 2):
                if i < nqk:
                    stageA(i)
                if 0 <= i - 1 < nqk:
                    stageB(i - 1)
                if 0 <= i - 2 < nqk:
                    stageC(i - 2)
            for half in range(2):
                slot = W.get("wv%d" % L, half, 4096)
                for tb in range(4):
                    ps, pt = next_ps()
                    emit("pe", mm_group(ps[:, 0:256], [(hb[:, kc, tb * P:(tb + 1) * P], ring[slot][:, kc * 256:(kc + 1) * 256]) for kc in range(CH)]),
                         reads=[("ring", slot)] + [("hb", kc) for kc in range(CH)], writes=[pt])
                    emit("act", act_fn(VT[:, 1 + tb, half * 256:(half + 1) * 256], ps[:, 0:256], AF.Copy), reads=[pt], writes=[("VTc", tb, half)])
                W.release(slot)
            OT = hb
            kpi = 0
            for kv in range(4):
                for qb in range(4):
                    first = (ti == 0 and qb == 0)
                    pt_buf = PTb[kpi % 2]
                    ptt = ("bigP", kpi % 2)
                    kpi += 1
                    kbs = [1] if first else [0, 1]
                    for par in range(2):
                        for kb in kbs:
                            ps, pt = next_ps()
                            kcol = qb * P + kb * P
                            lhs = KT[par * 64:(par + 1) * 64, kv, kcol:kcol + P]
                            rhs = QT[par * 64:(par + 1) * 64, kv * 4:(kv + 1) * 4, qb * P:(qb + 1) * P]
                            mk = mprevb if kb == 0 else mcurb

                            def sfn(e, ps=ps, lhs=lhs, rhs=rhs, mk=mk):
                                e.matmul(ps[:, :], lhs, rhs, start=True, stop=False)
                                return e.matmul(ps[:, :], mk, identqb, start=False, stop=True)
                            emit("pe", sfn, reads=[("KTc", kv), ("KTh",), ("cb",)] + [("bigQ", kv * 4 + i) for i in range(4)], writes=[pt])
                            emit("act", act_fn(pt_buf[:, kb, par * 512:(par + 1) * 512], ps[:, :], AF.Exp, scale=0.125), reads=[pt], writes=[ptt + (par, kb)])
                    dix = 4 + (kpi % 2)
                    den = tmpf[dix]
                    dt_ = ("tf", dix)
                    nums = []
                    for par in range(2):
                        psn, ptn = next_ps()
                        psd, ptd = next_ps()

                        def nfn(e, psn=psn, par=par, kbs=kbs, qb=qb, kv=kv, pt_buf=pt_buf):
                            ins = None
                            for i, kb in enumerate(kbs):
                                vsrc = VT[:, qb + kb, kv * P:(kv + 1) * P]
                                ins = e.matmul(psn[:, :], vsrc, pt_buf[:, kb, par * 512:(par + 1) * 512], start=(i == 0), stop=(i == len(kbs) - 1))
                            return ins

                        def dfn(e, psd=psd, par=par, kbs=kbs, pt_buf=pt_buf):
                            ins = None
                            for i, kb in enumerate(kbs):
                                ins = e.matmul(psd[:, :], identb_ones, pt_buf[:, kb, par * 512:(par + 1) * 512], start=(i == 0), stop=(i == len(kbs) - 1))
                            return ins
                        rd = [ptt + (par, kb) for kb in kbs]
                        emit("pe", nfn, reads=rd + [("VTc", t_, h_) for t_ in range(4) for h_ in range(2)] + [("VTh",)], writes=[ptn])
                        emit("pe", dfn, reads=rd + [("cb",)], writes=[ptd])
                        lo, hi = par * 64, (par + 1) * 64
                        esv = est[lo:hi, kv * 8 + par: kv * 8 + 8: 2].unsqueeze(2).to_broadcast([64, 4, P])
                        emit("dve", tt(den[lo:hi, :].rearrange("p (a b) -> p a b", a=4), psd[lo:hi, :].rearrange("p (a b) -> p a b", a=4), esv, ALU.add),
                             reads=[ptd, ("est",)], writes=[dt_])
                        nums.append((psn, ptn))
                    emit("dve", lambda e, den=den: e.reciprocal(out=den[:, :], in_=den[:, :]), reads=[dt_], writes=[dt_])
                    for par in range(2):
                        psn, ptn = nums[par]
                        lo, hi = par * 64, (par + 1) * 64
                        emit("dve", tt(OT[lo:hi, kv * 4:(kv + 1) * 4, qb * P:(qb + 1) * P], psn[lo:hi, :].rearrange("p (a b) -> p a b", a=4),
                                       den[lo:hi, :].rearrange("p (a b) -> p a b", a=4), ALU.mult),
                             reads=[ptn, dt_], writes=[("hb", kv * 4 + i_) for i_ in range(4)])
            emit("pool", cp(KT[:, :, 0:P], KT[:, :, T:T + P]), reads=[("KTc", kv) for kv in range(4)], writes=[("KTh",)])
            emit("pool", cp(VT[:, 0, :], VT[:, 4, :]), reads=[("VTc", 3, 0), ("VTc", 3, 1)], writes=[("VTh",)])
            for m in range(CH):
                slot = W.get("wo%d" % L, m, 2048)
                ps, pt = next_ps()
                emit("pe", mm_group(ps[:, :], [(ring[slot][:, kc * P:(kc + 1) * P], OT[:, kc, :]) for kc in range(CH)]),
                     reads=[("ring", slot)] + [("hb", kc) for kc in range(CH)], writes=[pt])
                W.release(slot)
                emit("dve", tt(xT[:, m, :], xT[:, m, :], ps[:, :], ALU.add), reads=[pt, ("x", m)], writes=[("x", m)])
                xstat(m)

        epsb = sb("epsb", [P, 1], F32)
        negpi = sb("negpi", [P, 1], F32)
        onesb = sb("onesb", [P, P], BF16)
        identb_ones = onesb[:, :]

        def body():
            state["ps"] = 0
            pg.reset()
            emit("pool", lambda e: e.memset(epsb[:, :], EPS), writes=[("cst2",)])
            emit("pool", lambda e: e.memset(negpi[:, :], -math.pi), writes=[("cst2",)])
            emit("pool", lambda e: e.memset(onesb[:, :], 1.0), writes=[("cst2",)])
            setup()
            for L in layers[:1]:
                cast_layer(L)
            if not pg.dry:
                cid = (sems["pool"], 3)
                for eng in ("pe", "act", "dve", "sp"):
                    pg.wait_all(eng, [cid])
            W.start() if not pg.dry else None
            outc = []
            for ti in range(NT):
                load_x(ti)
                for li, L in enumerate(layers):
                    if ti == 0 and li + 1 < len(layers):
                        cast_layer(layers[li + 1])
                    if do_mixer:
                        {"conf": conformer, "pool": pool_mixer, "attn": attention}[MIXERS[L]](L, ti)
                    if do_ffn:
                        ffn(L, ti, stats_next=(li + 1 < len(layers)))
                outc += store_x(ti)
            return outc

        pg.dry = True
        body()
        pg.dry = False
        outc = body()
        last = {}
        for s, v in outc:
            last[s.name] = (s, max(v, last.get(s.name, (s, 0))[1]))
        pg.wait_all("sp", list(last.values()))

        with nc.Block() as block:
            @block.sync
            def _(e):
                for f in pg.q["sp"]:
                    f(e)

            @block.tensor
            def _(e):
                for f in pg.q["pe"]:
                    f(e)

            @block.scalar
            def _(e):
                for f in pg.q["act"]:
                    f(e)

            @block.vector
            def _(e):
                for f in pg.q["dve"]:
                    f(e)

            @block.gpsimd
            def _(e):
                for f in pg.q["pool"]:
                    f(e)
        print("instructions:", pg.n_inst, {k: len(v) for k, v in pg.q.items()}, "sbuf left", nc.sbuf_bytes_remaining)
    return nc


def prepare_shared(inp, layers):
    vecs, voffs = pack_vec(inp)
    csts, coffs, cbarr = make_consts()
    slabs = weight_slabs(inp, layers)
    return vecs, voffs, csts, coffs, slabs, cbarr


def run(inp, S, layers, n_cores, do_ffn=True, do_mixer=True):
    vecs, voffs, csts, coffs, slabs, cbarr = prepare_shared(inp, layers)
    slab_shapes = {k: v.shape for k, v in slabs.items()}
    import time as _time
    _t0 = _time.time()
    nc = build_program(S, layers, slab_shapes, voffs, coffs, vecs.shape[1], csts.shape[1], do_ffn=do_ffn, do_mixer=do_mixer)
    print("build time", _time.time() - _t0, flush=True)
    x = np.asarray(inp["x"], np.float32)
    pos = np.asarray(inp["positions"], np.int32)
    in_maps = []
    for b in range(n_cores):
        m = {"x": np.ascontiguousarray(x[b]),
             "posb": np.ascontiguousarray(np.broadcast_to(pos[b][None, :], (P, S))),
             "vec": vecs, "cst": csts, "cstb": cbarr}
        m.update(slabs)
        in_maps.append(m)
    _t0 = _time.time()
    import os as _os
    _tr = _os.environ.get("KTRACE", "0") == "1"
    res = run_bass_kernel_spmd(nc, in_maps, core_ids=list(range(n_cores)), **({"trace": True} if _tr else {}))
    if _tr:
        print("exec_time_ns", res.exec_time_ns, flush=True)
    print("run time", _time.time() - _t0, flush=True)
    return np.stack([r["out"] for r in res.results], axis=0)


_INPUT_NAMES = (
    "x", "positions",
    "l0_norm_g", "l0_a_w_in", "l0_a_b_in", "l0_a_dw_w", "l0_a_dw_b", "l0_a_ln_g", "l0_a_ln_b", "l0_a_w_out", "l0_a_b_out",
    "l0_ffn_norm_g", "l0_ffn_w_up", "l0_ffn_dw_w", "l0_ffn_dw_b", "l0_ffn_w_down",
    "l1_norm_g", "l1_b_w_group", "l1_b_scale",
    "l1_ffn_norm_g", "l1_ffn_w_up", "l1_ffn_dw_w", "l1_ffn_dw_b", "l1_ffn_w_down",
    "l2_norm_g", "l2_c_w_qkv", "l2_c_q_norm_g", "l2_c_k_norm_g", "l2_c_sinks", "l2_c_w_o",
    "l2_ffn_norm_g", "l2_ffn_w_up", "l2_ffn_dw_w", "l2_ffn_dw_b", "l2_ffn_w_down",
    "l3_norm_g", "l3_a_w_in", "l3_a_b_in", "l3_a_dw_w", "l3_a_dw_b", "l3_a_ln_g", "l3_a_ln_b", "l3_a_w_out", "l3_a_b_out",
    "l3_ffn_norm_g", "l3_ffn_w_up", "l3_ffn_dw_w", "l3_ffn_dw_b", "l3_ffn_w_down",
)


def kernel(**inputs):
    inp = {k: inputs[k] for k in _INPUT_NAMES}
    return run(inp, 4096, [0, 1, 2, 3], 8).astype(np.float32)
```

```python
import math
import numpy as np
import concourse.bass as bass
import concourse.mybir as mybir
from concourse.bass_utils import run_bass_kernel_spmd

F32 = mybir.dt.float32
BF16 = mybir.dt.bfloat16
I32 = mybir.dt.int32
AF = mybir.ActivationFunctionType
ALU = mybir.AluOpType

P = 128
D = 2048
CH = 16
T = 512
DFF = 5632
NF = 44
EPS = 1e-6
RING = 4
SLOTW = 5632
NEG = -30000.0
TWO_PI = 2.0 * math.pi
ROPE_THETA = 500000.0
POOL_W = (2, 4, 8, 16)
CONV_DVE_TAPS = 6
MIXERS = ("conf", "pool", "attn", "conf")


class Sem:
    def __init__(self, h, name):
        self.h = h
        self.name = name
        self.val = 0


class Prog:
    ENGS = ("pe", "act", "dve", "pool", "sp")

    def __init__(self, nc, sems):
        self.nc = nc
        self.q = {e: [] for e in self.ENGS}
        self.esem = {e: sems[e] for e in ("pe", "act", "dve", "pool")}
        self.waited = {e: {} for e in self.ENGS}
        self.tok = {}
        self.big_frontier = []
        self.dry = False
        self.n_inst = 0
        self.small = set()

    def reset(self):
        self.tok = {}
        self.big_frontier = []

    def _entry(self, t):
        e = self.tok.get(t)
        if e is None:
            e = {"w": None, "r": []}
            if isinstance(t, tuple) and isinstance(t[0], str) and t[0].startswith("big"):
                e["r"] = list(self.big_frontier)
            self.tok[t] = e
        return e

    def switch_big(self):
        if self.dry:
            return
        fr = {}
        for t in list(self.tok.keys()):
            if isinstance(t, tuple) and isinstance(t[0], str) and t[0].startswith("big"):
                e = self.tok.pop(t)
                for d in ([e["w"]] if e["w"] else []) + e["r"]:
                    if d[0].name not in fr or fr[d[0].name][1] < d[1]:
                        fr[d[0].name] = d
        for d in self.big_frontier:
            if d[0].name not in fr or fr[d[0].name][1] < d[1]:
                fr[d[0].name] = d
        self.big_frontier = list(fr.values())

    def set_writer(self, t, sem, val):
        self.tok[t] = {"w": (sem, val), "r": []}

    def emit(self, eng, fn, reads=(), writes=(), dsem=None, small=False):
        if self.dry:
            return
        deps = {}

        def add(d):
            if d is None:
                return
            s, v = d
            if dsem is None and eng in self.esem and s is self.esem[eng] and (s.name, v) not in self.small:
                return
            if s.name not in deps or deps[s.name][1] < v:
                deps[s.name] = (s, v)

        for t in reads:
            add(self._entry(t)["w"])
        for t in writes:
            e = self._entry(t)
            add(e["w"])
            for d in e["r"]:
                add(d)
        wl = []
        wd = self.waited[eng]
        for name, (s, v) in deps.items():
            if wd.get(name, 0) >= v:
                continue
            wd[name] = v
            wl.append((s.h, v))
        if dsem is not None:
            dsem.val += 16
            cid = (dsem, dsem.val)
            inc = (dsem.h, 16)
        else:
            s = self.esem[eng]
            s.val += 1
            cid = (s, s.val)
            inc = (s.h, 1)

        def run(e, wl=wl, fn=fn, inc=inc):
            for h, v in wl:
                e.wait_ge(h, v)
            fn(e).then_inc(inc[0], inc[1])

        self.q[eng].append(run)
        self.n_inst += 1
        if small:
            self.small.add((cid[0].name, cid[1]))
        for t in writes:
            self.tok[t] = {"w": cid, "r": []}
        for t in reads:
            if t in writes:
                continue
            e = self._entry(t)
            r = [d for d in e["r"] if d[0] is not cid[0]]
            r.append(cid)
            e["r"] = r
        return cid

    def wait_all(self, eng, cids):
        if self.dry:
            return
        wl = [(s.h, v) for s, v in cids]

        def run(e, wl=wl):
            for h, v in wl:
                e.wait_ge(h, v)

        self.q[eng].append(run)


def _fm(v):
    v = np.asarray(v, dtype=np.float32)
    return np.ascontiguousarray(v.reshape(-1, P).T)


def _slab_pair(w, npair):
    K = w.shape[0]
    kc = K // P
    a = np.asarray(w).reshape(kc, P, 2, npair, P)
    return np.ascontiguousarray(a.transpose(3, 1, 2, 0, 4)).reshape(npair, P, 2 * kc * P)


def _slab_m(w):
    K, M = w.shape
    a = np.asarray(w).reshape(K // P, P, M // P, P)
    return np.ascontiguousarray(a.transpose(2, 1, 0, 3)).reshape(M // P, P, (K // P) * P)


def make_consts():
    c = {}
    c["ident"] = np.eye(P, dtype=np.float32)
    c["ones"] = np.ones((P, P), np.float32)
    bo = np.zeros((P, P), np.float32)
    bo[:64, :64] = 1
    bo[64:, 64:] = 1
    c["bones"] = bo
    pm = np.zeros((P, P), np.float32)
    for m in range(P):
        dh = m % 64
        if dh < 8:
            pm[m + 8, m] = 1
        elif dh < 16:
            pm[m - 8, m] = 1
    c["perm"] = pm
    ii = np.arange(P)[:, None]
    jj = np.arange(P)[None, :]
    cbf = {}
    cbf["ident"] = np.eye(P, dtype=np.float32)
    cbf["mprev"] = np.where(jj > ii, 0.0, NEG).astype(np.float32)
    cbf["mcur"] = np.where(jj <= ii, 0.0, NEG).astype(np.float32)
    cbf["identq"] = np.tile(np.eye(P, dtype=np.float32), (1, 4))
    inv_freq = (np.float32(ROPE_THETA) ** (-(np.arange(0, 16, 2, dtype=np.float32)) / np.float32(16))).astype(np.float32)
    invf = np.zeros((P, 1), np.float32)
    nsgn = np.zeros((P, 1), np.float32)
    for p in range(P):
        dh = p % 64
        if dh < 16:
            invf[p, 0] = inv_freq[dh % 8]
            nsgn[p, 0] = -1.0 if dh < 8 else 1.0
    c["invf"] = invf
    c["sgn"] = nsgn
    ic = np.zeros((P, 4 * 16), np.float32)
    for g, w in enumerate(POOL_W):
        ic[:, g * 16:(g + 1) * 16] = (1.0 / np.minimum(np.arange(16) + 1, w)).astype(np.float32)[None, :]
    c["invcnt"] = ic
    cbarr = np.ascontiguousarray(np.concatenate([cbf["ident"], cbf["mprev"], cbf["mcur"], cbf["identq"]], axis=1))
    offs = {}
    cols = []
    o = 0
    for k, v in c.items():
        offs[k] = (o, v.shape[1])
        cols.append(v)
        o += v.shape[1]
    return np.ascontiguousarray(np.concatenate(cols, axis=1)), offs, cbarr


def pack_vec(inp):
    parts = []
    offs = {}
    o = 0

    def add(name, a):
        nonlocal o
        a = np.ascontiguousarray(a, dtype=np.float32)
        offs[name] = (o, a.shape[1])
        parts.append(a)
        o += a.shape[1]

    for L in range(4):
        p = "l%d_" % L
        add(p + "norm_g", _fm(inp[p + "norm_g"]))
        add(p + "ffn_norm_g", _fm(inp[p + "ffn_norm_g"]))
        dw = np.asarray(inp[p + "ffn_dw_w"], np.float32).reshape(3, 88, P).transpose(2, 1, 0).reshape(P, 88 * 3)
        add(p + "ffn_dw_w", dw)
        add(p + "ffn_dw_b", _fm(inp[p + "ffn_dw_b"]))
        if MIXERS[L] == "conf":
            add(p + "a_b_in", _fm(inp[p + "a_b_in"]))
            dwc = np.asarray(inp[p + "a_dw_w"], np.float32).reshape(31, 16, P).transpose(2, 1, 0).reshape(P, 16 * 31)
            add(p + "a_dw_w", dwc)
            add(p + "a_dw_b", _fm(inp[p + "a_dw_b"]))
            add(p + "a_ln_g", _fm(inp[p + "a_ln_g"]))
            add(p + "a_ln_b", _fm(inp[p + "a_ln_b"]))
            add(p + "a_b_out", _fm(inp[p + "a_b_out"]))
        elif MIXERS[L] == "pool":
            add(p + "b_scale", _fm(inp[p + "b_scale"]))
        else:
            add(p + "c_q_norm_g", np.tile(np.asarray(inp[p + "c_q_norm_g"], np.float32), 2).reshape(P, 1))
            add(p + "c_k_norm_g", np.tile(np.asarray(inp[p + "c_k_norm_g"], np.float32), 2).reshape(P, 1))
            add(p + "c_sinks", np.broadcast_to(np.asarray(inp[p + "c_sinks"], np.float32)[None, :], (P, 32)))
    return np.ascontiguousarray(np.concatenate(parts, axis=1)), offs


def weight_slabs(inp, layers):
    out = {}
    for L in layers:
        p = "l%d_" % L
        out["wup%d" % L] = _slab_pair(inp[p + "ffn_w_up"], NF)
        out["wdn%d" % L] = _slab_m(inp[p + "ffn_w_down"])
        if MIXERS[L] == "conf":
            out["win%d" % L] = _slab_pair(inp[p + "a_w_in"], 16)
            out["wout%d" % L] = _slab_m(inp[p + "a_w_out"])
        elif MIXERS[L] == "pool":
            wg = np.asarray(inp[p + "b_w_group"]).reshape(4, 4, P, 512)
            out["wg%d" % L] = np.ascontiguousarray(wg.transpose(0, 2, 1, 3)).reshape(4, P, 2048)
        else:
            wqkv = np.asarray(inp[p + "c_w_qkv"])
            out["wq%d" % L] = _slab_m(wqkv[:, :2048])
            wk = wqkv[:, 2048:2304].reshape(16, P, 4, 64)
            wk = np.concatenate([wk, wk], axis=3)
            out["wk%d" % L] = np.ascontiguousarray(wk.transpose(2, 1, 0, 3)).reshape(4, P, 2048)
            wv = wqkv[:, 2304:2560].reshape(16, P, 2, 2, 64)
            wv = np.stack([wv, wv], axis=4)
            out["wv%d" % L] = np.ascontiguousarray(wv.transpose(2, 1, 0, 3, 4, 5)).reshape(2, P, 16 * 256)
            out["wo%d" % L] = _slab_m(inp[p + "c_w_o"])
    return out


def build_program(S, layers, slab_shapes, voffs, coffs, nvec, ncst, do_ffn=True, do_mixer=True):
    NT = S // T
    nc = bass.Bass("TRN2", target_bir_lowering=False)
    x_d = nc.dram_tensor("x", [S, D], F32, kind="ExternalInput").ap()
    pos_d = nc.dram_tensor("posb", [P, S], I32, kind="ExternalInput").ap()
    vec_d = nc.dram_tensor("vec", [P, nvec], F32, kind="ExternalInput").ap()
    cst_d = nc.dram_tensor("cst", [P, ncst], F32, kind="ExternalInput").ap()
    cstb_d = nc.dram_tensor("cstb", [P, 7 * P], F32, kind="ExternalInput").ap()
    out_d = nc.dram_tensor("out", [S, D], F32, kind="ExternalOutput").ap()
    w_in = {}
    w_sc = {}
    for name, shp in slab_shapes.items():
        w_in[name] = nc.dram_tensor(name, list(shp), F32, kind="ExternalInput").ap()
        w_sc[name] = nc.dram_tensor(name + "_b", list(shp), BF16).ap()
    for L in layers:
        if MIXERS[L] == "conf":
            w_sc["cvw%d" % L] = nc.dram_tensor("cvw%d_b" % L, [16, P, 31 * P], BF16).ap()

    sb = nc.alloc_sbuf_tensor
    xT = sb("xT", [P, CH, T], F32)
    hb = sb("hb", [P, CH, T], BF16)
    BIG = sb("BIG", [P, NF * T], BF16)
    ring = [sb("ring%d" % i, [P, SLOTW], BF16) for i in range(RING)]
    vec = sb("vecs", [P, nvec], F32)
    cst = sb("csts", [P, ncst], F32)
    cb = sb("cstbs", [P, 7 * P], BF16)
    R2 = sb("R2", [P, CH * (30 + T)], BF16)
    R2f = R2[:, :].bitcast(F32)
    PBW = 15 + T
    pbuf = [[R2f[:, (a * 2 + b) * PBW:(a * 2 + b + 1) * PBW] for b in range(2)] for a in range(2)]
    ubuf = [[R2f[:, 4 * PBW + (a * 2 + b) * T: 4 * PBW + (a * 2 + b + 1) * T] for b in range(2)] for a in range(2)]
    sqb = [sb("sq%d" % i, [P, T], F32) for i in range(4)]
    tmpf = [sb("tf%d" % i, [P, T], F32) for i in range(6)]
    ffst = {L: sb("ffst%d" % L, [P, 88, 2], F32) for L in layers}
    Gb = R2[:, :].rearrange("p (c t) -> p c t", c=CH)
    cvst = {L: sb("cvst%d" % L, [P, CH, 30], BF16) for L in layers if MIXERS[L] == "conf"}
    plst = sb("plst", [P, CH, 15], F32)
    KT = sb("KT", [P, 4, P + T], BF16)
    VT = sb("VT", [P, 5, 512], BF16)
    est = sb("est", [P, 32], F32)
    psum = [nc.alloc_psum_tensor("ps%d" % i, [P, 512], F32) for i in range(8)]

    actb = BIG[:, 0:NF * T].rearrange("p (j t) -> p j t", j=NF)
    bigf = BIG[:, :].bitcast(F32)
    yb = bigf[:, 0:CH * T].rearrange("p (c t) -> p c t", c=CH)
    Hb = bigf[:, 0:CH * (15 + T)].rearrange("p (c t) -> p c t", c=CH)
    QT = BIG[:, 0:CH * T].rearrange("p (c t) -> p c t", c=CH)
    PTb = [BIG[:, CH * T + i * 2048: CH * T + (i + 1) * 2048].rearrange("p (k q) -> p k q", k=2) for i in range(2)]
    stage = [bigf[:, i * D:(i + 1) * D] for i in range(2)]
    cosb = bigf[:, 6144:6144 + T]
    sinb = bigf[:, 6144 + T:6144 + 2 * T]
    dgb = [BIG[:, i * 31 * P:(i + 1) * 31 * P] for i in range(2)]
    posi = tmpf[3][:, :].bitcast(I32)

    def V(name, L=None):
        key = name if L is None else "l%d_%s" % (L, name)
        o, n = voffs[key]
        return vec[:, o:o + n]

    def C(name):
        o, n = coffs[name]
        return cst[:, o:o + n]

    identb = cb[:, 0:P]
    mprevb = cb[:, P:2 * P]
    mcurb = cb[:, 2 * P:3 * P]
    identqb = cb[:, 3 * P:7 * P]

    import contextlib
    with contextlib.ExitStack() as es:
        sems = {}
        for nm in ("pe", "act", "dve", "pool", "pre", "ld", "ld2", "xin0", "xin1", "xout0", "xout1", "misc0", "misc1") + tuple("r%d" % i for i in range(RING)):
            sems[nm] = Sem(es.enter_context(nc.semaphore(nm)), nm)
        for nm in slab_shapes:
            sems["c_" + nm] = Sem(es.enter_context(nc.semaphore("c_" + nm)), "c_" + nm)
        pg = Prog(nc, sems)
        emit = pg.emit
        state = {"ps": 0}

        held = set()

        def next_ps():
            while True:
                i = state["ps"] % 8
                state["ps"] += 1
                if i not in held:
                    return psum[i], ("ps", i)

        class WStream:
            def __init__(self):
                self.plan = []
                self.i = 0
                self.issued = 0

            def _issue(self, k):
                name, idx, width = self.plan[k]
                slot = k % RING
                src = w_sc[name][idx]
                emit("sp", lambda e, slot=slot, src=src, width=width: e.dma_start(out=ring[slot][:, 0:width], in_=src),
                     reads=[("wsc", name), ("wscratch2", 0), ("wscratch2", 1)], writes=[("ring", slot)], dsem=sems["r%d" % slot])

            def start(self):
                self.i = 0
                self.issued = 0
                while self.issued < min(RING, len(self.plan)):
                    self._issue(self.issued)
                    self.issued += 1

            def get(self, name, idx, width):
                if pg.dry:
                    self.plan.append((name, idx, width))
                    self.i += 1
                    return 0
                assert self.plan[self.i] == (name, idx, width), (self.plan[self.i], name, idx, width)
                slot = self.i % RING
                self.i += 1
                return slot

            def release(self, slot):
                if pg.dry:
                    return
                if self.issued < len(self.plan):
                    assert self.issued % RING == slot
                    self._issue(self.issued)
                    self.issued += 1

        W = WStream()

        def mm_group(ps_ap, pairs, start=True, stop=True):
            def fn(e):
                n = len(pairs)
                ins = None
                for i, (l, r) in enumerate(pairs):
                    ins = e.matmul(ps_ap, l, r, start=(start and i == 0), stop=(stop and i == n - 1))
                return ins
            return fn

        def act_fn(out, in_, func, bias=None, scale=None):
            kw = {}
            if bias is not None:
                kw["bias"] = bias
            if scale is not None:
                kw["scale"] = scale
            return lambda e: e.activation(out=out, in_=in_, func=func, **kw)

        def stt(out, in0, scalar, in1, op0, op1):
            return lambda e: e.scalar_tensor_tensor(out=out, in0=in0, scalar=scalar, in1=in1, op0=op0, op1=op1)

        def tt(out, in0, in1, op):
            return lambda e: e.tensor_tensor(out=out, in0=in0, in1=in1, op=op)

        def cp(out, in_):
            return lambda e: e.tensor_copy(out=out, in_=in_)

        def setup():
            emit("sp", lambda e: e.dma_start(out=vec[:, :], in_=vec_d), writes=[("vec",)], dsem=sems["ld"])
            emit("sp", lambda e: e.dma_start(out=cst[:, :], in_=cst_d), writes=[("cst",)], dsem=sems["ld2"])
            if not pg.dry:
                sems["pre"].val += 16
                pg.q["pool"].append(lambda e: e.dma_start(out=cb[:, :], in_=cstb_d).then_inc(sems["pre"].h, 16))
                pg.set_writer(("cb",), sems["pre"], sems["pre"].val)
            for L in layers:
                emit("pool", lambda e, L=L: e.memset(ffst[L][:, :, :], 0.0), writes=[("ffst", L)])
                if MIXERS[L] == "conf":
                    emit("pool", lambda e, L=L: e.memset(cvst[L][:, :, :], 0.0), writes=[("cvst", L)])
            emit("pool", lambda e: e.memset(plst[:, :, :], 0.0), writes=[("plst",)])
            emit("pool", lambda e: e.memset(KT[:, :, :], 0.0), writes=[("KT",)])
            emit("pool", lambda e: e.memset(VT[:, :, :], 0.0), writes=[("VT",)])
            if 2 in layers:
                emit("act", act_fn(est[:, :], V("c_sinks", 2), AF.Exp), reads=[("vec",)], writes=[("est",)])
            k = 0
            for L in layers:
                if MIXERS[L] != "conf":
                    continue
                dwc = V("a_dw_w", L)
                for c in range(CH):
                    buf = dgb[k % 2]
                    tokb = ("bigdgb", k % 2)
                    for j in range(31):
                        eng = "dve"
                        emit(eng, (lambda e, buf=buf, j=j, c=c, dwc=dwc: e.tensor_scalar_mul(
                            out=buf[:, j * P:(j + 1) * P], in0=identb, scalar1=dwc[:, c * 31 + j:c * 31 + j + 1])),
                            reads=[("cb",), ("vec",)], writes=[tokb + (j % 2,)])
                    emit("sp", lambda e, buf=buf, L=L, c=c: e.dma_start(out=w_sc["cvw%d" % L][c], in_=buf),
                         reads=[tokb + (0,), tokb + (1,)], writes=[("cvwd", L, c)], dsem=sems["misc%d" % (k % 2)])
                    k += 1
            if not pg.dry:
                pg.set_writer(("wscratch2", 0), sems["misc0"], sems["misc0"].val)
                pg.set_writer(("wscratch2", 1), sems["misc1"], sems["misc1"].val)

        def layer_tensors(L):
            m = MIXERS[L]
            if m == "conf":
                names = ["win%d" % L, "wout%d" % L]
            elif m == "pool":
                names = ["wg%d" % L]
            else:
                names = ["wq%d" % L, "wk%d" % L, "wv%d" % L, "wo%d" % L]
            return names + ["wup%d" % L, "wdn%d" % L]

        def cast_layer(L):
            if pg.dry:
                return
            for name in layer_tensors(L):
                shp = slab_shapes[name]
                ns, width = shp[0], shp[2]
                per = max(1, (8 << 20) // (P * width * 4))
                sm = sems["c_" + name]
                s0 = 0
                while s0 < ns:
                    s1 = min(ns, s0 + per)
                    src = w_in[name][s0:s1].rearrange("j p w -> (j p) w")
                    dst = w_sc[name][s0:s1].rearrange("j p w -> (j p) w")
                    sm.val += 16
                    pg.q["pool"].append(lambda e, src=src, dst=dst, sm=sm: e.dma_start(out=dst, in_=src).then_inc(sm.h, 16))
                    s0 = s1
                pg.set_writer(("wsc", name), sm, sm.val)

        def load_x(ti):
            pg.switch_big()
            xs_state["ready"] = False
            for tb in range(4):
                k = tb % 2
                st = stage[k]
                r0 = ti * T + tb * P
                emit("sp", lambda e, st=st, r0=r0: e.dma_start(out=st, in_=x_d[r0:r0 + P, :]),
                     writes=[("bigstage", k)], dsem=sems["xin%d" % k])
                for cg in range(4):
                    ps, pt = next_ps()

                    def tr(e, ps=ps, st=st, cg=cg):
                        ins = None
                        for q in range(4):
                            c = cg * 4 + q
                            ins = e.transpose(out=ps[:, q * P:(q + 1) * P], in_=st[:, c * P:(c + 1) * P], identity=C("ident"))
                        return ins
                    emit("pe", tr, reads=[("bigstage", k), ("cst",)], writes=[pt])
                    eng = "act" if cg % 2 == 0 else "dve"
                    dst = xT[:, cg * 4:(cg + 1) * 4, tb * P:(tb + 1) * P]
                    src = ps[:, :].rearrange("p (a b) -> p a b", a=4)
                    if eng == "act":
                        emit("act", act_fn(dst, src, AF.Copy), reads=[pt], writes=[("x", cg * 4 + q) for q in range(4)])
                    else:
                        emit("dve", cp(dst, src), reads=[pt], writes=[("x", cg * 4 + q) for q in range(4)])
                    if tb == 3:
                        for q in range(4):
                            xstat(cg * 4 + q)

        def store_x(ti):
            pg.switch_big()
            cids = []
            for tb in range(4):
                k = tb % 2
                st = stage[k]
                for cg in range(4):
                    ps, pt = next_ps()

                    def tr(e, ps=ps, cg=cg, tb=tb):
                        ins = None
                        for q in range(4):
                            c = cg * 4 + q
                            ins = e.transpose(out=ps[:, q * P:(q + 1) * P], in_=xT[:, c, tb * P:(tb + 1) * P], identity=C("ident"))
                        return ins
                    emit("pe", tr, reads=[("x", cg * 4 + q) for q in range(4)] + [("cst",)], writes=[pt])
                    eng = "act" if cg % 2 == 0 else "dve"
                    dst = st[:, cg * 512:(cg + 1) * 512]
                    if eng == "act":
                        emit("act", act_fn(dst, ps[:, :], AF.Copy), reads=[pt], writes=[("bigstage", k, cg)])
                    else:
                        emit("dve", cp(dst, ps[:, :]), reads=[pt], writes=[("bigstage", k, cg)])
                r0 = ti * T + tb * P
                cid = emit("sp", lambda e, st=st, r0=r0: e.dma_start(out=out_d[r0:r0 + P, :], in_=st),
                           reads=[("bigstage", k, cg) for cg in range(4)], writes=[("outd", ti, tb)], dsem=sems["xout%d" % k])
                for cg in range(4):
                    pass
                cids.append(cid)
            return cids

        def col_stats(src_fn, n, ps, pt, func=AF.Square, bias_fn=None):
            items = []
            for c in range(n):
                sq = sqb[c % 4]
                emit("act", act_fn(sq[:, :], src_fn(c)[0], func, bias=(bias_fn(c) if bias_fn else None)),
                     reads=src_fn(c)[1], writes=[("sq", c % 4)])
                items.append((sq[:, :], [("sq", c % 4)]))
                presum_step(items, c, (4, 5))
            presum_mm(ps, pt, (4, 5))

        def presum_step(items, c, accs, engs=("dve", "dve")):
            if c < 2:
                return
            par = c % 2
            eng = engs[par]
            acc = tmpf[accs[par]]
            at = ("tf", accs[par])
            if c < 4:
                emit(eng, tt(acc[:, :], items[c - 2][0], items[c][0], ALU.add), reads=items[c - 2][1] + items[c][1], writes=[at])
            else:
                emit(eng, tt(acc[:, :], acc[:, :], items[c][0], ALU.add), reads=[at] + items[c][1], writes=[at])

        def presum_mm(ps, pt, accs):
            a0, a1 = tmpf[accs[0]], tmpf[accs[1]]

            def fn(e):
                e.matmul(ps[:, :], C("ones"), a0[:, :], start=True, stop=False)
                return e.matmul(ps[:, :], C("ones"), a1[:, :], start=False, stop=True)
            emit("pe", fn, reads=[("tf", accs[0]), ("tf", accs[1]), ("cst",)], writes=[pt])

        def rstd_from(ps, pt, scale, out_i):
            sd = tmpf[out_i]
            emit("act", act_fn(sd[:, :], ps[:, :], AF.Sqrt, bias=V_eps(), scale=scale), reads=[pt, ("cst",)], writes=[("tf", out_i)])
            emit("dve", lambda e: e.reciprocal(out=sd[:, :], in_=sd[:, :]), reads=[("tf", out_i)], writes=[("tf", out_i)])
            return sd

        def V_eps():
            return epsb[:, 0:1]

        xs_state = {"items": [], "ready": False}

        def xstat(c):
            if c == 0:
                xs_state["items"] = []
            sq = sqb[c % 4]
            emit("act", act_fn(sq[:, :], xT[:, c, :], AF.Square), reads=[("x", c)], writes=[("sq", c % 4)])
            xs_state["items"].append((sq[:, :], [("sq", c % 4)]))
            presum_step(xs_state["items"], c, (4, 5))
            if c == CH - 1:
                xs_state["ready"] = True

        def x_sumsq(ps, pt):
            if xs_state["ready"]:
                xs_state["ready"] = False
                presum_mm(ps, pt, (4, 5))
            else:
                col_stats(lambda c: (xT[:, c, :], [("x", c)]), CH, ps, pt)

        def prog_groups(sb_list):
            outs = [next_ps() for _ in sb_list]
            slots = sorted(set(sl for sl, _ in sb_list))
            for kc in range(CH):
                def fn(e, kc=kc):
                    ins = None
                    for (sl, base), (ps, pt) in zip(sb_list, outs):
                        ins = e.matmul(ps[:, :], ring[sl][:, base + kc * P: base + (kc + 1) * P], hb[:, kc, :],
                                       start=(kc == 0), stop=(kc == CH - 1))
                    return ins
                emit("pe", fn, reads=[("ring", sl) for sl in slots] + [("hb", kc)], writes=[pt for _, pt in outs])
            return outs

        def rmsnorm_to_hb(gname, L):
            g = V(gname, L)
            ps, pt = next_ps()
            x_sumsq(ps, pt)
            rs = rstd_from(ps, pt, 1.0 / D, 0)
            for c in range(CH):
                emit("dve", stt(hb[:, c, :], xT[:, c, :], g[:, c:c + 1], rs[:, :], ALU.mult, ALU.mult),
                     reads=[("x", c), ("tf", 0), ("vec",)], writes=[("hb", c)])

        def ffn(L, ti, stats_next=True):
            pg.switch_big()
            rmsnorm_to_hb("ffn_norm_g", L)
            dww = V("ffn_dw_w", L)
            dwb = V("ffn_dw_b", L)

            def post(j, half, ps, pt):
                f = half * NF + j
                pb = pbuf[half][j % 2]
                pbt = ("bigpb", half, j % 2)
                ub = ubuf[half][j % 2]
                ubt = ("bigub", half, j % 2)
                emit("pool", cp(pb[:, 0:2], ffst[L][:, f, :]), reads=[("ffst", L, f), ("ffst", L)], writes=[pbt + (0,)])
                emit("act", act_fn(pb[:, 2:2 + T], ps[:, :], AF.Copy), reads=[pt], writes=[pbt + (1,)])
                emit("pool", cp(ffst[L][:, f, :], pb[:, T:T + 2]), reads=[pbt + (0,), pbt + (1,)], writes=[("ffst", L, f)])
                emit("act", act_fn(ub[:, :], ps[:, :], AF.Identity, bias=dwb[:, f:f + 1], scale=dww[:, f * 3 + 2: f * 3 + 3]),
                     reads=[pt, ("vec",)], writes=[ubt])
                emit("dve", stt(ub[:, :], pb[:, 1:1 + T], dww[:, f * 3 + 1:f * 3 + 2], ub[:, :], ALU.mult, ALU.add),
                     reads=[pbt + (0,), pbt + (1,), ubt], writes=[ubt])
                emit("dve", stt(ub[:, :], pb[:, 0:T], dww[:, f * 3:f * 3 + 1], ub[:, :], ALU.mult, ALU.add),
                     reads=[pbt + (0,), pbt + (1,), ubt], writes=[ubt])

            def gate(j):
                ug = ubuf[0][j % 2]
                uv = ubuf[1][j % 2]
                emit("act", act_fn(ug[:, :], ug[:, :], AF.Silu), reads=[("bigub", 0, j % 2)], writes=[("bigub", 0, j % 2)])
                emit("dve", tt(actb[:, j, :], ug[:, :], uv[:, :], ALU.mult),
                     reads=[("bigub", 0, j % 2), ("bigub", 1, j % 2)], writes=[("bigact", j)])

            s0 = W.get("wup%d" % L, 0, 4096)
            s1 = W.get("wup%d" % L, 1, 4096)
            grp = prog_groups([(s0, 0), (s0, 2048), (s1, 0), (s1, 2048)])
            W.release(s0)
            W.release(s1)
            post(0, 0, *grp[0])
            post(0, 1, *grp[1])
            gate(0)
            post(1, 0, *grp[2])
            post(1, 1, *grp[3])
            gate(1)
            for j in range(2, NF):
                slot = W.get("wup%d" % L, j, 4096)
                for half in range(2):
                    ps, pt = next_ps()
                    base = half * 2048
                    emit("pe", mm_group(ps[:, :], [(ring[slot][:, base + kc * P: base + (kc + 1) * P], hb[:, kc, :]) for kc in range(CH)]),
                         reads=[("ring", slot)] + [("hb", kc) for kc in range(CH)], writes=[pt])
                    post(j, half, ps, pt)
                W.release(slot)
                gate(j)
            for m in range(CH):
                slot = W.get("wdn%d" % L, m, 5632)
                ps, pt = next_ps()
                pairs = [(ring[slot][:, k * P:(k + 1) * P], actb[:, k, :]) for k in range(NF)]
                if m == 0:
                    ks = NF - 4
                    emit("pe", mm_group(ps[:, :], pairs[:ks], start=True, stop=False),
                         reads=[("ring", slot)] + [("bigact", k) for k in range(ks)], writes=[pt])
                    emit("pe", mm_group(ps[:, :], pairs[ks:], start=False, stop=True),
                         reads=[("ring", slot)] + [("bigact", k) for k in range(ks, NF)], writes=[pt])
                else:
                    emit("pe", mm_group(ps[:, :], pairs),
                         reads=[("ring", slot)] + [("bigact", k) for k in range(NF)], writes=[pt])
                W.release(slot)
                emit("dve", tt(xT[:, m, :], xT[:, m, :], ps[:, :], ALU.add), reads=[pt, ("x", m)], writes=[("x", m)])
                if stats_next:
                    xstat(m)

        def conformer(L, ti):
            pg.switch_big()
            rmsnorm_to_hb("norm_g", L)
            b_in = V("a_b_in", L)
            emit("pool", cp(Gb[:, :, 0:30], cvst[L][:, :, :]), reads=[("cvst", L)], writes=[("bigGh",)])
            s0 = W.get("win%d" % L, 0, 4096)
            s1 = W.get("win%d" % L, 1, 4096)
            grp0 = prog_groups([(s0, 0), (s0, 2048), (s1, 0), (s1, 2048)])
            W.release(s0)
            W.release(s1)
            for c in range(CH):
                if c < 2:
                    (psa, pta), (psg, ptg) = grp0[2 * c], grp0[2 * c + 1]
                else:
                    slot = W.get("win%d" % L, c, 4096)
                    psa, pta = next_ps()
                    emit("pe", mm_group(psa[:, :], [(ring[slot][:, kc * P:(kc + 1) * P], hb[:, kc, :]) for kc in range(CH)]),
                         reads=[("ring", slot)] + [("hb", kc) for kc in range(CH)], writes=[pta])
                    psg, ptg = next_ps()
                    emit("pe", mm_group(psg[:, :], [(ring[slot][:, 2048 + kc * P:2048 + (kc + 1) * P], hb[:, kc, :]) for kc in range(CH)]),
                         reads=[("ring", slot)] + [("hb", kc) for kc in range(CH)], writes=[ptg])
                    W.release(slot)
                sg = tmpf[2 + c % 2]
                emit("act", act_fn(sg[:, :], psg[:, :], AF.Sigmoid, bias=b_in[:, CH + c:CH + c + 1]),
                     reads=[ptg, ("vec",)], writes=[("tf", 2 + c % 2)])
                emit("dve", stt(Gb[:, c, 30:30 + T], psa[:, :], b_in[:, c:c + 1], sg[:, :], ALU.add, ALU.mult),
                     reads=[pta, ("tf", 2 + c % 2), ("vec",)], writes=[("bigG", c)])
            emit("pool", cp(cvst[L][:, :, :], Gb[:, :, T:T + 30]), reads=[("bigG", c) for c in range(CH)], writes=[("cvst", L)])
            dw_b = V("a_dw_b", L)
            it_y = []
            it_q = []
            dwc = V("a_dw_w", L)
            ND = CONV_DVE_TAPS

            def dve_taps(c):
                acc = tmpf[c % 2]
                at = ("tf", c % 2)
                rd = [("bigG", c), ("bigGh",), ("vec",)]
                emit("act", act_fn(acc[:, :], Gb[:, c, 0:T], AF.Identity, scale=dwc[:, c * 31:c * 31 + 1]), reads=rd, writes=[at])
                for j in range(1, ND):
                    emit("dve", stt(acc[:, :], Gb[:, c, j:j + T], dwc[:, c * 31 + j:c * 31 + j + 1], acc[:, :], ALU.mult, ALU.add),
                         reads=rd + [at], writes=[at])

            if ND > 0:
                dve_taps(0)
            for c in range(CH):
                slot = W.get("cvw%d" % L, c, 31 * P)
                ps, pt = next_ps()
                emit("pe", mm_group(ps[:, :], [(ring[slot][:, j * P:(j + 1) * P], Gb[:, c, j:j + T]) for j in range(ND, 31)]),
                     reads=[("ring", slot), ("bigG", c), ("bigGh",)], writes=[pt])
                W.release(slot)
                sq = sqb[c % 4]
                if ND > 0:
                    if c + 1 < CH:
                        dve_taps(c + 1)
                    emit("dve", stt(yb[:, c, :], ps[:, :], dw_b[:, c:c + 1], tmpf[c % 2][:, :], ALU.add, ALU.add),
                         reads=[pt, ("tf", c % 2), ("vec",)], writes=[("bigy", c)])
                    emit("act", act_fn(sq[:, :], yb[:, c, :], AF.Square), reads=[("bigy", c)], writes=[("sq", c % 4)])
                else:
                    emit("act", act_fn(yb[:, c, :], ps[:, :], AF.Identity, bias=dw_b[:, c:c + 1]), reads=[pt, ("vec",)], writes=[("bigy", c)])
                    emit("act", act_fn(sq[:, :], ps[:, :], AF.Square, bias=dw_b[:, c:c + 1]), reads=[pt, ("vec",)], writes=[("sq", c % 4)])
                it_y.append((yb[:, c, :], [("bigy", c)]))
                it_q.append((sq[:, :], [("sq", c % 4)]))
                presum_step(it_y, c, (2, 3))
                presum_step(it_q, c, (4, 5))
            ps_s, pt_s = next_ps()
            ps_q, pt_q = next_ps()
            presum_mm(ps_s, pt_s, (2, 3))
            presum_mm(ps_q, pt_q, (4, 5))
            mean = tmpf[1]
            emit("act", act_fn(mean[:, :], ps_s[:, :], AF.Identity, scale=1.0 / D), reads=[pt_s], writes=[("tf", 1)])
            msq = tmpf[4]
            emit("dve", tt(msq[:, :], mean[:, :], mean[:, :], ALU.mult), reads=[("tf", 1)], writes=[("tf", 4)])
            var = tmpf[5]
            emit("dve", stt(var[:, :], ps_q[:, :], 1.0 / D, msq[:, :], ALU.mult, ALU.subtract), reads=[pt_q, ("tf", 4)], writes=[("tf", 5)])
            emit("act", act_fn(var[:, :], var[:, :], AF.Sqrt, bias=V_eps()), reads=[("tf", 5), ("cst",)], writes=[("tf", 5)])
            emit("dve", lambda e: e.reciprocal(out=var[:, :], in_=var[:, :]), reads=[("tf", 5)], writes=[("tf", 5)])
            ln_g = V("a_ln_g", L)
            ln_b = V("a_ln_b", L)
            for c in range(CH):
                eng = "dve"
                emit(eng, tt(yb[:, c, :], yb[:, c, :], mean[:, :], ALU.subtract), reads=[("bigy", c), ("tf", 1)], writes=[("bigy", c)])
                emit(eng, tt(yb[:, c, :], yb[:, c, :], var[:, :], ALU.mult), reads=[("bigy", c), ("tf", 5)], writes=[("bigy", c)])
                emit("act", act_fn(hb[:, c, :], yb[:, c, :], AF.Silu, bias=ln_b[:, c:c + 1], scale=ln_g[:, c:c + 1]),
                     reads=[("bigy", c), ("vec",)], writes=[("hb", c)])
            b_out = V("a_b_out", L)
            sl3 = [W.get("wout%d" % L, m, 2048) for m in range(3)]
            grp3 = prog_groups([(sl, 0) for sl in sl3])
            for sl in sl3:
                W.release(sl)
            for m in range(CH):
                if m < 3:
                    ps, pt = grp3[m]
                else:
                    slot = W.get("wout%d" % L, m, 2048)
                    ps, pt = next_ps()
                    emit("pe", mm_group(ps[:, :], [(ring[slot][:, kc * P:(kc + 1) * P], hb[:, kc, :]) for kc in range(CH)]),
                         reads=[("ring", slot)] + [("hb", kc) for kc in range(CH)], writes=[pt])
                    W.release(slot)
                emit("dve", stt(xT[:, m, :], ps[:, :], b_out[:, m:m + 1], xT[:, m, :], ALU.add, ALU.add),
                     reads=[pt, ("x", m), ("vec",)], writes=[("x", m)])
                xstat(m)

        def pool_mixer(L, ti):
            pg.switch_big()
            g = V("norm_g", L)
            ps, pt = next_ps()
            x_sumsq(ps, pt)
            rs = rstd_from(ps, pt, 1.0 / D, 0)
            emit("pool", cp(Hb[:, :, 0:15], plst[:, :, :]), reads=[("plst",)], writes=[("bigHh",)], small=True)
            ic = C("invcnt")
            sc = V("b_scale", L)
            for gi in range(4):
                w = POOL_W[gi]
                for c in range(gi * 4, gi * 4 + 4):
                    emit("dve", stt(Hb[:, c, 15:15 + T], xT[:, c, :], g[:, c:c + 1], rs[:, :], ALU.mult, ALU.mult),
                         reads=[("x", c), ("tf", 0), ("vec",)], writes=[("bigH", c)])
                for c in range(gi * 4, gi * 4 + 4):
                    eng = "dve"
                    bufs = [pbuf[0][c % 2], pbuf[1][c % 2]]
                    bt = [("bigpb", 0, c % 2, 0), ("bigpb", 1, c % 2, 0)]
                    src = Hb[:, c, :]
                    srct = [("bigH", c), ("bigHh",)]
                    sh = 1
                    bi = 0
                    while sh < w:
                        dst = bufs[bi]
                        n = 15 + T
                        emit(eng, tt(dst[:, sh:n], src[:, sh:n], src[:, 0:n - sh], ALU.add), reads=srct, writes=[bt[bi]])
                        src = dst
                        srct = [bt[bi]]
                        bi ^= 1
                        sh *= 2
                    emit("dve", stt(hb[:, c, :], src[:, 15:15 + T], 1.0 / w, Hb[:, c, 15:15 + T], ALU.mult, ALU.subtract),
                         reads=srct + [("bigH", c)], writes=[("hb", c)])
                    if ti == 0:
                        t16 = tmpf[2 + c % 2]
                        emit(eng, tt(t16[:, 0:16], src[:, 15:31], ic[:, gi * 16:(gi + 1) * 16], ALU.mult), reads=srct + [("cst",)], writes=[("tf", 2 + c % 2)], small=True)
                        emit(eng, tt(hb[:, c, 0:16], t16[:, 0:16], Hb[:, c, 15:31], ALU.subtract), reads=[("tf", 2 + c % 2), ("bigH", c)], writes=[("hb", c)], small=True)
                slot = W.get("wg%d" % L, gi, 2048)
                for mc in range(4):
                    ps, pt = next_ps()
                    emit("pe", mm_group(ps[:, :], [(ring[slot][:, kc * 512 + mc * P: kc * 512 + (mc + 1) * P], hb[:, gi * 4 + kc, :]) for kc in range(4)]),
                         reads=[("ring", slot)] + [("hb", gi * 4 + kc) for kc in range(4)], writes=[pt])
                    m = gi * 4 + mc
                    emit("dve", stt(xT[:, m, :], ps[:, :], sc[:, m:m + 1], xT[:, m, :], ALU.mult, ALU.add),
                         reads=[pt, ("x", m), ("vec",)], writes=[("x", m)])
                    xstat(m)
                W.release(slot)
            emit("pool", cp(plst[:, :, :], Hb[:, :, T:T + 15]), reads=[("bigH", c) for c in range(CH)], writes=[("plst",)])

        def attention(L, ti):
            pg.switch_big()
            rmsnorm_to_hb("norm_g", L)
            emit("sp", lambda e: e.dma_start(out=posi, in_=pos_d[:, ti * T:(ti + 1) * T]), writes=[("tf", 3)], dsem=sems["ld"])
            ang = tmpf[1]
            emit("dve", cp(ang[:, :], posi), reads=[("tf", 3)], writes=[("tf", 1)])
            emit("dve", lambda e: e.tensor_single_scalar(out=ang[:, :], in_=ang[:, :], scalar=C("invf")[:, 0:1], op=ALU.mult),
                 reads=[("tf", 1), ("cst",)], writes=[("tf", 1)])
            emit("dve", lambda e: e.tensor_single_scalar(out=ang[:, :], in_=ang[:, :], scalar=1.0 / TWO_PI, op=ALU.mult),
                 reads=[("tf", 1)], writes=[("tf", 1)])
            for which, dstb, dtk in ((0, sinb, ("bigsin",)), (1, cosb, ("bigcos",))):
                qq = tmpf[4]
                if which == 1:
                    emit("dve", lambda e: e.tensor_single_scalar(out=ang[:, :], in_=ang[:, :], scalar=0.25, op=ALU.add),
                         reads=[("tf", 1)], writes=[("tf", 1)])
                emit("dve", cp(posi, ang[:, :]), reads=[("tf", 1)], writes=[("tf", 3)])
                emit("dve", cp(qq[:, :], posi), reads=[("tf", 3)], writes=[("tf", 4)])
                emit("dve", tt(qq[:, :], ang[:, :], qq[:, :], ALU.subtract), reads=[("tf", 1), ("tf", 4)], writes=[("tf", 4)])
                emit("act", act_fn(dstb, qq[:, :], AF.Sin, scale=6.28318), reads=[("tf", 4)], writes=[dtk])
            emit("dve", lambda e: e.tensor_single_scalar(out=sinb, in_=sinb, scalar=C("sgn")[:, 0:1], op=ALU.mult),
                 reads=[("bigsin",), ("cst",)], writes=[("bigsin",)])

            gq = V("c_q_norm_g", L)
            gk = V("c_k_norm_g", L)

            chunks = [("wq%d" % L, c, gq, QT[:, c, :], ("bigQ", c)) for c in range(CH)] + \
                     [("wk%d" % L, kv, gk, KT[:, kv, P:P + T], ("KTc", kv)) for kv in range(4)]
            sl3 = [W.get("wq%d" % L, c, 2048) for c in range(3)]
            grpq = prog_groups([(sl, 0) for sl in sl3])
            for sl in sl3:
                W.release(sl)
            st = {}

            def stageA(k):
                name, idx, gcol, dst, dtok = chunks[k]
                if k < 3:
                    ps, pt = grpq[k]
                else:
                    slot = W.get(name, idx, 2048)
                    ps, pt = next_ps()
                    emit("pe", mm_group(ps[:, :], [(ring[slot][:, kc * P:(kc + 1) * P], hb[:, kc, :]) for kc in range(CH)]),
                         reads=[("ring", slot)] + [("hb", kc) for kc in range(CH)], writes=[pt])
                    W.release(slot)
                held.add(pt[1])
                sq = sqb[k % 2]
                emit("act", act_fn(sq[:, :], ps[:, :], AF.Square), reads=[pt], writes=[("sq", k % 2)])
                st[k] = (ps, pt)

            def stageB(k):
                name, idx, gcol, dst, dtok = chunks[k]
                ps, pt = st[k]
                sq = sqb[k % 2]
                ps2, pt2 = next_ps()
                emit("pe", lambda e: e.matmul(ps2[:, :], C("bones"), sq[:, :], start=True, stop=True), reads=[("sq", k % 2), ("cst",)], writes=[pt2])
                rs = tmpf[2 + k % 2]
                rt = ("tf", 2 + k % 2)
                emit("act", act_fn(rs[:, :], ps2[:, :], AF.Sqrt, bias=V_eps(), scale=1.0 / 64), reads=[pt2, ("cst",)], writes=[rt])
                emit("dve", lambda e: e.reciprocal(out=rs[:, :], in_=rs[:, :]), reads=[rt], writes=[rt])
                qn = ubuf[0][k % 2]
                qt = ("bigub", 0, k % 2)
                emit("dve", stt(qn[:, :], ps[:, :], gcol[:, 0:1], rs[:, :], ALU.mult, ALU.mult), reads=[pt, rt, ("vec",)], writes=[qt])
                held.discard(pt[1])

            def stageC(k):
                name, idx, gcol, dst, dtok = chunks[k]
                qn = ubuf[0][k % 2]
                qt = ("bigub", 0, k % 2)
                ps3, pt3 = next_ps()
                emit("pe", lambda e: e.matmul(ps3[:, :], C("perm"), qn[:, :], start=True, stop=True), reads=[qt, ("cst",)], writes=[pt3])
                t2 = ubuf[1][k % 2]
                tt2 = ("bigub", 1, k % 2)
                emit("dve", tt(t2[:, :], ps3[:, :], sinb, ALU.mult), reads=[pt3, ("bigsin",)], writes=[tt2])
                emit("dve", tt(qn[:, :], qn[:, :], cosb, ALU.mult), reads=[qt, ("bigcos",)], writes=[qt])
                emit("dve", tt(dst, qn[:, :], t2[:, :], ALU.add), reads=[qt, tt2], writes=[dtok])

            nqk = len(chunks)
            for i in range(nqk + 2):
                if i < nqk:
                    stageA(i)
                if 0 <= i - 1 < nqk:
                    stageB(i - 1)
                if 0 <= i - 2 < nqk:
                    stageC(i - 2)
            for half in range(2):
                slot = W.get("wv%d" % L, half, 4096)
                for tb in range(4):
                    ps, pt = next_ps()
                    emit("pe", mm_group(ps[:, 0:256], [(hb[:, kc, tb * P:(tb + 1) * P], ring[slot][:, kc * 256:(kc + 1) * 256]) for kc in range(CH)]),
                         reads=[("ring", slot)] + [("hb", kc) for kc in range(CH)], writes=[pt])
                    emit("act", act_fn(VT[:, 1 + tb, half * 256:(half + 1) * 256], ps[:, 0:256], AF.Copy), reads=[pt], writes=[("VTc", tb, half)])
                W.release(slot)
            OT = hb
            kpi = 0
            for kv in range(4):
                for qb in range(4):
                    first = (ti == 0 and qb == 0)
                    pt_buf = PTb[kpi % 2]
                    ptt = ("bigP", kpi % 2)
                    kpi += 1
                    kbs = [1] if first else [0, 1]
                    for par in range(2):
                        for kb in kbs:
                            ps, pt = next_ps()
                            kcol = qb * P + kb * P
                            lhs = KT[par * 64:(par + 1) * 64, kv, kcol:kcol + P]
                            rhs = QT[par * 64:(par + 1) * 64, kv * 4:(kv + 1) * 4, qb * P:(qb + 1) * P]
                            mk = mprevb if kb == 0 else mcurb

                            def sfn(e, ps=ps, lhs=lhs, rhs=rhs, mk=mk):
                                e.matmul(ps[:, :], lhs, rhs, start=True, stop=False)
                                return e.matmul(ps[:, :], mk, identqb, start=False, stop=True)
                            emit("pe", sfn, reads=[("KTc", kv), ("KTh",), ("cb",)] + [("bigQ", kv * 4 + i) for i in range(4)], writes=[pt])
                            emit("act", act_fn(pt_buf[:, kb, par * 512:(par + 1) * 512], ps[:, :], AF.Exp, scale=0.125), reads=[pt], writes=[ptt + (par, kb)])
                    dix = 4 + (kpi % 2)
                    den = tmpf[dix]
                    dt_ = ("tf", dix)
                    nums = []
                    for par in range(2):
                        psn, ptn = next_ps()
                        psd, ptd = next_ps()

                        def nfn(e, psn=psn, par=par, kbs=kbs, qb=qb, kv=kv, pt_buf=pt_buf):
                            ins = None
                            for i, kb in enumerate(kbs):
                                vsrc = VT[:, qb + kb, kv * P:(kv + 1) * P]
                                ins = e.matmul(psn[:, :], vsrc, pt_buf[:, kb, par * 512:(par + 1) * 512], start=(i == 0), stop=(i == len(kbs) - 1))
                            return ins

                        def dfn(e, psd=psd, par=par, kbs=kbs, pt_buf=pt_buf):
                            ins = None
                            for i, kb in enumerate(kbs):
                                ins = e.matmul(psd[:, :], identb_ones, pt_buf[:, kb, par * 512:(par + 1) * 512], start=(i == 0), stop=(i == len(kbs) - 1))
                            return ins
                        rd = [ptt + (par, kb) for kb in kbs]
                        emit("pe", nfn, reads=rd + [("VTc", t_, h_) for t_ in range(4) for h_ in range(2)] + [("VTh",)], writes=[ptn])
                        emit("pe", dfn, reads=rd + [("cb",)], writes=[ptd])
                        lo, hi = par * 64, (par + 1) * 64
                        esv = est[lo:hi, kv * 8 + par: kv * 8 + 8: 2].unsqueeze(2).to_broadcast([64, 4, P])
                        emit("dve", tt(den[lo:hi, :].rearrange("p (a b) -> p a b", a=4), psd[lo:hi, :].rearrange("p (a b) -> p a b", a=4), esv, ALU.add),
                             reads=[ptd, ("est",)], writes=[dt_])
                        nums.append((psn, ptn))
                    emit("dve", lambda e, den=den: e.reciprocal(out=den[:, :], in_=den[:, :]), reads=[dt_], writes=[dt_])
                    for par in range(2):
                        psn, ptn = nums[par]
                        lo, hi = par * 64, (par + 1) * 64
                        emit("dve", tt(OT[lo:hi, kv * 4:(kv + 1) * 4, qb * P:(qb + 1) * P], psn[lo:hi, :].rearrange("p (a b) -> p a b", a=4),
                                       den[lo:hi, :].rearrange("p (a b) -> p a b", a=4), ALU.mult),
                             reads=[ptn, dt_], writes=[("hb", kv * 4 + i_) for i_ in range(4)])
            emit("pool", cp(KT[:, :, 0:P], KT[:, :, T:T + P]), reads=[("KTc", kv) for kv in range(4)], writes=[("KTh",)])
            emit("pool", cp(VT[:, 0, :], VT[:, 4, :]), reads=[("VTc", 3, 0), ("VTc", 3, 1)], writes=[("VTh",)])
            for m in range(CH):
                slot = W.get("wo%d" % L, m, 2048)
                ps, pt = next_ps()
                emit("pe", mm_group(ps[:, :], [(ring[slot][:, kc * P:(kc + 1) * P], OT[:, kc, :]) for kc in range(CH)]),
                     reads=[("ring", slot)] + [("hb", kc) for kc in range(CH)], writes=[pt])
                W.release(slot)
                emit("dve", tt(xT[:, m, :], xT[:, m, :], ps[:, :], ALU.add), reads=[pt, ("x", m)], writes=[("x", m)])
                xstat(m)

        epsb = sb("epsb", [P, 1], F32)
        negpi = sb("negpi", [P, 1], F32)
        onesb = sb("onesb", [P, P], BF16)
        identb_ones = onesb[:, :]

        def body():
            state["ps"] = 0
            pg.reset()
            emit("pool", lambda e: e.memset(epsb[:, :], EPS), writes=[("cst2",)])
            emit("pool", lambda e: e.memset(negpi[:, :], -math.pi), writes=[("cst2",)])
            emit("pool", lambda e: e.memset(onesb[:, :], 1.0), writes=[("cst2",)])
            setup()
            for L in layers[:1]:
                cast_layer(L)
            if not pg.dry:
                cid = (sems["pool"], 3)
                for eng in ("pe", "act", "dve", "sp"):
                    pg.wait_all(eng, [cid])
            W.start() if not pg.dry else None
            outc = []
            for ti in range(NT):
                load_x(ti)
                for li, L in enumerate(layers):
                    if ti == 0 and li + 1 < len(layers):
                        cast_layer(layers[li + 1])
                    if do_mixer:
                        {"conf": conformer, "pool": pool_mixer, "attn": attention}[MIXERS[L]](L, ti)
                    if do_ffn:
                        ffn(L, ti, stats_next=(li + 1 < len(layers)))
                outc += store_x(ti)
            return outc

        pg.dry = True
        body()
        pg.dry = False
        outc = body()
        last = {}
        for s, v in outc:
            last[s.name] = (s, max(v, last.get(s.name, (s, 0))[1]))
        pg.wait_all("sp", list(last.values()))

        with nc.Block() as block:
            @block.sync
            def _(e):
                for f in pg.q["sp"]:
                    f(e)

            @block.tensor
            def _(e):
                for f in pg.q["pe"]:
                    f(e)

            @block.scalar
            def _(e):
                for f in pg.q["act"]:
                    f(e)

            @block.vector
            def _(e):
                for f in pg.q["dve"]:
                    f(e)

            @block.gpsimd
            def _(e):
                for f in pg.q["pool"]:
                    f(e)
        print("instructions:", pg.n_inst, {k: len(v) for k, v in pg.q.items()}, "sbuf left", nc.sbuf_bytes_remaining)
    return nc


def prepare_shared(inp, layers):
    vecs, voffs = pack_vec(inp)
    csts, coffs, cbarr = make_consts()
    slabs = weight_slabs(inp, layers)
    return vecs, voffs, csts, coffs, slabs, cbarr


def run(inp, S, layers, n_cores, do_ffn=True, do_mixer=True):
    vecs, voffs, csts, coffs, slabs, cbarr = prepare_shared(inp, layers)
    slab_shapes = {k: v.shape for k, v in slabs.items()}
    import time as _time
    _t0 = _time.time()
    nc = build_program(S, layers, slab_shapes, voffs, coffs, vecs.shape[1], csts.shape[1], do_ffn=do_ffn, do_mixer=do_mixer)
    print("build time", _time.time() - _t0, flush=True)
    x = np.asarray(inp["x"], np.float32)
    pos = np.asarray(inp["positions"], np.int32)
    in_maps = []
    for b in range(n_cores):
        m = {"x": np.ascontiguousarray(x[b]),
             "posb": np.ascontiguousarray(np.broadcast_to(pos[b][None, :], (P, S))),
             "vec": vecs, "cst": csts, "cstb": cbarr}
        m.update(slabs)
        in_maps.append(m)
    _t0 = _time.time()
    import os as _os
    _tr = _os.environ.get("KTRACE", "0") == "1"
    res = run_bass_kernel_spmd(nc, in_maps, core_ids=list(range(n_cores)), **({"trace": True} if _tr else {}))
    if _tr:
        print("exec_time_ns", res.exec_time_ns, flush=True)
    print("run time", _time.time() - _t0, flush=True)
    return np.stack([r["out"] for r in res.results], axis=0)


_INPUT_NAMES = (
    "x", "positions",
    "l0_norm_g", "l0_a_w_in", "l0_a_b_in", "l0_a_dw_w", "l0_a_dw_b", "l0_a_ln_g", "l0_a_ln_b", "l0_a_w_out", "l0_a_b_out",
    "l0_ffn_norm_g", "l0_ffn_w_up", "l0_ffn_dw_w", "l0_ffn_dw_b", "l0_ffn_w_down",
    "l1_norm_g", "l1_b_w_group", "l1_b_scale",
    "l1_ffn_norm_g", "l1_ffn_w_up", "l1_ffn_dw_w", "l1_ffn_dw_b", "l1_ffn_w_down",
    "l2_norm_g", "l2_c_w_qkv", "l2_c_q_norm_g", "l2_c_k_norm_g", "l2_c_sinks", "l2_c_w_o",
    "l2_ffn_norm_g", "l2_ffn_w_up", "l2_ffn_dw_w", "l2_ffn_dw_b", "l2_ffn_w_down",
    "l3_norm_g", "l3_a_w_in", "l3_a_b_in", "l3_a_dw_w", "l3_a_dw_b", "l3_a_ln_g", "l3_a_ln_b", "l3_a_w_out", "l3_a_b_out",
    "l3_ffn_norm_g", "l3_ffn_w_up", "l3_ffn_dw_w", "l3_ffn_dw_b", "l3_ffn_w_down",
)


def kernel(**inputs):
    inp = {k: inputs[k] for k in _INPUT_NAMES}
    return run(inp, 4096, [0, 1, 2, 3], 8).astype(np.float32)
```

```python
import math
import numpy as np
import concourse.bass as bass
import concourse.mybir as mybir
from concourse.bass_utils import run_bass_kernel_spmd

F32 = mybir.dt.float32
BF16 = mybir.dt.bfloat16
I32 = mybir.dt.int32
AF = mybir.ActivationFunctionType
ALU = mybir.AluOpType

P = 128
D = 2048
CH = 16
T = 512
DFF = 5632
NF = 44
EPS = 1e-6
RING = 4
SLOTW = 5632
NEG = -30000.0
TWO_PI = 2.0 * math.pi
ROPE_THETA = 500000.0
POOL_W = (2, 4, 8, 16)
CONV_DVE_TAPS = 6
MIXERS = ("conf", "pool", "attn", "conf")


class Sem:
    def __init__(self, h, name):
        self.h = h
        self.name = name
        self.val = 0


class Prog:
    ENGS = ("pe", "act", "dve", "pool", "sp")

    def __init__(self, nc, sems):
        self.nc = nc
        self.q = {e: [] for e in self.ENGS}
        self.esem = {e: sems[e] for e in ("pe", "act", "dve", "pool")}
        self.waited = {e: {} for e in self.ENGS}
        self.tok = {}
        self.big_frontier = []
        self.dry = False
        self.n_inst = 0
        self.small = set()

    def reset(self):
        self.tok = {}
        self.big_frontier = []

    def _entry(self, t):
        e = self.tok.get(t)
        if e is None:
            e = {"w": None, "r": []}
            if isinstance(t, tuple) and isinstance(t[0], str) and t[0].startswith("big"):
                e["r"] = list(self.big_frontier)
            self.tok[t] = e
        return e

    def switch_big(self):
        if self.dry:
            return
        fr = {}
        for t in list(self.tok.keys()):
            if isinstance(t, tuple) and isinstance(t[0], str) and t[0].startswith("big"):
                e = self.tok.pop(t)
                for d in ([e["w"]] if e["w"] else []) + e["r"]:
                    if d[0].name not in fr or fr[d[0].name][1] < d[1]:
                        fr[d[0].name] = d
        for d in self.big_frontier:
            if d[0].name not in fr or fr[d[0].name][1] < d[1]:
                fr[d[0].name] = d
        self.big_frontier = list(fr.values())

    def set_writer(self, t, sem, val):
        self.tok[t] = {"w": (sem, val), "r": []}

    def emit(self, eng, fn, reads=(), writes=(), dsem=None, small=False):
        if self.dry:
            return
        deps = {}

        def add(d):
            if d is None:
                return
            s, v = d
            if dsem is None and eng in self.esem and s is self.esem[eng] and (s.name, v) not in self.small:
                return
            if s.name not in deps or deps[s.name][1] < v:
                deps[s.name] = (s, v)

        for t in reads:
            add(self._entry(t)["w"])
        for t in writes:
            e = self._entry(t)
            add(e["w"])
            for d in e["r"]:
                add(d)
        wl = []
        wd = self.waited[eng]
        for name, (s, v) in deps.items():
            if wd.get(name, 0) >= v:
                continue
            wd[name] = v
            wl.append((s.h, v))
        if dsem is not None:
            dsem.val += 16
            cid = (dsem, dsem.val)
            inc = (dsem.h, 16)
        else:
            s = self.esem[eng]
            s.val += 1
            cid = (s, s.val)
            inc = (s.h, 1)

        def run(e, wl=wl, fn=fn, inc=inc):
            for h, v in wl:
                e.wait_ge(h, v)
            fn(e).then_inc(inc[0], inc[1])

        self.q[eng].append(run)
        self.n_inst += 1
        if small:
            self.small.add((cid[0].name, cid[1]))
        for t in writes:
            self.tok[t] = {"w": cid, "r": []}
        for t in reads:
            if t in writes:
                continue
            e = self._entry(t)
            r = [d for d in e["r"] if d[0] is not cid[0]]
            r.append(cid)
            e["r"] = r
        return cid

    def wait_all(self, eng, cids):
        if self.dry:
            return
        wl = [(s.h, v) for s, v in cids]

        def run(e, wl=wl):
            for h, v in wl:
                e.wait_ge(h, v)

        self.q[eng].append(run)


def _fm(v):
    v = np.asarray(v, dtype=np.float32)
    return np.ascontiguousarray(v.reshape(-1, P).T)


def _slab_pair(w, npair):
    K = w.shape[0]
    kc = K // P
    a = np.asarray(w).reshape(kc, P, 2, npair, P)
    return np.ascontiguousarray(a.transpose(3, 1, 2, 0, 4)).reshape(npair, P, 2 * kc * P)


def _slab_m(w):
    K, M = w.shape
    a = np.asarray(w).reshape(K // P, P, M // P, P)
    return np.ascontiguousarray(a.transpose(2, 1, 0, 3)).reshape(M // P, P, (K // P) * P)


def make_consts():
    c = {}
    c["ident"] = np.eye(P, dtype=np.float32)
    c["ones"] = np.ones((P, P), np.float32)
    bo = np.zeros((P, P), np.float32)
    bo[:64, :64] = 1
    bo[64:, 64:] = 1
    c["bones"] = bo
    pm = np.zeros((P, P), np.float32)
    for m in range(P):
        dh = m % 64
        if dh < 8:
            pm[m + 8, m] = 1
        elif dh < 16:
            pm[m - 8, m] = 1
    c["perm"] = pm
    ii = np.arange(P)[:, None]
    jj = np.arange(P)[None, :]
    cbf = {}
    cbf["ident"] = np.eye(P, dtype=np.float32)
    cbf["mprev"] = np.where(jj > ii, 0.0, NEG).astype(np.float32)
    cbf["mcur"] = np.where(jj <= ii, 0.0, NEG).astype(np.float32)
    cbf["identq"] = np.tile(np.eye(P, dtype=np.float32), (1, 4))
    inv_freq = (np.float32(ROPE_THETA) ** (-(np.arange(0, 16, 2, dtype=np.float32)) / np.float32(16))).astype(np.float32)
    invf = np.zeros((P, 1), np.float32)
    nsgn = np.zeros((P, 1), np.float32)
    for p in range(P):
        dh = p % 64
        if dh < 16:
            invf[p, 0] = inv_freq[dh % 8]
            nsgn[p, 0] = -1.0 if dh < 8 else 1.0
    c["invf"] = invf
    c["sgn"] = nsgn
    ic = np.zeros((P, 4 * 16), np.float32)
    for g, w in enumerate(POOL_W):
        ic[:, g * 16:(g + 1) * 16] = (1.0 / np.minimum(np.arange(16) + 1, w)).astype(np.float32)[None, :]
    c["invcnt"] = ic
    cbarr = np.ascontiguousarray(np.concatenate([cbf["ident"], cbf["mprev"], cbf["mcur"], cbf["identq"]], axis=1))
    offs = {}
    cols = []
    o = 0
    for k, v in c.items():
        offs[k] = (o, v.shape[1])
        cols.append(v)
        o += v.shape[1]
    return np.ascontiguousarray(np.concatenate(cols, axis=1)), offs, cbarr


def pack_vec(inp):
    parts = []
    offs = {}
    o = 0

    def add(name, a):
        nonlocal o
        a = np.ascontiguousarray(a, dtype=np.float32)
        offs[name] = (o, a.shape[1])
        parts.append(a)
        o += a.shape[1]

    for L in range(4):
        p = "l%d_" % L
        add(p + "norm_g", _fm(inp[p + "norm_g"]))
        add(p + "ffn_norm_g", _fm(inp[p + "ffn_norm_g"]))
        dw = np.asarray(inp[p + "ffn_dw_w"], np.float32).reshape(3, 88, P).transpose(2, 1, 0).reshape(P, 88 * 3)
        add(p + "ffn_dw_w", dw)
        add(p + "ffn_dw_b", _fm(inp[p + "ffn_dw_b"]))
        if MIXERS[L] == "conf":
            add(p + "a_b_in", _fm(inp[p + "a_b_in"]))
            dwc = np.asarray(inp[p + "a_dw_w"], np.float32).reshape(31, 16, P).transpose(2, 1, 0).reshape(P, 16 * 31)
            add(p + "a_dw_w", dwc)
            add(p + "a_dw_b", _fm(inp[p + "a_dw_b"]))
            add(p + "a_ln_g", _fm(inp[p + "a_ln_g"]))
            add(p + "a_ln_b", _fm(inp[p + "a_ln_b"]))
            add(p + "a_b_out", _fm(inp[p + "a_b_out"]))
        elif MIXERS[L] == "pool":
            add(p + "b_scale", _fm(inp[p + "b_scale"]))
        else:
            add(p + "c_q_norm_g", np.tile(np.asarray(inp[p + "c_q_norm_g"], np.float32), 2).reshape(P, 1))
            add(p + "c_k_norm_g", np.tile(np.asarray(inp[p + "c_k_norm_g"], np.float32), 2).reshape(P, 1))
            add(p + "c_sinks", np.broadcast_to(np.asarray(inp[p + "c_sinks"], np.float32)[None, :], (P, 32)))
    return np.ascontiguousarray(np.concatenate(parts, axis=1)), offs


def weight_slabs(inp, layers):
    out = {}
    for L in layers:
        p = "l%d_" % L
        out["wup%d" % L] = _slab_pair(inp[p + "ffn_w_up"], NF)
        out["wdn%d" % L] = _slab_m(inp[p + "ffn_w_down"])
        if MIXERS[L] == "conf":
            out["win%d" % L] = _slab_pair(inp[p + "a_w_in"], 16)
            out["wout%d" % L] = _slab_m(inp[p + "a_w_out"])
        elif MIXERS[L] == "pool":
            wg = np.asarray(inp[p + "b_w_group"]).reshape(4, 4, P, 512)
            out["wg%d" % L] = np.ascontiguousarray(wg.transpose(0, 2, 1, 3)).reshape(4, P, 2048)
        else:
            wqkv = np.asarray(inp[p + "c_w_qkv"])
            out["wq%d" % L] = _slab_m(wqkv[:, :2048])
            wk = wqkv[:, 2048:2304].reshape(16, P, 4, 64)
            wk = np.concatenate([wk, wk], axis=3)
            out["wk%d" % L] = np.ascontiguousarray(wk.transpose(2, 1, 0, 3)).reshape(4, P, 2048)
            wv = wqkv[:, 2304:2560].reshape(16, P, 2, 2, 64)
            wv = np.stack([wv, wv], axis=4)
            out["wv%d" % L] = np.ascontiguousarray(wv.transpose(2, 1, 0, 3, 4, 5)).reshape(2, P, 16 * 256)
            out["wo%d" % L] = _slab_m(inp[p + "c_w_o"])
    return out


def build_program(S, layers, slab_shapes, voffs, coffs, nvec, ncst, do_ffn=True, do_mixer=True):
    NT = S // T
    nc = bass.Bass("TRN2", target_bir_lowering=False)
    x_d = nc.dram_tensor("x", [S, D], F32, kind="ExternalInput").ap()
    pos_d = nc.dram_tensor("posb", [P, S], I32, kind="ExternalInput").ap()
    vec_d = nc.dram_tensor("vec", [P, nvec], F32, kind="ExternalInput").ap()
    cst_d = nc.dram_tensor("cst", [P, ncst], F32, kind="ExternalInput").ap()
    cstb_d = nc.dram_tensor("cstb", [P, 7 * P], F32, kind="ExternalInput").ap()
    out_d = nc.dram_tensor("out", [S, D], F32, kind="ExternalOutput").ap()
    w_in = {}
    w_sc = {}
    for name, shp in slab_shapes.items():
        w_in[name] = nc.dram_tensor(name, list(shp), F32, kind="ExternalInput").ap()
        w_sc[name] = nc.dram_tensor(name + "_b", list(shp), BF16).ap()
    for L in layers:
        if MIXERS[L] == "conf":
            w_sc["cvw%d" % L] = nc.dram_tensor("cvw%d_b" % L, [16, P, 31 * P], BF16).ap()

    sb = nc.alloc_sbuf_tensor
    xT = sb("xT", [P, CH, T], F32)
    hb = sb("hb", [P, CH, T], BF16)
    BIG = sb("BIG", [P, NF * T], BF16)
    ring = [sb("ring%d" % i, [P, SLOTW], BF16) for i in range(RING)]
    vec = sb("vecs", [P, nvec], F32)
    cst = sb("csts", [P, ncst], F32)
    cb = sb("cstbs", [P, 7 * P], BF16)
    R2 = sb("R2", [P, CH * (30 + T)], BF16)
    R2f = R2[:, :].bitcast(F32)
    PBW = 15 + T
    pbuf = [[R2f[:, (a * 2 + b) * PBW:(a * 2 + b + 1) * PBW] for b in range(2)] for a in range(2)]
    ubuf = [[R2f[:, 4 * PBW + (a * 2 + b) * T: 4 * PBW + (a * 2 + b + 1) * T] for b in range(2)] for a in range(2)]
    sqb = [sb("sq%d" % i, [P, T], F32) for i in range(4)]
    tmpf = [sb("tf%d" % i, [P, T], F32) for i in range(6)]
    ffst = {L: sb("ffst%d" % L, [P, 88, 2], F32) for L in layers}
    Gb = R2[:, :].rearrange("p (c t) -> p c t", c=CH)
    cvst = {L: sb("cvst%d" % L, [P, CH, 30], BF16) for L in layers if MIXERS[L] == "conf"}
    plst = sb("plst", [P, CH, 15], F32)
    KT = sb("KT", [P, 4, P + T], BF16)
    VT = sb("VT", [P, 5, 512], BF16)
    est = sb("est", [P, 32], F32)
    psum = [nc.alloc_psum_tensor("ps%d" % i, [P, 512], F32) for i in range(8)]

    actb = BIG[:, 0:NF * T].rearrange("p (j t) -> p j t", j=NF)
    bigf = BIG[:, :].bitcast(F32)
    yb = bigf[:, 0:CH * T].rearrange("p (c t) -> p c t", c=CH)
    Hb = bigf[:, 0:CH * (15 + T)].rearrange("p (c t) -> p c t", c=CH)
    QT = BIG[:, 0:CH * T].rearrange("p (c t) -> p c t", c=CH)
    PTb = [BIG[:, CH * T + i * 2048: CH * T + (i + 1) * 2048].rearrange("p (k q) -> p k q", k=2) for i in range(2)]
    stage = [bigf[:, i * D:(i + 1) * D] for i in range(2)]
    cosb = bigf[:, 6144:6144 + T]
    sinb = bigf[:, 6144 + T:6144 + 2 * T]
    dgb = [BIG[:, i * 31 * P:(i + 1) * 31 * P] for i in range(2)]
    posi = tmpf[3][:, :].bitcast(I32)

    def V(name, L=None):
        key = name if L is None else "l%d_%s" % (L, name)
        o, n = voffs[key]
        return vec[:, o:o + n]

    def C(name):
        o, n = coffs[name]
        return cst[:, o:o + n]

    identb = cb[:, 0:P]
    mprevb = cb[:, P:2 * P]
    mcurb = cb[:, 2 * P:3 * P]
    identqb = cb[:, 3 * P:7 * P]

    import contextlib
    with contextlib.ExitStack() as es:
        sems = {}
        for nm in ("pe", "act", "dve", "pool", "pre", "ld", "ld2", "xin0", "xin1", "xout0", "xout1", "misc0", "misc1") + tuple("r%d" % i for i in range(RING)) + tuple("wb%d" % i for i in range(RING)):
            sems[nm] = Sem(es.enter_context(nc.semaphore(nm)), nm)
        for nm in slab_shapes:
            sems["c_" + nm] = Sem(es.enter_context(nc.semaphore("c_" + nm)), "c_" + nm)
        pg = Prog(nc, sems)
        emit = pg.emit
        state = {"ps": 0, "ti": 0}

        held = set()

        def next_ps():
            while True:
                i = state["ps"] % 8
                state["ps"] += 1
                if i not in held:
                    return psum[i], ("ps", i)

        class WStream:
            def __init__(self):
                self.plan = []
                self.i = 0
                self.issued = 0

            def _issue(self, k):
                name, idx, width, pti = self.plan[k]
                slot = k % RING
                if pti == 0 and name in w_in:
                    src32 = w_in[name][idx]
                    emit("pool", lambda e, slot=slot, src32=src32, width=width: e.dma_start(out=ring[slot][:, 0:width], in_=src32),
                         writes=[("ring", slot)], dsem=sems["r%d" % slot])
                    dst = w_sc[name][idx]
                    emit("sp", lambda e, slot=slot, dst=dst, width=width: e.dma_start(out=dst, in_=ring[slot][:, 0:width]),
                         reads=[("ring", slot)], writes=[("wsc", name, idx)], dsem=sems["wb%d" % slot])
                    return
                src = w_sc[name][idx]
                rd = [("wsc", name, idx)] if name in w_in else [("wscratch2", 0), ("wscratch2", 1)]
                emit("sp", lambda e, slot=slot, src=src, width=width: e.dma_start(out=ring[slot][:, 0:width], in_=src),
                     reads=rd, writes=[("ring", slot)], dsem=sems["r%d" % slot])

            def start(self):
                self.i = 0
                self.issued = 0
                while self.issued < min(RING, len(self.plan)):
                    self._issue(self.issued)
                    self.issued += 1

            def get(self, name, idx, width):
                if pg.dry:
                    self.plan.append((name, idx, width, state["ti"]))
                    self.i += 1
                    return 0
                assert self.plan[self.i][:3] == (name, idx, width), (self.plan[self.i], name, idx, width)
                slot = self.i % RING
                self.i += 1
                return slot

            def release(self, slot):
                if pg.dry:
                    return
                if self.issued < len(self.plan):
                    assert self.issued % RING == slot
                    self._issue(self.issued)
                    self.issued += 1

        W = WStream()

        def mm_group(ps_ap, pairs, start=True, stop=True):
            def fn(e):
                n = len(pairs)
                ins = None
                for i, (l, r) in enumerate(pairs):
                    ins = e.matmul(ps_ap, l, r, start=(start and i == 0), stop=(stop and i == n - 1))
                return ins
            return fn

        def act_fn(out, in_, func, bias=None, scale=None):
            kw = {}
            if bias is not None:
                kw["bias"] = bias
            if scale is not None:
                kw["scale"] = scale
            return lambda e: e.activation(out=out, in_=in_, func=func, **kw)

        def stt(out, in0, scalar, in1, op0, op1):
            return lambda e: e.scalar_tensor_tensor(out=out, in0=in0, scalar=scalar, in1=in1, op0=op0, op1=op1)

        def tt(out, in0, in1, op):
            return lambda e: e.tensor_tensor(out=out, in0=in0, in1=in1, op=op)

        def cp(out, in_):
            return lambda e: e.tensor_copy(out=out, in_=in_)

        def setup():
            emit("sp", lambda e: e.dma_start(out=vec[:, :], in_=vec_d), writes=[("vec",)], dsem=sems["ld"])
            emit("sp", lambda e: e.dma_start(out=cst[:, :], in_=cst_d), writes=[("cst",)], dsem=sems["ld2"])
            if not pg.dry:
                sems["pre"].val += 16
                pg.q["pool"].append(lambda e: e.dma_start(out=cb[:, :], in_=cstb_d).then_inc(sems["pre"].h, 16))
                pg.set_writer(("cb",), sems["pre"], sems["pre"].val)
            for L in layers:
                emit("pool", lambda e, L=L: e.memset(ffst[L][:, :, :], 0.0), writes=[("ffst", L)])
                if MIXERS[L] == "conf":
                    emit("pool", lambda e, L=L: e.memset(cvst[L][:, :, :], 0.0), writes=[("cvst", L)])
            emit("pool", lambda e: e.memset(plst[:, :, :], 0.0), writes=[("plst",)])
            emit("pool", lambda e: e.memset(KT[:, :, :], 0.0), writes=[("KT",)])
            emit("pool", lambda e: e.memset(VT[:, :, :], 0.0), writes=[("VT",)])
            if 2 in layers:
                emit("act", act_fn(est[:, :], V("c_sinks", 2), AF.Exp), reads=[("vec",)], writes=[("est",)])
            k = 0
            for L in layers:
                if MIXERS[L] != "conf":
                    continue
                dwc = V("a_dw_w", L)
                for c in range(CH):
                    buf = dgb[k % 2]
                    tokb = ("bigdgb", k % 2)
                    for j in range(31):
                        eng = "dve"
                        emit(eng, (lambda e, buf=buf, j=j, c=c, dwc=dwc: e.tensor_scalar_mul(
                            out=buf[:, j * P:(j + 1) * P], in0=identb, scalar1=dwc[:, c * 31 + j:c * 31 + j + 1])),
                            reads=[("cb",), ("vec",)], writes=[tokb + (j % 2,)])
                    emit("sp", lambda e, buf=buf, L=L, c=c: e.dma_start(out=w_sc["cvw%d" % L][c], in_=buf),
                         reads=[tokb + (0,), tokb + (1,)], writes=[("cvwd", L, c)], dsem=sems["misc%d" % (k % 2)])
                    k += 1
            if not pg.dry:
                pg.set_writer(("wscratch2", 0), sems["misc0"], sems["misc0"].val)
                pg.set_writer(("wscratch2", 1), sems["misc1"], sems["misc1"].val)

        def layer_tensors(L):
            m = MIXERS[L]
            if m == "conf":
                names = ["win%d" % L, "wout%d" % L]
            elif m == "pool":
                names = ["wg%d" % L]
            else:
                names = ["wq%d" % L, "wk%d" % L, "wv%d" % L, "wo%d" % L]
            return names + ["wup%d" % L, "wdn%d" % L]

        def cast_layer(L):
            if pg.dry:
                return
            for name in layer_tensors(L):
                shp = slab_shapes[name]
                ns, width = shp[0], shp[2]
                per = max(1, (8 << 20) // (P * width * 4))
                sm = sems["c_" + name]
                s0 = 0
                while s0 < ns:
                    s1 = min(ns, s0 + per)
                    src = w_in[name][s0:s1].rearrange("j p w -> (j p) w")
                    dst = w_sc[name][s0:s1].rearrange("j p w -> (j p) w")
                    sm.val += 16
                    pg.q["pool"].append(lambda e, src=src, dst=dst, sm=sm: e.dma_start(out=dst, in_=src).then_inc(sm.h, 16))
                    s0 = s1
                pg.set_writer(("wsc", name), sm, sm.val)

        def load_x(ti):
            pg.switch_big()
            xs_state["ready"] = False
            for tb in range(4):
                k = tb % 2
                st = stage[k]
                r0 = ti * T + tb * P
                emit("sp", lambda e, st=st, r0=r0: e.dma_start(out=st, in_=x_d[r0:r0 + P, :]),
                     writes=[("bigstage", k)], dsem=sems["xin%d" % k])
                for cg in range(4):
                    ps, pt = next_ps()

                    def tr(e, ps=ps, st=st, cg=cg):
                        ins = None
                        for q in range(4):
                            c = cg * 4 + q
                            ins = e.transpose(out=ps[:, q * P:(q + 1) * P], in_=st[:, c * P:(c + 1) * P], identity=C("ident"))
                        return ins
                    emit("pe", tr, reads=[("bigstage", k), ("cst",)], writes=[pt])
                    eng = "act" if cg % 2 == 0 else "dve"
                    dst = xT[:, cg * 4:(cg + 1) * 4, tb * P:(tb + 1) * P]
                    src = ps[:, :].rearrange("p (a b) -> p a b", a=4)
                    if eng == "act":
                        emit("act", act_fn(dst, src, AF.Copy), reads=[pt], writes=[("x", cg * 4 + q) for q in range(4)])
                    else:
                        emit("dve", cp(dst, src), reads=[pt], writes=[("x", cg * 4 + q) for q in range(4)])
                    if tb == 3:
                        for q in range(4):
                            xstat(cg * 4 + q)

        def store_x(ti):
            pg.switch_big()
            cids = []
            for tb in range(4):
                k = tb % 2
                st = stage[k]
                for cg in range(4):
                    ps, pt = next_ps()

                    def tr(e, ps=ps, cg=cg, tb=tb):
                        ins = None
                        for q in range(4):
                            c = cg * 4 + q
                            ins = e.transpose(out=ps[:, q * P:(q + 1) * P], in_=xT[:, c, tb * P:(tb + 1) * P], identity=C("ident"))
                        return ins
                    emit("pe", tr, reads=[("x", cg * 4 + q) for q in range(4)] + [("cst",)], writes=[pt])
                    eng = "act" if cg % 2 == 0 else "dve"
                    dst = st[:, cg * 512:(cg + 1) * 512]
                    if eng == "act":
                        emit("act", act_fn(dst, ps[:, :], AF.Copy), reads=[pt], writes=[("bigstage", k, cg)])
                    else:
                        emit("dve", cp(dst, ps[:, :]), reads=[pt], writes=[("bigstage", k, cg)])
                r0 = ti * T + tb * P
                cid = emit("sp", lambda e, st=st, r0=r0: e.dma_start(out=out_d[r0:r0 + P, :], in_=st),
                           reads=[("bigstage", k, cg) for cg in range(4)], writes=[("outd", ti, tb)], dsem=sems["xout%d" % k])
                for cg in range(4):
                    pass
                cids.append(cid)
            return cids

        def col_stats(src_fn, n, ps, pt, func=AF.Square, bias_fn=None):
            items = []
            for c in range(n):
                sq = sqb[c % 4]
                emit("act", act_fn(sq[:, :], src_fn(c)[0], func, bias=(bias_fn(c) if bias_fn else None)),
                     reads=src_fn(c)[1], writes=[("sq", c % 4)])
                items.append((sq[:, :], [("sq", c % 4)]))
                presum_step(items, c, (4, 5))
            presum_mm(ps, pt, (4, 5))

        def presum_step(items, c, accs, engs=("dve", "dve")):
            if c < 2:
                return
            par = c % 2
            eng = engs[par]
            acc = tmpf[accs[par]]
            at = ("tf", accs[par])
            if c < 4:
                emit(eng, tt(acc[:, :], items[c - 2][0], items[c][0], ALU.add), reads=items[c - 2][1] + items[c][1], writes=[at])
            else:
                emit(eng, tt(acc[:, :], acc[:, :], items[c][0], ALU.add), reads=[at] + items[c][1], writes=[at])

        def presum_mm(ps, pt, accs):
            a0, a1 = tmpf[accs[0]], tmpf[accs[1]]

            def fn(e):
                e.matmul(ps[:, :], C("ones"), a0[:, :], start=True, stop=False)
                return e.matmul(ps[:, :], C("ones"), a1[:, :], start=False, stop=True)
            emit("pe", fn, reads=[("tf", accs[0]), ("tf", accs[1]), ("cst",)], writes=[pt])

        def rstd_from(ps, pt, scale, out_i):
            sd = tmpf[out_i]
            emit("act", act_fn(sd[:, :], ps[:, :], AF.Sqrt, bias=V_eps(), scale=scale), reads=[pt, ("cst",)], writes=[("tf", out_i)])
            emit("dve", lambda e: e.reciprocal(out=sd[:, :], in_=sd[:, :]), reads=[("tf", out_i)], writes=[("tf", out_i)])
            return sd

        def V_eps():
            return epsb[:, 0:1]

        xs_state = {"items": [], "ready": False}

        def xstat(c):
            if c == 0:
                xs_state["items"] = []
            sq = sqb[c % 4]
            emit("act", act_fn(sq[:, :], xT[:, c, :], AF.Square), reads=[("x", c)], writes=[("sq", c % 4)])
            xs_state["items"].append((sq[:, :], [("sq", c % 4)]))
            presum_step(xs_state["items"], c, (4, 5))
            if c == CH - 1:
                xs_state["ready"] = True

        def x_sumsq(ps, pt):
            if xs_state["ready"]:
                xs_state["ready"] = False
                presum_mm(ps, pt, (4, 5))
            else:
                col_stats(lambda c: (xT[:, c, :], [("x", c)]), CH, ps, pt)

        def prog_groups(sb_list):
            outs = [next_ps() for _ in sb_list]
            slots = sorted(set(sl for sl, _ in sb_list))
            for kc in range(CH):
                def fn(e, kc=kc):
                    ins = None
                    for (sl, base), (ps, pt) in zip(sb_list, outs):
                        ins = e.matmul(ps[:, :], ring[sl][:, base + kc * P: base + (kc + 1) * P], hb[:, kc, :],
                                       start=(kc == 0), stop=(kc == CH - 1))
                    return ins
                emit("pe", fn, reads=[("ring", sl) for sl in slots] + [("hb", kc)], writes=[pt for _, pt in outs])
            return outs

        def rmsnorm_to_hb(gname, L):
            g = V(gname, L)
            ps, pt = next_ps()
            x_sumsq(ps, pt)
            rs = rstd_from(ps, pt, 1.0 / D, 0)
            for c in range(CH):
                emit("dve", stt(hb[:, c, :], xT[:, c, :], g[:, c:c + 1], rs[:, :], ALU.mult, ALU.mult),
                     reads=[("x", c), ("tf", 0), ("vec",)], writes=[("hb", c)])

        def ffn(L, ti, stats_next=True):
            pg.switch_big()
            rmsnorm_to_hb("ffn_norm_g", L)
            dww = V("ffn_dw_w", L)
            dwb = V("ffn_dw_b", L)

            def post(j, half, ps, pt):
                f = half * NF + j
                pb = pbuf[half][j % 2]
                pbt = ("bigpb", half, j % 2)
                ub = ubuf[half][j % 2]
                ubt = ("bigub", half, j % 2)
                emit("pool", cp(pb[:, 0:2], ffst[L][:, f, :]), reads=[("ffst", L, f), ("ffst", L)], writes=[pbt + (0,)])
                emit("act", act_fn(pb[:, 2:2 + T], ps[:, :], AF.Copy), reads=[pt], writes=[pbt + (1,)])
                emit("pool", cp(ffst[L][:, f, :], pb[:, T:T + 2]), reads=[pbt + (0,), pbt + (1,)], writes=[("ffst", L, f)])
                emit("act", act_fn(ub[:, :], ps[:, :], AF.Identity, bias=dwb[:, f:f + 1], scale=dww[:, f * 3 + 2: f * 3 + 3]),
                     reads=[pt, ("vec",)], writes=[ubt])
                emit("dve", stt(ub[:, :], pb[:, 1:1 + T], dww[:, f * 3 + 1:f * 3 + 2], ub[:, :], ALU.mult, ALU.add),
                     reads=[pbt + (0,), pbt + (1,), ubt], writes=[ubt])
                emit("dve", stt(ub[:, :], pb[:, 0:T], dww[:, f * 3:f * 3 + 1], ub[:, :], ALU.mult, ALU.add),
                     reads=[pbt + (0,), pbt + (1,), ubt], writes=[ubt])

            def gate(j):
                ug = ubuf[0][j % 2]
                uv = ubuf[1][j % 2]
                emit("act", act_fn(ug[:, :], ug[:, :], AF.Silu), reads=[("bigub", 0, j % 2)], writes=[("bigub", 0, j % 2)])
                emit("dve", tt(actb[:, j, :], ug[:, :], uv[:, :], ALU.mult),
                     reads=[("bigub", 0, j % 2), ("bigub", 1, j % 2)], writes=[("bigact", j)])

            s0 = W.get("wup%d" % L, 0, 4096)
            s1 = W.get("wup%d" % L, 1, 4096)
            grp = prog_groups([(s0, 0), (s0, 2048), (s1, 0), (s1, 2048)])
            W.release(s0)
            W.release(s1)
            post(0, 0, *grp[0])
            post(0, 1, *grp[1])
            gate(0)
            post(1, 0, *grp[2])
            post(1, 1, *grp[3])
            gate(1)
            for j in range(2, NF):
                slot = W.get("wup%d" % L, j, 4096)
                for half in range(2):
                    ps, pt = next_ps()
                    base = half * 2048
                    emit("pe", mm_group(ps[:, :], [(ring[slot][:, base + kc * P: base + (kc + 1) * P], hb[:, kc, :]) for kc in range(CH)]),
                         reads=[("ring", slot)] + [("hb", kc) for kc in range(CH)], writes=[pt])
                    post(j, half, ps, pt)
                W.release(slot)
                gate(j)
            for m in range(CH):
                slot = W.get("wdn%d" % L, m, 5632)
                ps, pt = next_ps()
                pairs = [(ring[slot][:, k * P:(k + 1) * P], actb[:, k, :]) for k in range(NF)]
                if m == 0:
                    ks = NF - 4
                    emit("pe", mm_group(ps[:, :], pairs[:ks], start=True, stop=False),
                         reads=[("ring", slot)] + [("bigact", k) for k in range(ks)], writes=[pt])
                    emit("pe", mm_group(ps[:, :], pairs[ks:], start=False, stop=True),
                         reads=[("ring", slot)] + [("bigact", k) for k in range(ks, NF)], writes=[pt])
                else:
                    emit("pe", mm_group(ps[:, :], pairs),
                         reads=[("ring", slot)] + [("bigact", k) for k in range(NF)], writes=[pt])
                W.release(slot)
                emit("dve", tt(xT[:, m, :], xT[:, m, :], ps[:, :], ALU.add), reads=[pt, ("x", m)], writes=[("x", m)])
                if stats_next:
                    xstat(m)

        def conformer(L, ti):
            pg.switch_big()
            rmsnorm_to_hb("norm_g", L)
            b_in = V("a_b_in", L)
            emit("pool", cp(Gb[:, :, 0:30], cvst[L][:, :, :]), reads=[("cvst", L)], writes=[("bigGh",)])
            s0 = W.get("win%d" % L, 0, 4096)
            s1 = W.get("win%d" % L, 1, 4096)
            grp0 = prog_groups([(s0, 0), (s0, 2048), (s1, 0), (s1, 2048)])
            W.release(s0)
            W.release(s1)
            for c in range(CH):
                if c < 2:
                    (psa, pta), (psg, ptg) = grp0[2 * c], grp0[2 * c + 1]
                else:
                    slot = W.get("win%d" % L, c, 4096)
                    psa, pta = next_ps()
                    emit("pe", mm_group(psa[:, :], [(ring[slot][:, kc * P:(kc + 1) * P], hb[:, kc, :]) for kc in range(CH)]),
                         reads=[("ring", slot)] + [("hb", kc) for kc in range(CH)], writes=[pta])
                    psg, ptg = next_ps()
                    emit("pe", mm_group(psg[:, :], [(ring[slot][:, 2048 + kc * P:2048 + (kc + 1) * P], hb[:, kc, :]) for kc in range(CH)]),
                         reads=[("ring", slot)] + [("hb", kc) for kc in range(CH)], writes=[ptg])
                    W.release(slot)
                sg = tmpf[2 + c % 2]
                emit("act", act_fn(sg[:, :], psg[:, :], AF.Sigmoid, bias=b_in[:, CH + c:CH + c + 1]),
                     reads=[ptg, ("vec",)], writes=[("tf", 2 + c % 2)])
                emit("dve", stt(Gb[:, c, 30:30 + T], psa[:, :], b_in[:, c:c + 1], sg[:, :], ALU.add, ALU.mult),
                     reads=[pta, ("tf", 2 + c % 2), ("vec",)], writes=[("bigG", c)])
            emit("pool", cp(cvst[L][:, :, :], Gb[:, :, T:T + 30]), reads=[("bigG", c) for c in range(CH)], writes=[("cvst", L)])
            dw_b = V("a_dw_b", L)
            it_y = []
            it_q = []
            dwc = V("a_dw_w", L)
            ND = CONV_DVE_TAPS

            def dve_taps(c):
                acc = tmpf[c % 2]
                at = ("tf", c % 2)
                rd = [("bigG", c), ("bigGh",), ("vec",)]
                emit("act", act_fn(acc[:, :], Gb[:, c, 0:T], AF.Identity, scale=dwc[:, c * 31:c * 31 + 1]), reads=rd, writes=[at])
                for j in range(1, ND):
                    emit("dve", stt(acc[:, :], Gb[:, c, j:j + T], dwc[:, c * 31 + j:c * 31 + j + 1], acc[:, :], ALU.mult, ALU.add),
                         reads=rd + [at], writes=[at])

            if ND > 0:
                dve_taps(0)
            for c in range(CH):
                slot = W.get("cvw%d" % L, c, 31 * P)
                ps, pt = next_ps()
                emit("pe", mm_group(ps[:, :], [(ring[slot][:, j * P:(j + 1) * P], Gb[:, c, j:j + T]) for j in range(ND, 31)]),
                     reads=[("ring", slot), ("bigG", c), ("bigGh",)], writes=[pt])
                W.release(slot)
                sq = sqb[c % 4]
                if ND > 0:
                    if c + 1 < CH:
                        dve_taps(c + 1)
                    emit("dve", stt(yb[:, c, :], ps[:, :], dw_b[:, c:c + 1], tmpf[c % 2][:, :], ALU.add, ALU.add),
                         reads=[pt, ("tf", c % 2), ("vec",)], writes=[("bigy", c)])
                    emit("act", act_fn(sq[:, :], yb[:, c, :], AF.Square), reads=[("bigy", c)], writes=[("sq", c % 4)])
                else:
                    emit("act", act_fn(yb[:, c, :], ps[:, :], AF.Identity, bias=dw_b[:, c:c + 1]), reads=[pt, ("vec",)], writes=[("bigy", c)])
                    emit("act", act_fn(sq[:, :], ps[:, :], AF.Square, bias=dw_b[:, c:c + 1]), reads=[pt, ("vec",)], writes=[("sq", c % 4)])
                it_y.append((yb[:, c, :], [("bigy", c)]))
                it_q.append((sq[:, :], [("sq", c % 4)]))
                presum_step(it_y, c, (2, 3))
                presum_step(it_q, c, (4, 5))
            ps_s, pt_s = next_ps()
            ps_q, pt_q = next_ps()
            presum_mm(ps_s, pt_s, (2, 3))
            presum_mm(ps_q, pt_q, (4, 5))
            mean = tmpf[1]
            emit("act", act_fn(mean[:, :], ps_s[:, :], AF.Identity, scale=1.0 / D), reads=[pt_s], writes=[("tf", 1)])
            msq = tmpf[4]
            emit("dve", tt(msq[:, :], mean[:, :], mean[:, :], ALU.mult), reads=[("tf", 1)], writes=[("tf", 4)])
            var = tmpf[5]
            emit("dve", stt(var[:, :], ps_q[:, :], 1.0 / D, msq[:, :], ALU.mult, ALU.subtract), reads=[pt_q, ("tf", 4)], writes=[("tf", 5)])
            emit("act", act_fn(var[:, :], var[:, :], AF.Sqrt, bias=V_eps()), reads=[("tf", 5), ("cst",)], writes=[("tf", 5)])
            emit("dve", lambda e: e.reciprocal(out=var[:, :], in_=var[:, :]), reads=[("tf", 5)], writes=[("tf", 5)])
            ln_g = V("a_ln_g", L)
            ln_b = V("a_ln_b", L)
            for c in range(CH):
                eng = "dve"
                emit(eng, tt(yb[:, c, :], yb[:, c, :], mean[:, :], ALU.subtract), reads=[("bigy", c), ("tf", 1)], writes=[("bigy", c)])
                emit(eng, tt(yb[:, c, :], yb[:, c, :], var[:, :], ALU.mult), reads=[("bigy", c), ("tf", 5)], writes=[("bigy", c)])
                emit("act", act_fn(hb[:, c, :], yb[:, c, :], AF.Silu, bias=ln_b[:, c:c + 1], scale=ln_g[:, c:c + 1]),
                     reads=[("bigy", c), ("vec",)], writes=[("hb", c)])
            b_out = V("a_b_out", L)
            sl3 = [W.get("wout%d" % L, m, 2048) for m in range(3)]
            grp3 = prog_groups([(sl, 0) for sl in sl3])
            for sl in sl3:
                W.release(sl)
            for m in range(CH):
                if m < 3:
                    ps, pt = grp3[m]
                else:
                    slot = W.get("wout%d" % L, m, 2048)
                    ps, pt = next_ps()
                    emit("pe", mm_group(ps[:, :], [(ring[slot][:, kc * P:(kc + 1) * P], hb[:, kc, :]) for kc in range(CH)]),
                         reads=[("ring", slot)] + [("hb", kc) for kc in range(CH)], writes=[pt])
                    W.release(slot)
                emit("dve", stt(xT[:, m, :], ps[:, :], b_out[:, m:m + 1], xT[:, m, :], ALU.add, ALU.add),
                     reads=[pt, ("x", m), ("vec",)], writes=[("x", m)])
                xstat(m)

        def pool_mixer(L, ti):
            pg.switch_big()
            g = V("norm_g", L)
            ps, pt = next_ps()
            x_sumsq(ps, pt)
            rs = rstd_from(ps, pt, 1.0 / D, 0)
            emit("pool", cp(Hb[:, :, 0:15], plst[:, :, :]), reads=[("plst",)], writes=[("bigHh",)], small=True)
            ic = C("invcnt")
            sc = V("b_scale", L)
            for gi in range(4):
                w = POOL_W[gi]
                for c in range(gi * 4, gi * 4 + 4):
                    emit("dve", stt(Hb[:, c, 15:15 + T], xT[:, c, :], g[:, c:c + 1], rs[:, :], ALU.mult, ALU.mult),
                         reads=[("x", c), ("tf", 0), ("vec",)], writes=[("bigH", c)])
                for c in range(gi * 4, gi * 4 + 4):
                    eng = "dve"
                    bufs = [pbuf[0][c % 2], pbuf[1][c % 2]]
                    bt = [("bigpb", 0, c % 2, 0), ("bigpb", 1, c % 2, 0)]
                    src = Hb[:, c, :]
                    srct = [("bigH", c), ("bigHh",)]
                    sh = 1
                    bi = 0
                    while sh < w:
                        dst = bufs[bi]
                        n = 15 + T
                        emit(eng, tt(dst[:, sh:n], src[:, sh:n], src[:, 0:n - sh], ALU.add), reads=srct, writes=[bt[bi]])
                        src = dst
                        srct = [bt[bi]]
                        bi ^= 1
                        sh *= 2
                    emit("dve", stt(hb[:, c, :], src[:, 15:15 + T], 1.0 / w, Hb[:, c, 15:15 + T], ALU.mult, ALU.subtract),
                         reads=srct + [("bigH", c)], writes=[("hb", c)])
                    if ti == 0:
                        t16 = tmpf[2 + c % 2]
                        emit(eng, tt(t16[:, 0:16], src[:, 15:31], ic[:, gi * 16:(gi + 1) * 16], ALU.mult), reads=srct + [("cst",)], writes=[("tf", 2 + c % 2)], small=True)
                        emit(eng, tt(hb[:, c, 0:16], t16[:, 0:16], Hb[:, c, 15:31], ALU.subtract), reads=[("tf", 2 + c % 2), ("bigH", c)], writes=[("hb", c)], small=True)
                slot = W.get("wg%d" % L, gi, 2048)
                for mc in range(4):
                    ps, pt = next_ps()
                    emit("pe", mm_group(ps[:, :], [(ring[slot][:, kc * 512 + mc * P: kc * 512 + (mc + 1) * P], hb[:, gi * 4 + kc, :]) for kc in range(4)]),
                         reads=[("ring", slot)] + [("hb", gi * 4 + kc) for kc in range(4)], writes=[pt])
                    m = gi * 4 + mc
                    emit("dve", stt(xT[:, m, :], ps[:, :], sc[:, m:m + 1], xT[:, m, :], ALU.mult, ALU.add),
                         reads=[pt, ("x", m), ("vec",)], writes=[("x", m)])
                    xstat(m)
                W.release(slot)
            emit("pool", cp(plst[:, :, :], Hb[:, :, T:T + 15]), reads=[("bigH", c) for c in range(CH)], writes=[("plst",)])

        def attention(L, ti):
            pg.switch_big()
            rmsnorm_to_hb("norm_g", L)
            emit("sp", lambda e: e.dma_start(out=posi, in_=pos_d[:, ti * T:(ti + 1) * T]), writes=[("tf", 3)], dsem=sems["ld"])
            ang = tmpf[1]
            emit("dve", cp(ang[:, :], posi), reads=[("tf", 3)], writes=[("tf", 1)])
            emit("dve", lambda e: e.tensor_single_scalar(out=ang[:, :], in_=ang[:, :], scalar=C("invf")[:, 0:1], op=ALU.mult),
                 reads=[("tf", 1), ("cst",)], writes=[("tf", 1)])
            emit("dve", lambda e: e.tensor_single_scalar(out=ang[:, :], in_=ang[:, :], scalar=1.0 / TWO_PI, op=ALU.mult),
                 reads=[("tf", 1)], writes=[("tf", 1)])
            for which, dstb, dtk in ((0, sinb, ("bigsin",)), (1, cosb, ("bigcos",))):
                qq = tmpf[4]
                if which == 1:
                    emit("dve", lambda e: e.tensor_single_scalar(out=ang[:, :], in_=ang[:, :], scalar=0.25, op=ALU.add),
                         reads=[("tf", 1)], writes=[("tf", 1)])
                emit("dve", cp(posi, ang[:, :]), reads=[("tf", 1)], writes=[("tf", 3)])
                emit("dve", cp(qq[:, :], posi), reads=[("tf", 3)], writes=[("tf", 4)])
                emit("dve", tt(qq[:, :], ang[:, :], qq[:, :], ALU.subtract), reads=[("tf", 1), ("tf", 4)], writes=[("tf", 4)])
                emit("act", act_fn(dstb, qq[:, :], AF.Sin, scale=6.28318), reads=[("tf", 4)], writes=[dtk])
            emit("dve", lambda e: e.tensor_single_scalar(out=sinb, in_=sinb, scalar=C("sgn")[:, 0:1], op=ALU.mult),
                 reads=[("bigsin",), ("cst",)], writes=[("bigsin",)])

            gq = V("c_q_norm_g", L)
            gk = V("c_k_norm_g", L)

            chunks = [("wq%d" % L, c, gq, QT[:, c, :], ("bigQ", c)) for c in range(CH)] + \
                     [("wk%d" % L, kv, gk, KT[:, kv, P:P + T], ("KTc", kv)) for kv in range(4)]
            sl3 = [W.get("wq%d" % L, c, 2048) for c in range(3)]
            grpq = prog_groups([(sl, 0) for sl in sl3])
            for sl in sl3:
                W.release(sl)
            st = {}

            def stageA(k):
                name, idx, gcol, dst, dtok = chunks[k]
                if k < 3:
                    ps, pt = grpq[k]
                else:
                    slot = W.get(name, idx, 2048)
                    ps, pt = next_ps()
                    emit("pe", mm_group(ps[:, :], [(ring[slot][:, kc * P:(kc + 1) * P], hb[:, kc, :]) for kc in range(CH)]),
                         reads=[("ring", slot)] + [("hb", kc) for kc in range(CH)], writes=[pt])
                    W.release(slot)
                held.add(pt[1])
                sq = sqb[k % 2]
                emit("act", act_fn(sq[:, :], ps[:, :], AF.Square), reads=[pt], writes=[("sq", k % 2)])
                st[k] = (ps, pt)

            def stageB(k):
                name, idx, gcol, dst, dtok = chunks[k]
                ps, pt = st[k]
                sq = sqb[k % 2]
                ps2, pt2 = next_ps()
                emit("pe", lambda e: e.matmul(ps2[:, :], C("bones"), sq[:, :], start=True, stop=True), reads=[("sq", k % 2), ("cst",)], writes=[pt2])
                rs = tmpf[2 + k % 2]
                rt = ("tf", 2 + k % 2)
                emit("act", act_fn(rs[:, :], ps2[:, :], AF.Sqrt, bias=V_eps(), scale=1.0 / 64), reads=[pt2, ("cst",)], writes=[rt])
                emit("dve", lambda e: e.reciprocal(out=rs[:, :], in_=rs[:, :]), reads=[rt], writes=[rt])
                qn = ubuf[0][k % 2]
                qt = ("bigub", 0, k % 2)
                emit("dve", stt(qn[:, :], ps[:, :], gcol[:, 0:1], rs[:, :], ALU.mult, ALU.mult), reads=[pt, rt, ("vec",)], writes=[qt])
                held.discard(pt[1])

            def stageC(k):
                name, idx, gcol, dst, dtok = chunks[k]
                qn = ubuf[0][k % 2]
                qt = ("bigub", 0, k % 2)
                ps3, pt3 = next_ps()
                emit("pe", lambda e: e.matmul(ps3[:, :], C("perm"), qn[:, :], start=True, stop=True), reads=[qt, ("cst",)], writes=[pt3])
                t2 = ubuf[1][k % 2]
                tt2 = ("bigub", 1, k % 2)
                emit("dve", tt(t2[:, :], ps3[:, :], sinb, ALU.mult), reads=[pt3, ("bigsin",)], writes=[tt2])
                emit("dve", tt(qn[:, :], qn[:, :], cosb, ALU.mult), reads=[qt, ("bigcos",)], writes=[qt])
                emit("dve", tt(dst, qn[:, :], t2[:, :], ALU.add), reads=[qt, tt2], writes=[dtok])

            nqk = len(chunks)
            for i in range(nqk + 2):
                if i < nqk:
                    stageA(i)
                if 0 <= i - 1 < nqk:
                    stageB(i - 1)
                if 0 <= i - 2 < nqk:
                    stageC(i - 2)
            for half in range(2):
                slot = W.get("wv%d" % L, half, 4096)
                for tb in range(4):
                    ps, pt = next_ps()
                    emit("pe", mm_group(ps[:, 0:256], [(hb[:, kc, tb * P:(tb + 1) * P], ring[slot][:, kc * 256:(kc + 1) * 256]) for kc in range(CH)]),
                         reads=[("ring", slot)] + [("hb", kc) for kc in range(CH)], writes=[pt])
                    emit("act", act_fn(VT[:, 1 + tb, half * 256:(half + 1) * 256], ps[:, 0:256], AF.Copy), reads=[pt], writes=[("VTc", tb, half)])
                W.release(slot)
            OT = hb
            kpi = 0
            for kv in range(4):
                for qb in range(4):
                    first = (ti == 0 and qb == 0)
                    pt_buf = PTb[kpi % 2]
                    ptt = ("bigP", kpi % 2)
                    kpi += 1
                    kbs = [1] if first else [0, 1]
                    for par in range(2):
                        for kb in kbs:
                            ps, pt = next_ps()
                            kcol = qb * P + kb * P
                            lhs = KT[par * 64:(par + 1) * 64, kv, kcol:kcol + P]
                            rhs = QT[par * 64:(par + 1) * 64, kv * 4:(kv + 1) * 4, qb * P:(qb + 1) * P]
                            mk = mprevb if kb == 0 else mcurb

                            def sfn(e, ps=ps, lhs=lhs, rhs=rhs, mk=mk):
                                e.matmul(ps[:, :], lhs, rhs, start=True, stop=False)
                                return e.matmul(ps[:, :], mk, identqb, start=False, stop=True)
                            emit("pe", sfn, reads=[("KTc", kv), ("KTh",), ("cb",)] + [("bigQ", kv * 4 + i) for i in range(4)], writes=[pt])
                            emit("act", act_fn(pt_buf[:, kb, par * 512:(par + 1) * 512], ps[:, :], AF.Exp, scale=0.125), reads=[pt], writes=[ptt + (par, kb)])
                    dix = 4 + (kpi % 2)
                    den = tmpf[dix]
                    dt_ = ("tf", dix)
                    nums = []
                    for par in range(2):
                        psn, ptn = next_ps()
                        psd, ptd = next_ps()

                        def nfn(e, psn=psn, par=par, kbs=kbs, qb=qb, kv=kv, pt_buf=pt_buf):
                            ins = None
                            for i, kb in enumerate(kbs):
                                vsrc = VT[:, qb + kb, kv * P:(kv + 1) * P]
                                ins = e.matmul(psn[:, :], vsrc, pt_buf[:, kb, par * 512:(par + 1) * 512], start=(i == 0), stop=(i == len(kbs) - 1))
                            return ins

                        def dfn(e, psd=psd, par=par, kbs=kbs, pt_buf=pt_buf):
                            ins = None
                            for i, kb in enumerate(kbs):
                                ins = e.matmul(psd[:, :], identb_ones, pt_buf[:, kb, par * 512:(par + 1) * 512], start=(i == 0), stop=(i == len(kbs) - 1))
                            return ins
                        rd = [ptt + (par, kb) for kb in kbs]
                        emit("pe", nfn, reads=rd + [("VTc", t_, h_) for t_ in range(4) for h_ in range(2)] + [("VTh",)], writes=[ptn])
                        emit("pe", dfn, reads=rd + [("cb",)], writes=[ptd])
                        lo, hi = par * 64, (par + 1) * 64
                        esv = est[lo:hi, kv * 8 + par: kv * 8 + 8: 2].unsqueeze(2).to_broadcast([64, 4, P])
                        emit("dve", tt(den[lo:hi, :].rearrange("p (a b) -> p a b", a=4), psd[lo:hi, :].rearrange("p (a b) -> p a b", a=4), esv, ALU.add),
                             reads=[ptd, ("est",)], writes=[dt_])
                        nums.append((psn, ptn))
                    emit("dve", lambda e, den=den: e.reciprocal(out=den[:, :], in_=den[:, :]), reads=[dt_], writes=[dt_])
                    for par in range(2):
                        psn, ptn = nums[par]
                        lo, hi = par * 64, (par + 1) * 64
                        emit("dve", tt(OT[lo:hi, kv * 4:(kv + 1) * 4, qb * P:(qb + 1) * P], psn[lo:hi, :].rearrange("p (a b) -> p a b", a=4),
                                       den[lo:hi, :].rearrange("p (a b) -> p a b", a=4), ALU.mult),
                             reads=[ptn, dt_], writes=[("hb", kv * 4 + i_) for i_ in range(4)])
            emit("pool", cp(KT[:, :, 0:P], KT[:, :, T:T + P]), reads=[("KTc", kv) for kv in range(4)], writes=[("KTh",)])
            emit("pool", cp(VT[:, 0, :], VT[:, 4, :]), reads=[("VTc", 3, 0), ("VTc", 3, 1)], writes=[("VTh",)])
            for m in range(CH):
                slot = W.get("wo%d" % L, m, 2048)
                ps, pt = next_ps()
                emit("pe", mm_group(ps[:, :], [(ring[slot][:, kc * P:(kc + 1) * P], OT[:, kc, :]) for kc in range(CH)]),
                     reads=[("ring", slot)] + [("hb", kc) for kc in range(CH)], writes=[pt])
                W.release(slot)
                emit("dve", tt(xT[:, m, :], xT[:, m, :], ps[:, :], ALU.add), reads=[pt, ("x", m)], writes=[("x", m)])
                xstat(m)

        epsb = sb("epsb", [P, 1], F32)
        negpi = sb("negpi", [P, 1], F32)
        onesb = sb("onesb", [P, P], BF16)
        identb_ones = onesb[:, :]

        def body():
            state["ps"] = 0
            pg.reset()
            emit("pool", lambda e: e.memset(epsb[:, :], EPS), writes=[("cst2",)])
            emit("pool", lambda e: e.memset(negpi[:, :], -math.pi), writes=[("cst2",)])
            emit("pool", lambda e: e.memset(onesb[:, :], 1.0), writes=[("cst2",)])
            setup()
            if not pg.dry:
                cid = (sems["pool"], 3)
                for eng in ("pe", "act", "dve", "sp"):
                    pg.wait_all(eng, [cid])
            W.start() if not pg.dry else None
            outc = []
            for ti in range(NT):
                state["ti"] = ti
                load_x(ti)
                for li, L in enumerate(layers):
                    if do_mixer:
                        {"conf": conformer, "pool": pool_mixer, "attn": attention}[MIXERS[L]](L, ti)
                    if do_ffn:
                        ffn(L, ti, stats_next=(li + 1 < len(layers)))
                outc += store_x(ti)
            return outc

        pg.dry = True
        body()
        pg.dry = False
        outc = body()
        last = {}
        for s, v in outc:
            last[s.name] = (s, max(v, last.get(s.name, (s, 0))[1]))
        pg.wait_all("sp", list(last.values()))

        with nc.Block() as block:
            @block.sync
            def _(e):
                for f in pg.q["sp"]:
                    f(e)

            @block.tensor
            def _(e):
                for f in pg.q["pe"]:
                    f(e)

            @block.scalar
            def _(e):
                for f in pg.q["act"]:
                    f(e)

            @block.vector
            def _(e):
                for f in pg.q["dve"]:
                    f(e)

            @block.gpsimd
            def _(e):
                for f in pg.q["pool"]:
                    f(e)
        print("instructions:", pg.n_inst, {k: len(v) for k, v in pg.q.items()}, "sbuf left", nc.sbuf_bytes_remaining)
    return nc


def prepare_shared(inp, layers):
    vecs, voffs = pack_vec(inp)
    csts, coffs, cbarr = make_consts()
    slabs = weight_slabs(inp, layers)
    return vecs, voffs, csts, coffs, slabs, cbarr


def run(inp, S, layers, n_cores, do_ffn=True, do_mixer=True):
    vecs, voffs, csts, coffs, slabs, cbarr = prepare_shared(inp, layers)
    slab_shapes = {k: v.shape for k, v in slabs.items()}
    import time as _time
    _t0 = _time.time()
    nc = build_program(S, layers, slab_shapes, voffs, coffs, vecs.shape[1], csts.shape[1], do_ffn=do_ffn, do_mixer=do_mixer)
    print("build time", _time.time() - _t0, flush=True)
    x = np.asarray(inp["x"], np.float32)
    pos = np.asarray(inp["positions"], np.int32)
    in_maps = []
    for b in range(n_cores):
        m = {"x": np.ascontiguousarray(x[b]),
             "posb": np.ascontiguousarray(np.broadcast_to(pos[b][None, :], (P, S))),
             "vec": vecs, "cst": csts, "cstb": cbarr}
        m.update(slabs)
        in_maps.append(m)
    _t0 = _time.time()
    import os as _os
    _tr = _os.environ.get("KTRACE", "0") == "1"
    res = run_bass_kernel_spmd(nc, in_maps, core_ids=list(range(n_cores)), **({"trace": True} if _tr else {}))
    if _tr:
        print("exec_time_ns", res.exec_time_ns, flush=True)
    print("run time", _time.time() - _t0, flush=True)
    return np.stack([r["out"] for r in res.results], axis=0)


_INPUT_NAMES = (
    "x", "positions",
    "l0_norm_g", "l0_a_w_in", "l0_a_b_in", "l0_a_dw_w", "l0_a_dw_b", "l0_a_ln_g", "l0_a_ln_b", "l0_a_w_out", "l0_a_b_out",
    "l0_ffn_norm_g", "l0_ffn_w_up", "l0_ffn_dw_w", "l0_ffn_dw_b", "l0_ffn_w_down",
    "l1_norm_g", "l1_b_w_group", "l1_b_scale",
    "l1_ffn_norm_g", "l1_ffn_w_up", "l1_ffn_dw_w", "l1_ffn_dw_b", "l1_ffn_w_down",
    "l2_norm_g", "l2_c_w_qkv", "l2_c_q_norm_g", "l2_c_k_norm_g", "l2_c_sinks", "l2_c_w_o",
    "l2_ffn_norm_g", "l2_ffn_w_up", "l2_ffn_dw_w", "l2_ffn_dw_b", "l2_ffn_w_down",
    "l3_norm_g", "l3_a_w_in", "l3_a_b_in", "l3_a_dw_w", "l3_a_dw_b", "l3_a_ln_g", "l3_a_ln_b", "l3_a_w_out", "l3_a_b_out",
    "l3_ffn_norm_g", "l3_ffn_w_up", "l3_ffn_dw_w", "l3_ffn_dw_b", "l3_ffn_w_down",
)


def kernel(**inputs):
    inp = {k: inputs[k] for k in _INPUT_NAMES}
    return run(inp, 4096, [0, 1, 2, 3], 8).astype(np.float32)
```
